# Optimizing a Trainium2 kernel written in Bass

```python
import math
import jax, jax.numpy as jnp
from jax import lax
import numpy as np

D_MODEL = 1024
BATCH = 8
SEQ = 2048
DEPTH = 1
DEC_BATCH = 128
DEC_SEQ = 1
PAST_LEN = 16384
PAGE_SIZE = 128

GLA_HEADS = 4
GLA_DK = 128
GLA_DV = 256
GLA_RANK = 16
GLA_TAU = 16.0
HGRN_HEADS = 8
HGRN_EXPAND = 128
HGRN_DV = 128
CHUNK = 64
DEEPNORM_ALPHA = (2.0 * DEPTH) ** 0.25
DEEPNORM_BETA = (8.0 * DEPTH) ** -0.25
NORM_EPS = 1e-5

GLA_K = GLA_HEADS * GLA_DK
GLA_V = GLA_HEADS * GLA_DV
HGRN_K = HGRN_HEADS * HGRN_EXPAND
HGRN_V = HGRN_HEADS * HGRN_DV
SPLITS = (GLA_K, GLA_K, GLA_V, GLA_V, GLA_RANK, HGRN_K, HGRN_K, HGRN_V, HGRN_V, D_MODEL, D_MODEL)
IN_DIM = int(sum(SPLITS))
SPLIT_POINTS = [int(s) for s in np.cumsum(SPLITS)[:-1]]

kernel_name = "gla_hgrn2_parallel_gated_deepnorm_step"


def _chunk_gated_linear(q, k, v, log_decay, s0):
    f32 = jnp.float32
    B, T, H, DK = q.shape
    DV = v.shape[-1]
    C = min(CHUNK, T)
    n = -(-T // C)
    pad = n * C - T
    q, k, v, log_decay = (a.astype(f32) for a in (q, k, v, log_decay))
    if pad > 0:
        cfg = ((0, 0), (0, pad), (0, 0), (0, 0))
        q, k, v, log_decay = (jnp.pad(a, cfg) for a in (q, k, v, log_decay))

    def to_chunks(a):
        d = a.shape[-1]
        return a.reshape(B, n, C, H, d).transpose(1, 0, 3, 2, 4)

    qc, kc, vc, gc = to_chunks(q), to_chunks(k), to_chunks(v), to_chunks(log_decay)
    mask = jnp.tril(jnp.ones((C, C), dtype=bool))

    def step(S, inp):
        qb, kb, vb, gb = inp
        b = jnp.cumsum(gb, axis=2)
        o_inter = jnp.einsum('bhtk,bhkv->bhtv', qb * jnp.exp(b), S)
        diff = b[:, :, :, None, :] - b[:, :, None, :, :]
        decay = jnp.exp(jnp.where(mask[None, None, :, :, None], diff, -jnp.inf))
        scores = jnp.einsum('bhtk,bhsk,bhtsk->bhts', qb, kb, decay)
        o_intra = jnp.einsum('bhts,bhsv->bhtv', scores, vb)
        b_last = b[:, :, -1:, :]
        S_new = jnp.exp(b_last[:, :, 0, :])[..., None] * S + jnp.einsum(
            'bhsk,bhsv->bhkv', kb * jnp.exp(b_last - b), vb)
        return S_new, o_inter + o_intra

    s_final, o = lax.scan(step, s0.astype(f32), (qc, kc, vc, gc))
    o = o.transpose(1, 0, 3, 2, 4).reshape(B, n * C, H, DV)[:, :T]
    return o, s_final


def _head_rmsnorm(o, g):
    o = o * lax.rsqrt(jnp.mean(o * o, axis=-1, keepdims=True) + NORM_EPS)
    return o * g.astype(jnp.float32)


def _mixer_layer(x, s_gla, s_hgrn, w_in, w_gate_lr, b_gate_lr, gla_norm_g, w_br_gla,
                 lb, hgrn_norm_g, w_br_hgrn, w_out, ln_g, ln_b):
    f32 = jnp.float32
    B, T, _ = x.shape
    h = jnp.einsum('btd,de->bte', x, w_in)
    gq, gk, gv, gr, ga, hq, hf, hi, hr, mg_a, mg_b = jnp.split(h, SPLIT_POINTS, axis=-1)

    q = gq.astype(f32).reshape(B, T, GLA_HEADS, GLA_DK) * (GLA_DK ** -0.5)
    k = gk.astype(f32).reshape(B, T, GLA_HEADS, GLA_DK)
    v = gv.astype(f32).reshape(B, T, GLA_HEADS, GLA_DV)
    a_logit = jnp.einsum('btr,rk->btk', ga, w_gate_lr) + b_gate_lr
    log_alpha = jax.nn.log_sigmoid(a_logit.astype(f32)).reshape(B, T, GLA_HEADS, GLA_DK) / GLA_TAU
    o_a, s_gla_new = _chunk_gated_linear(q, k, v, log_alpha, s_gla)
    o_a = _head_rmsnorm(o_a, gla_norm_g).reshape(B, T, GLA_V).astype(x.dtype) * jax.nn.silu(gr)
    p_a = jnp.einsum('btv,vd->btd', o_a, w_br_gla)

    qh = jax.nn.silu(hq.astype(f32)).reshape(B, T, HGRN_HEADS, HGRN_EXPAND)
    lbf = lb.astype(f32)
    hf32 = hf.astype(f32)
    f = lbf + (1.0 - lbf) * jax.nn.sigmoid(hf32)
    log_f = jnp.log(f).reshape(B, T, HGRN_HEADS, HGRN_EXPAND)
    kh = ((1.0 - lbf) * jax.nn.sigmoid(-hf32)).reshape(B, T, HGRN_HEADS, HGRN_EXPAND)
    ih = hi.astype(f32).reshape(B, T, HGRN_HEADS, HGRN_DV)
    o_b, s_hgrn_new = _chunk_gated_linear(qh, kh, ih, log_f, s_hgrn)
    o_b = _head_rmsnorm(o_b, hgrn_norm_g).reshape(B, T, HGRN_V).astype(x.dtype) * jax.nn.silu(hr)
    p_b = jnp.einsum('btv,vd->btd', o_b, w_br_hgrn)

    merged = jax.nn.sigmoid(mg_a) * p_a + jax.nn.sigmoid(mg_b) * p_b
    y = jnp.einsum('btd,de->bte', merged, w_out)

    z = (DEEPNORM_ALPHA * x + y).astype(f32)
    mu = jnp.mean(z, axis=-1, keepdims=True)
    zc = z - mu
    var = jnp.mean(zc * zc, axis=-1, keepdims=True)
    out = zc * lax.rsqrt(var + NORM_EPS) * ln_g.astype(f32) + ln_b.astype(f32)
    return out.astype(x.dtype), s_gla_new.astype(s_gla.dtype), s_hgrn_new.astype(s_hgrn.dtype)


def setup_inputs(seed: int = 0) -> dict:
    key = jax.random.key(seed)
    ks = jax.random.split(key, 16)
    f32 = jnp.float32
    nrm = lambda k, shape, s: (jax.random.normal(k, shape, f32) * s)
    return {
        "x_prompt": nrm(ks[0], (BATCH, SEQ, D_MODEL), 1.0),
        "x_sample": nrm(ks[1], (DEC_BATCH, DEC_SEQ, D_MODEL), 1.0),
        "state_gla": nrm(ks[2], (DEPTH, DEC_BATCH, GLA_HEADS, GLA_DK, GLA_DV), 1.0),
        "state_hgrn": nrm(ks[3], (DEPTH, DEC_BATCH, HGRN_HEADS, HGRN_EXPAND, HGRN_DV), 0.5),
        "w_in": nrm(ks[4], (DEPTH, D_MODEL, IN_DIM), D_MODEL ** -0.5),
        "w_gate_lr": nrm(ks[5], (DEPTH, GLA_RANK, GLA_K), GLA_RANK ** -0.5),
        "b_gate_lr": nrm(ks[6], (DEPTH, GLA_K), 0.1),
        "gla_norm_g": 1.0 + nrm(ks[7], (DEPTH, GLA_HEADS, GLA_DV), 0.02),
        "w_br_gla": nrm(ks[8], (DEPTH, GLA_V, D_MODEL), DEEPNORM_BETA * GLA_V ** -0.5),
        "hgrn_lb_param": nrm(ks[9], (DEPTH + 1, HGRN_K), 0.1),
        "hgrn_norm_g": 1.0 + nrm(ks[10], (DEPTH, HGRN_HEADS, HGRN_DV), 0.02),
        "w_br_hgrn": nrm(ks[11], (DEPTH, HGRN_V, D_MODEL), DEEPNORM_BETA * HGRN_V ** -0.5),
        "w_out": nrm(ks[12], (DEPTH, D_MODEL, D_MODEL), DEEPNORM_BETA * D_MODEL ** -0.5),
        "ln_g": 1.0 + nrm(ks[13], (DEPTH, D_MODEL), 0.02),
        "ln_b": nrm(ks[14], (DEPTH, D_MODEL), 0.02),
    }


def reference(x_prompt, x_sample, state_gla, state_hgrn, w_in, w_gate_lr, b_gate_lr,
              gla_norm_g, w_br_gla, hgrn_lb_param, hgrn_norm_g, w_br_hgrn, w_out, ln_g, ln_b):
    lb_all = jnp.cumsum(jax.nn.softmax(hgrn_lb_param.astype(jnp.float32), axis=0), axis=0)
    yp, ys = x_prompt, x_sample
    gla_p, hgrn_p, gla_s, hgrn_s = [], [], [], []
    for l in range(DEPTH):
        weights = (w_in[l], w_gate_lr[l], b_gate_lr[l], gla_norm_g[l], w_br_gla[l], lb_all[l],
                   hgrn_norm_g[l], w_br_hgrn[l], w_out[l], ln_g[l], ln_b[l])
        s0_gla = jnp.zeros((yp.shape[0], GLA_HEADS, GLA_DK, GLA_DV), state_gla.dtype)
        s0_hgrn = jnp.zeros((yp.shape[0], HGRN_HEADS, HGRN_EXPAND, HGRN_DV), state_hgrn.dtype)
        yp, sg_p, sh_p = _mixer_layer(yp, s0_gla, s0_hgrn, *weights)
        ys, sg_s, sh_s = _mixer_layer(ys, state_gla[l], state_hgrn[l], *weights)
        gla_p.append(sg_p)
        hgrn_p.append(sh_p)
        gla_s.append(sg_s)
        hgrn_s.append(sh_s)
    new_gla_prompt = jnp.stack(gla_p, axis=0)
    new_hgrn_prompt = jnp.stack(hgrn_p, axis=0)
    new_gla_sample = jnp.stack(gla_s, axis=0)
    new_hgrn_sample = jnp.stack(hgrn_s, axis=0)
    return (yp, ys, new_gla_prompt, new_hgrn_prompt, new_gla_sample, new_hgrn_sample)
```

```python
import numpy as np
import concourse.bass as bass
import concourse.mybir as mybir
from concourse.bass_utils import run_bass_kernel_spmd

F32 = mybir.dt.float32
BF16 = mybir.dt.bfloat16
AF = mybir.ActivationFunctionType
ALU = mybir.AluOpType

NCORES = 8
D = 1024
T = 2048
NS = 16
TT = T + NS
NBLK = 4
BLK = 512
IN_DIM = 9232
GA_OFF = 3072
HQ_OFF = 3088
HF_OFF = HQ_OFF + 1024
HI_OFF = HQ_OFF + 2048
HR_OFF = HQ_OFF + 3072
MGA_OFF = HQ_OFF + 4096
MGB_OFF = MGA_OFF + 1024
ALPHA = 2.0 ** 0.25
EPS = 1e-5


class Sched:
    def __init__(self, nc, cache, me, eobj):
        self.nc = nc
        self.cache = cache
        self.me = me
        self.e = eobj
        self.cnt = {k: 0 for k in ("pe", "act", "dve", "pool")}
        self.waited = {}
        self.last_w = {}
        self.readers = {}
        self.dcnt = {}
        self.capture = None

    def emit(self, rec, eng=None):
        if rec[0] == "op":
            self.op(rec[1], rec[2], rec[3], rec[4])
        elif rec[0] == "flex":
            out, in_ = rec[2]
            if eng == "act":
                self.op("act", lambda e: e.activation(out=out, in_=in_, func=AF.Copy), rec[3], rec[4])
            else:
                self.op("dve", lambda e: e.tensor_copy(out, in_), rec[3], rec[4])
        else:
            self.dma(rec[1], rec[2][0], rec[2][1], rec[2][2], rec[3], rec[4])

    def copy(self, out, in_, R=(), W=()):
        W = list(W) + [k for k in R if len(k) == 3 and k[0] == "P" and k[1] in "ABDX"]
        if self.capture is None:
            self.op("dve", lambda e: e.tensor_copy(out, in_), R, W)
            return
        size = 1
        for v in out.shape[1:]:
            size *= v
        self.capture.append(("flex", "dve", (out, in_), list(R), list(W), 230 + 0.83 * size, 120 + 1.12 * size))

    def _semh(self, s):
        k = "sem_" + s
        if k not in self.cache:
            self.cache[k] = self.nc.alloc_semaphore("s_" + s)
        if s not in self.cnt and s not in self.dcnt:
            self.dcnt[s] = 0
        return self.cache[k]

    def _deps(self, R, W):
        best = {}
        def add(sv):
            s, v = sv
            if v > best.get(s, 0):
                best[s] = v
        for b in R:
            if b in self.last_w:
                add(self.last_w[b])
        for b in W:
            if b in self.last_w:
                add(self.last_w[b])
            for sv in self.readers.get(b, {}).items():
                add(sv)
        return best

    def _wait(self, eng, best):
        for s, v in best.items():
            if s == "pe" and eng == "pe":
                continue
            if self.waited.get((eng, s), 0) >= v:
                continue
            h = self._semh(s)
            if eng == self.me:
                self.e.wait_ge(h, v)
            self.waited[(eng, s)] = v

    def _book(self, me, R, W):
        s, v = me
        for b in R:
            d = self.readers.setdefault(b, {})
            if v > d.get(s, 0):
                d[s] = v
        for b in W:
            self.last_w[b] = me
            self.readers[b] = {}

    def op(self, eng, fn, R=(), W=()):
        W = list(W) + [k for k in R if len(k) == 3 and k[0] == "P" and k[1] in "ABDX"]
        if self.capture is not None:
            fe = _FakeEng(eng)
            fn(fe)
            calls = fe.calls

            def replay(e, calls=calls):
                r = None
                for name, args, kw in calls:
                    r = getattr(e, name)(*args, **kw)
                return r
            self.capture.append(("op", eng, replay, list(R), list(W), fe.dur, fe.tset))
            return
        self._wait(eng, self._deps(R, W))
        self.cnt[eng] += 1
        h = self._semh(eng)
        if eng == self.me:
            fn(self.e).then_inc(h, 1)
        self._book((eng, self.cnt[eng]), R, W)

    def dma(self, q, out, in_, sem, R=(), W=()):
        if self.capture is not None:
            self.capture.append(("dma", q, (out, in_, sem), list(R), list(W)))
            return
        self._wait(q, self._deps(R, W))
        h = self._semh(sem)
        self.dcnt[sem] += 16
        if q == self.me:
            self.e.dma_start(out=out, in_=in_).then_inc(h, 16)
        self._book((sem, self.dcnt[sem]), R, W)

    def final_wait(self, eng, sems):
        for s in sems:
            if self.dcnt.get(s, 0) > 0 and eng == self.me:
                self.e.wait_ge(self._semh(s), self.dcnt[s])


class _FakeIns:
    def then_inc(self, *a, **k):
        return self


class _FakeEng:
    def __init__(self, kind):
        self.kind = kind
        self.dur = 0.0
        self.tset = None
        self.calls = []

    def __getattr__(self, name):
        def call(*args, **kw):
            self.calls.append((name, args, kw))
            out = kw.get("out", args[0] if args else None)
            size = 1
            try:
                for v in out.shape[1:]:
                    size *= v
            except Exception:
                size = 256
            if name == "matmul":
                self.dur += 70 + 0.62 * size
            elif self.kind == "act":
                self.dur += 230 + 0.83 * size
                f = kw.get("func")
                if f in (AF.Silu, AF.Tanh):
                    self.tset = 18
                elif f in (AF.Exp, AF.Ln):
                    self.tset = 6
            elif self.kind == "dve":
                self.dur += (120 + 1.12 * size) * (2.0 if name == "tensor_tensor_scan" else 1.0)
            elif self.kind == "pool":
                self.dur += 320 + 1.6 * size
            else:
                self.dur += 100
            return _FakeIns()
        return call


class ListScheduler:
    def __init__(self):
        self.free = {k: 0.0 for k in ("pe", "act", "dve", "pool", "sp")}
        self.wfin = {}
        self.rfin = {}
        self.tset = None

    def schedule(self, recs):
        n = len(recs)
        dur = [0.0] * n
        lat = [0.0] * n
        tset = [None] * n
        for i, r in enumerate(recs):
            if r[0] == "op":
                dur[i] = r[5] + 60.0
                lat[i] = dur[i]
                tset[i] = r[6]
            elif r[0] == "flex":
                dur[i] = min(r[5], r[6]) + 60.0
                lat[i] = dur[i]
            else:
                dur[i] = 1000.0 if r[1] == "pool" else 150.0
                lat[i] = dur[i] + 2600.0
        preds = [set() for _ in range(n)]
        lw, rd = {}, {}
        for i, r in enumerate(recs):
            R, W = r[3], r[4]
            for k in R:
                if k in lw:
                    preds[i].add(lw[k])
            for k in W:
                if k in lw:
                    preds[i].add(lw[k])
                for j in rd.get(k, ()):
                    preds[i].add(j)
            for k in R:
                rd.setdefault(k, []).append(i)
            for k in W:
                lw[k] = i
                rd[k] = []
            preds[i].discard(i)
        succs = [[] for _ in range(n)]
        for i in range(n):
            for j in preds[i]:
                succs[j].append(i)
        prio = [0.0] * n
        for i in range(n - 1, -1, -1):
            m = 0.0
            for j in succs[i]:
                if prio[j] > m:
                    m = prio[j]
            prio[i] = lat[i] + m
        base = [0.0] * n
        for i, r in enumerate(recs):
            b = 0.0
            for k in r[3]:
                b = max(b, self.wfin.get(k, 0.0))
            for k in r[4]:
                b = max(b, self.wfin.get(k, 0.0), self.rfin.get(k, 0.0))
            base[i] = b
        npred = [len(p) for p in preds]
        fin = [0.0] * n
        ready = [i for i in range(n) if npred[i] == 0]
        order = []
        choice = {}
        while ready:
            best, bkey = None, None
            for i in ready:
                dep = base[i]
                for j in preds[i]:
                    if fin[j] + 80.0 > dep:
                        dep = fin[j] + 80.0
                if recs[i][0] == "flex":
                    sa = max(self.free["act"], dep)
                    sd = max(self.free["dve"], dep)
                    if sa + recs[i][5] < sd + recs[i][6]:
                        eng, st, du = "act", sa, recs[i][5] + 60.0
                    else:
                        eng, st, du = "dve", sd, recs[i][6] + 60.0
                else:
                    eng = recs[i][1]
                    st = max(self.free[eng], dep)
                    du = dur[i]
                    if eng == "act" and tset[i] is not None and self.tset is not None and tset[i] != self.tset:
                        st += 2600.0
                key = (round(st / 600.0), -prio[i], i)
                if bkey is None or key < bkey:
                    best, bkey, bst, beng, bdu = i, key, st, eng, du
            i = best
            eng = beng
            if recs[i][0] == "flex":
                choice[i] = eng
                lat[i] = bdu
            if eng == "act" and tset[i] is not None:
                self.tset = tset[i]
            self.free[eng] = bst + bdu
            fin[i] = bst + lat[i]
            order.append(i)
            ready.remove(i)
            for j in succs[i]:
                npred[j] -= 1
                if npred[j] == 0:
                    ready.append(j)
        assert len(order) == n
        for i, r in enumerate(recs):
            for k in r[3]:
                self.rfin[k] = max(self.rfin.get(k, 0.0), fin[i])
            for k in r[4]:
                self.wfin[k] = fin[i]
                self.rfin[k] = 0.0
        return order, choice


class SBAlloc:
    def __init__(self, nc, cache):
        self.nc = nc
        self.cache = cache
        self.cur = (nc.sbuf_base + 63) // 64 * 64
        self.top = nc.sbuf_top
        self.n = 0

    def alloc(self, name, shape, dtype):
        isz = 2 if dtype == BF16 else 4
        size = isz
        for s in shape[1:]:
            size *= s
        size = (size + 63) // 64 * 64
        assert self.cur + size <= self.top, (name, self.cur, size, self.top)
        self.n += 1
        k = f"sb_{name}_{self.n}"
        if k not in self.cache:
            self.cache[k] = self.nc.alloc_sbuf_tensor_at(f"{name}_{self.n}", list(shape), dtype, offset=self.cur)
        self.cur += size
        return self.cache[k]


def build_nc():
    nc = bass.Bass("TRN2", target_bir_lowering=False)
    dt_in = lambda n, s: nc.dram_tensor(n, list(s), F32, kind="ExternalInput").ap()
    dt_out = lambda n, s: nc.dram_tensor(n, list(s), F32, kind="ExternalOutput").ap()
    xT_d = dt_in("xT", (D, TT))
    xtok_d = dt_in("xtok", (TT, D))
    sg_d = dt_in("sg", (NS, 4, 128, 256))
    sh_d = dt_in("sh", (NS, 8, 128, 128))
    win_d = dt_in("w_in", (D, IN_DIM))
    wlr_d = dt_in("wlr", (16, 512))
    bgl_d = dt_in("bgl", (128, 4))
    glag_d = dt_in("glag", (128, 1024))
    lbp_d = dt_in("lbp", (128, 16))
    hgg_d = dt_in("hgg", (128, 1024))
    wbg_d = dt_in("wbg", (D, D))
    wbh_d = dt_in("wbh", (D, D))
    wout_d = dt_in("wout", (D, D))
    lng_d = dt_in("lng", (128, D))
    lnb_d = dt_in("lnb", (128, D))
    y_d = dt_out("y", (TT, D))
    gp_d = dt_out("gp", (4, 128, 256))
    hp_d = dt_out("hp", (8, 128, 128))
    gs_d = dt_out("gs", (NS, 4, 128, 256))
    hs_d = dt_out("hs", (NS, 8, 128, 128))

    PA = [nc.alloc_psum_tensor(f"PA{i}", [128, 512], F32) for i in range(2)]
    PB = [nc.alloc_psum_tensor(f"PB{i}", [128, 512], F32) for i in range(2)]
    PD = [nc.alloc_psum_tensor(f"PD{i}", [128, 512], F32) for i in range(2)]
    PX = [nc.alloc_psum_tensor(f"PX{i}", [128, 512], F32) for i in range(2)]
    cache = {}

    def program(me, eobj):
        S = Sched(nc, cache, me, eobj)
        sb = SBAlloc(nc, cache)
        win_r = win_d.rearrange("(c p) n -> p c n", p=128)

        xT = sb.alloc("xT", [128, 8, TT], BF16)
        onT = sb.alloc("onT", [128, 16, TT], BF16)
        ident_f = sb.alloc("identf", [128, 128], F32)
        ident_b = sb.alloc("identb", [128, 128], BF16)
        U4 = sb.alloc("U4", [128, 4, 128], F32)
        msk = sb.alloc("msk", [128, 4, 128], F32)
        idrow = sb.alloc("idrow", [128, 16, 16], F32)
        negb = sb.alloc("negb", [128, 4], F32)
        lbp = sb.alloc("lbp", [128, 16], F32)
        c1 = sb.alloc("c1", [128, 8], F32)
        nc1 = sb.alloc("nc1", [128, 8], F32)
        c2 = sb.alloc("c2", [128, 8], F32)
        region_mark = sb.cur
        wga = sb.alloc("wga", [128, 8, 16], BF16)
        wlr = sb.alloc("wlr", [16, 512], BF16)
        gaT = sb.alloc("gaT", [16, TT], BF16)

        rot = {"A": 0, "B": 0, "D": 0, "X": 0, "K": 0, "O": 0}

        def nxt(k):
            rot[k] ^= 1
            return rot[k]

        S.op("pool", lambda e: e.memset(ident_f[:], 1.0), W=["identf"])
        S.op("pool", lambda e: e.affine_select(out=ident_f[:], in_=ident_f[:], pattern=[[1, 128]],
                                               compare_op=ALU.is_equal, fill=0.0, base=0,
                                               channel_multiplier=-1), R=["identf"], W=["identf"])
        S.op("pool", lambda e: e.memset(U4[:], 1.0), W=["U4"])
        S.op("pool", lambda e: e.affine_select(out=U4[:], in_=U4[:], pattern=[[0, 4], [1, 128]],
                                               compare_op=ALU.is_ge, fill=0.0, base=0,
                                               channel_multiplier=-1), R=["U4"], W=["U4"])
        S.op("pool", lambda e: e.memset(msk[:], 1.0), W=["msk"])
        S.op("pool", lambda e: e.memset(msk[:, :, 0:1], 0.0), R=["msk"], W=["msk"])
        S.op("pool", lambda e: e.memset(idrow[:], 1.0), W=["idrow"])
        S.op("pool", lambda e: e.affine_select(out=idrow[:], in_=idrow[:], pattern=[[1, 16], [-1, 16]],
                                               compare_op=ALU.is_equal, fill=0.0, base=0,
                                               channel_multiplier=0), R=["idrow"], W=["idrow"])
        S.op("dve", lambda e: e.tensor_copy(ident_b[:], ident_f[:]), R=["identf"], W=["identb"])

        S.dma("sp", negb[:], bgl_d, "ld_negb", W=["negb"])
        S.dma("sp", lbp[:], lbp_d, "ld_lbp", W=["lbp"])
        S.op("dve", lambda e: e.tensor_scalar(negb[:], negb[:], -1.0, None, op0=ALU.mult), R=["negb"], W=["negb"])
        S.op("dve", lambda e: e.tensor_tensor(c2[:], lbp[:, 0:8], lbp[:, 8:16], op=ALU.subtract), R=["lbp"], W=["c2"])
        S.op("act", lambda e: e.activation(out=c2[:], in_=c2[:], func=AF.Tanh, scale=0.5), R=["c2"], W=["c2"])
        S.op("dve", lambda e: e.tensor_scalar(c1[:], c2[:], -0.25, 0.25, op0=ALU.mult, op1=ALU.add), R=["c2"], W=["c1"])
        S.op("dve", lambda e: e.tensor_scalar(nc1[:], c2[:], 0.25, -0.25, op0=ALU.mult, op1=ALU.add), R=["c2"], W=["nc1"])
        S.op("dve", lambda e: e.tensor_scalar(c2[:], c2[:], 0.25, 0.75, op0=ALU.mult, op1=ALU.add), R=["c2", "c1", "nc1"], W=["c2"])

        S.dma("pool", wga[:], win_r[:, :, GA_OFF:GA_OFF + 16], "ld_wga", W=["wga"])
        S.dma("pool", wlr[:], wlr_d, "ld_wlr", W=["wlr"])
        xT_r = xT_d.rearrange("(c p) n -> p c n", p=128)
        def XK(t0):
            return [f"xT{c}_{min(t0 // BLK, NBLK)}" for c in range(8)]

        def load_xT(bi):
            c0 = bi * BLK
            n = BLK if bi < NBLK else NS
            for c in range(8):
                S.dma("pool", xT[:, c, c0:c0 + n], xT_r[:, c, c0:c0 + n], f"ld_xT_{bi}", W=[f"xT{c}_{bi}"])
        load_xT(0)

        wu = [sb.alloc(f"wu{i}", [128, 8, 1024], BF16) for i in range(2)]
        g_u = [sb.alloc(f"g_u{i}", [128, 256], F32) for i in range(2)]
        GT = 1
        NSL = 6
        S0b = [sb.alloc(f"S0b{i}", [128, GT, 256], F32) for i in range(NSL)]
        S0bf = [sb.alloc(f"S0bf{i}", [128, 256], BF16) for i in range(2)]
        th = [sb.alloc(f"th_{i}", [128, 512], F32) for i in range(2)]
        sq = [sb.alloc(f"sq_{i}", [128, 512], F32) for i in range(2)]
        g1 = sb.alloc("g1", [128, 4, 128], F32)
        Eb = sb.alloc("Eb", [128, 4, 128], F32)
        keT = sb.alloc("keT", [128, 4, 128], BF16)
        kdT = sb.alloc("kdT", [128, 4, 128], BF16)
        qeT = [[sb.alloc(f"qeT_{p}{i}", [128, 512], BF16) for i in range(2)] for p in range(2)]
        kd = [[sb.alloc(f"kd_{p}{i}", [128, 512], BF16) for i in range(2)] for p in range(2)]
        ATb = [[sb.alloc(f"ATb_{p}{i}", [128, 4, 128], BF16) for i in range(2)] for p in range(2)]
        EbL = [[sb.alloc(f"EbL_{p}{i}", [128, 4], F32) for i in range(2)] for p in range(2)]
        vbf = [sb.alloc(f"vbf{p}", [128, 4, 256], BF16) for p in range(3)]
        ug = [sb.alloc(f"ug{p}", [128, 4, 256], F32) for p in range(3)]
        Sst = sb.alloc("Sst", [128, 256], F32)
        Sbf = [sb.alloc(f"Sbf{i}", [128, 4, 256], BF16) for i in range(2)]
        onb = [sb.alloc("onb0", [128, 4, 256], BF16), sb.alloc("onb1", [128, 4, 128], BF16)]
        junk = sb.alloc("junk", [128, 256], BF16)
        ssq = sb.alloc("ssq", [128, 8], F32)
        rstd = sb.alloc("rstd", [128, 8], F32)
        eps_t = sb.alloc("eps_t", [128, 1], F32)
        one_t = sb.alloc("one_t", [128, 1], F32)
        s_e = sb.alloc("s_e", [128, 2, NS], F32)
        s_g = sb.alloc("s_g", [128, 2, NS], F32)
        s_q = sb.alloc("s_q", [128, 2, NS], F32)
        s_qb = sb.alloc("s_qb", [128, 2, NS], BF16)
        s_qe = sb.alloc("s_qe", [128, 2, NS], F32)
        s_k = sb.alloc("s_k", [128, 2, NS], F32)
        s_kb = sb.alloc("s_kb", [128, 2, NS], BF16)
        ktok = sb.alloc("ktok", [16, 2, 128], F32)
        Ks = [sb.alloc(f"Ks{i}", [16, 128], BF16) for i in range(2)]
        Qsel = sb.alloc("Qsel", [128, 2, NS, NS], BF16)
        qkd = sb.alloc("qkd", [16, 2, NS], BF16)
        s_v = sb.alloc("s_v", [16, 256], BF16)
        s_u = sb.alloc("s_u", [16, 256], F32)
        s_on = sb.alloc("s_on", [16, 256], BF16)
        S.op("dve", lambda e: e.memset(eps_t[:], EPS), W=["eps_t"])
        S.op("dve", lambda e: e.memset(one_t[:], 1.0), W=["one_t"])

        def load_unit_weights(u, slot):
            w = wu[slot]
            key = f"wu{slot}"
            if u < 4:
                h = u
                segs = [(0, h * 128, 128), (128, 512 + h * 128, 128), (256, 1024 + h * 256, 256),
                        (512, 2048 + h * 256, 256)]
            else:
                j = u - 4
                segs = [(0, HQ_OFF + j * 256, 256), (256, HF_OFF + j * 256, 256),
                        (512, HI_OFF + j * 256, 256), (768, HR_OFF + j * 256, 256)]
            for (o, c0, n) in segs:
                S.dma("pool", w[:, :, o:o + n], win_r[:, :, c0:c0 + n], f"ld_wu{slot}", W=[key])

        order = [0, 1, 2, 3, 4, 5, 6, 7]
        load_unit_weights(order[0], 0)
        for bi_ in range(1, NBLK + 1):
            load_xT(bi_)
        load_unit_weights(order[1], 1)

        for bi in range(NBLK + 1):
            t0 = bi * BLK
            n = BLK if bi < NBLK else NS
            a = nxt("A")
            def f(e, a=a, t0=t0, n=n):
                for c in range(8):
                    r = e.matmul(PA[a][0:16, 0:n], lhsT=wga[:, c, :], rhs=xT[:, c, t0:t0 + n],
                                 start=(c == 0), stop=(c == 7))
                return r
            S.op("pe", f, R=["wga"] + XK(t0), W=[f"PA{a}"])
            S.op("act", lambda e, a=a, t0=t0, n=n: e.activation(out=gaT[:, t0:t0 + n], in_=PA[a][0:16, 0:n], func=AF.Copy),
                 R=[f"PA{a}"], W=["gaT"])

        def proj_fm(slot, woff, t0, n):
            a = nxt("A")
            def f(e):
                for c in range(8):
                    r = e.matmul(PA[a][:, 0:n], lhsT=wu[slot][:, c, woff:woff + 128], rhs=xT[:, c, t0:t0 + n],
                                 start=(c == 0), stop=(c == 7))
                return r
            S.op("pe", f, R=[f"wu{slot}"] + XK(t0), W=[f"PA{a}"])
            return a

        def proj_tm(slot, woff, t0, m):
            b = nxt("B")
            def f(e):
                for c in range(8):
                    r = e.matmul(PB[b][0:m, :], lhsT=xT[:, c, t0:t0 + m], rhs=wu[slot][:, c, woff:woff + 512],
                                 start=(c == 0), stop=(c == 7))
                return r
            S.op("pe", f, R=[f"wu{slot}"] + XK(t0), W=[f"PB{b}"])
            return b

        def rstd_from(ssq_ap, rstd_ap, m, dv, keys_r, keys_w):
            S.op("act", lambda e: e.activation(out=rstd_ap, in_=ssq_ap, func=AF.Ln, scale=1.0 / dv, bias=eps_t[0:m, :]),
                 R=keys_r, W=keys_w)
            S.op("act", lambda e: e.activation(out=rstd_ap, in_=rstd_ap, func=AF.Exp, scale=-0.5),
                 R=keys_w, W=keys_w)

        class Item:
            pass

        items = []
        for ui, u in enumerate(order):
            for bi in list(range(NBLK)) + ["s"]:
                it = Item()
                it.ui, it.u, it.slot, it.bi = ui, u, ui % 2, bi
                it.p = len(items) % 2
                it.q3 = len(items) % 3
                it.gla = u < 4
                it.nh = 1 if it.gla else 2
                it.DV = 256 if it.gla else 128
                it.vr_off = 256 if it.gla else 512
                it.vc0 = 2 * u if it.gla else 8 + 2 * (u - 4)
                it.gk = f"g_u{ui % 2}"
                items.append(it)

        def hd_of(it, e_):
            return 2 * (it.u - 4) + e_

        def vsl_of(it, e_):
            return slice(0, 256) if it.gla else slice(e_ * 128, (e_ + 1) * 128)

        def alpha1(it):
            slot, q3 = it.slot, it.q3
            gu = g_u[it.ui % 2]
            if it.bi == 0:
                src = glag_d[:, it.u * 256:(it.u + 1) * 256] if it.gla else hgg_d[:, (it.u - 4) * 256:(it.u - 3) * 256]
                S.dma("sp", gu[:], src, f"ld_gu{it.ui % 2}", W=[it.gk])
            if it.bi == "s":
                b = proj_tm(slot, it.vr_off, T, NS)
                S.op("act", lambda e: e.activation(out=s_v[:], in_=PB[b][0:NS, 0:256], func=AF.Copy), R=[f"PB{b}"], W=["s_v"])
                S.op("act", lambda e: e.activation(out=s_u[:], in_=PB[b][0:NS, 256:512], func=AF.Copy), R=[f"PB{b}"], W=["s_u"])
                if not it.gla:
                    for e_ in range(2):
                        a = proj_fm(slot, e_ * 128, T, NS)
                        S.op("act", lambda e, a=a, e_=e_: e.activation(out=s_q[:, e_, :], in_=PA[a][:, 0:NS], func=AF.Copy),
                             R=[f"PA{a}"], W=["s_q"])
                        a = proj_fm(slot, 256 + e_ * 128, T, NS)
                        S.op("dve", lambda e, a=a, e_=e_: e.tensor_copy(s_k[:, e_, :], PA[a][:, 0:NS]),
                             R=[f"PA{a}"], W=["s_k"])
                for g in range(NSL):
                    load_S0(it, g)
                yield
                return
            t0 = it.bi * BLK
            for i in range(4):
                b = proj_tm(slot, it.vr_off, t0 + i * 128, 128)
                S.copy(vbf[q3][:, i, :], PB[b][:, 0:256], R=[f"PB{b}"], W=[f"vbf{q3}_{i}"])
                S.copy(ug[q3][:, i, :], PB[b][:, 256:512], R=[f"PB{b}"], W=[f"ug{q3}_{i}"])
                yield
            yield "TILES_DONE"
            if not it.gla:
                yield "WAIT_BETA"
                for e_ in range(2):
                    a = proj_fm(slot, e_ * 128, t0, BLK)
                    S.copy(sq[e_][:], PA[a][:], R=[f"PA{a}"], W=[f"sq_{e_}"])
                    yield
                    a = proj_fm(slot, 256 + e_ * 128, t0, BLK)
                    S.copy(th[e_][:], PA[a][:], R=[f"PA{a}"], W=[f"th_{e_}"])
                    yield

        def alpha2(it):
            q3 = it.q3
            gu = g_u[it.ui % 2]
            if it.bi == "s":
                S.op("act", lambda e: e.activation(out=s_u[:], in_=s_u[:], func=AF.Silu), R=["s_u"], W=["s_u"])
                S.op("pool", lambda e: e.tensor_tensor(s_u[:], s_u[:], gu[0:NS, :], op=ALU.mult), R=["s_u", it.gk], W=["s_u"])
                if not it.gla:
                    S.op("act", lambda e: e.activation(out=s_q[:], in_=s_q[:], func=AF.Silu), R=["s_q"], W=["s_q"])
                    S.op("act", lambda e: e.activation(out=s_k[:], in_=s_k[:], func=AF.Tanh, scale=0.5), R=["s_k"], W=["s_k"])
                return
            ugk = [f"ug{q3}_{i}" for i in range(4)]
            S.op("act", lambda e: e.activation(out=ug[q3][:], in_=ug[q3][:], func=AF.Silu), R=ugk, W=ugk)
            for i in range(4):
                S.op("pool", lambda e, i=i: e.tensor_tensor(ug[q3][:, i, :], ug[q3][:, i, :], gu[:], op=ALU.mult),
                     R=[f"ug{q3}_{i}", it.gk], W=[f"ug{q3}_{i}"])
            if not it.gla:
                for e_ in range(2):
                    S.op("act", lambda e, e_=e_: e.activation(out=sq[e_][:], in_=sq[e_][:], func=AF.Silu),
                         R=[f"sq_{e_}"], W=[f"sq_{e_}"])
                    S.op("act", lambda e, e_=e_: e.activation(out=th[e_][:], in_=th[e_][:], func=AF.Tanh, scale=0.5),
                         R=[f"th_{e_}"], W=[f"th_{e_}"])

        def load_S0(it, g):
            sl = g % NSL
            n0 = g * GT
            if it.gla:
                S.dma("sp", S0b[sl][:], sg_d[n0:n0 + GT, it.u].rearrange("n k v -> k n v"), f"ld_S0{sl}", W=[f"S0b{sl}"])
            else:
                j = it.u - 4
                for hh in range(2):
                    S.dma("sp", S0b[sl][:, :, hh * 128:(hh + 1) * 128],
                          sh_d[n0:n0 + GT, 2 * j + hh].rearrange("n k v -> k n v"), f"ld_S0{sl}", W=[f"S0b{sl}"])

        def beta(it):
            slot, p = it.slot, it.p
            g1f = g1[:].rearrange("p c t -> p (c t)")
            Ebf = Eb[:].rearrange("p c t -> p (c t)")
            keTf = keT[:].rearrange("p c t -> p (c t)")
            mskf = msk[:].rearrange("p c t -> p (c t)")
            if it.bi == "s":
                nh = it.nh
                for e_ in range(nh):
                    if it.gla:
                        h = it.u
                        a = nxt("A")
                        S.op("pe", lambda e, a=a, h=h: e.matmul(PA[a][:, 0:NS], lhsT=wlr[:, h * 128:(h + 1) * 128],
                                                              rhs=gaT[:, T:T + NS], start=True, stop=True),
                             R=["wlr", "gaT"], W=[f"PA{a}"])
                        S.op("act", lambda e, a=a, h=h, e_=e_: e.activation(out=s_g[:, e_, :], in_=PA[a][:, 0:NS], func=AF.Exp,
                                                                          scale=-1.0, bias=negb[:, h:h + 1]),
                             R=[f"PA{a}", "negb"], W=["s_g"])
                        S.op("act", lambda e, e_=e_: e.activation(out=s_g[:, e_, :], in_=s_g[:, e_, :], func=AF.Ln, scale=1.0,
                                                                  bias=one_t[:]), R=["s_g", "one_t"], W=["s_g"])
                        sE = -1.0 / 16.0
                        a = proj_fm(slot, 128, T, NS)
                        S.op("act", lambda e, a=a, e_=e_: e.activation(out=s_k[:, e_, :], in_=PA[a][:, 0:NS], func=AF.Copy),
                             R=[f"PA{a}"], W=["s_k"])
                        a = proj_fm(slot, 0, T, NS)
                        S.op("act", lambda e, a=a, e_=e_: e.activation(out=s_q[:, e_, :], in_=PA[a][:, 0:NS], func=AF.Identity,
                                                                      scale=128.0 ** -0.5), R=[f"PA{a}"], W=["s_q"])
                    else:
                        hd = hd_of(it, e_)
                        S.op("act", lambda e, e_=e_, hd=hd: e.activation(out=s_g[:, e_, :], in_=s_k[:, e_, :], func=AF.Ln,
                                                                        scale=c1[:, hd:hd + 1], bias=c2[:, hd:hd + 1]),
                             R=["s_k", "c1", "c2"], W=["s_g"])
                        S.op("dve", lambda e, e_=e_, hd=hd: e.tensor_scalar(s_k[:, e_, :], s_k[:, e_, :], nc1[:, hd:hd + 1],
                                                                           c1[:, hd:hd + 1], op0=ALU.mult, op1=ALU.add),
                             R=["s_k", "c1", "nc1", "s_g"], W=["s_k"])
                        sE = 1.0
                    S.op("act", lambda e, e_=e_, sE=sE: e.activation(out=s_e[:, e_, :], in_=s_g[:, e_, :], func=AF.Exp, scale=sE),
                         R=["s_g"], W=["s_e"])
                    yield
                S.op("dve", lambda e: e.tensor_copy(s_kb[:, 0:nh, :], s_k[:, 0:nh, :]), R=["s_k"], W=["s_kb"])
                S.op("dve", lambda e: e.tensor_copy(s_qb[:, 0:nh, :], s_q[:, 0:nh, :]), R=["s_q"], W=["s_qb"])
                S.op("dve", lambda e: e.tensor_tensor(s_qe[:, 0:nh, :], s_q[:, 0:nh, :], s_e[:, 0:nh, :], op=ALU.mult),
                     R=["s_q", "s_e"], W=["s_qe"])
                S.op("dve", lambda e: e.tensor_tensor(
                    Qsel[:, 0:nh, :, :], s_qe[:, 0:nh, :].unsqueeze(3).broadcast_to([128, nh, NS, NS]),
                    idrow[:].unsqueeze(1).broadcast_to([128, nh, NS, NS]), op=ALU.mult), R=["s_qe", "idrow"], W=["Qsel"])
                yield
                for e_ in range(nh):
                    x = nxt("X")
                    S.op("pe", lambda e, e_=e_, x=x: e.matmul(PX[x][0:NS, 0:128], lhsT=s_kb[:, e_, :], rhs=ident_b[:],
                                                            start=True, stop=True), R=["s_kb", "identb"], W=[f"PX{x}"])
                    S.op("dve", lambda e, e_=e_, x=x: e.tensor_copy(ktok[:, e_, :], PX[x][0:NS, 0:128]),
                         R=[f"PX{x}"], W=["ktok"])
                    x = nxt("X")
                    S.op("pe", lambda e, e_=e_, x=x: e.matmul(PX[x][0:NS, 0:NS], lhsT=s_qb[:, e_, :], rhs=s_kb[:, e_, :],
                                                            start=True, stop=True), R=["s_kb", "s_qb"], W=[f"PX{x}"])
                    S.op("dve", lambda e, e_=e_, x=x: e.tensor_tensor(qkd[:, e_, :], PX[x][0:NS, 0:NS], ident_f[0:NS, 0:NS],
                                                                    op=ALU.mult), R=[f"PX{x}", "identf"], W=["qkd"])
                    yield
                return
            t0 = it.bi * BLK
            for e_ in range(it.nh):
                if it.gla:
                    h = it.u
                    a = nxt("A")
                    S.op("pe", lambda e, a=a, h=h: e.matmul(PA[a][:], lhsT=wlr[:, h * 128:(h + 1) * 128],
                                                          rhs=gaT[:, t0:t0 + BLK], start=True, stop=True),
                         R=["wlr", "gaT"], W=[f"PA{a}"])
                    S.op("act", lambda e, a=a, h=h: e.activation(out=g1f, in_=PA[a][:], func=AF.Exp, scale=-1.0,
                                                               bias=negb[:, h:h + 1]), R=[f"PA{a}", "negb"], W=["g1"])
                    S.op("act", lambda e: e.activation(out=g1f, in_=g1f, func=AF.Ln, scale=1.0, bias=one_t[:]),
                         R=["g1", "one_t"], W=["g1"])
                    sE = -1.0 / 16.0
                else:
                    hd = hd_of(it, e_)
                    S.op("act", lambda e, e_=e_, hd=hd: e.activation(out=g1f, in_=th[e_][:], func=AF.Ln, scale=c1[:, hd:hd + 1],
                                                                    bias=c2[:, hd:hd + 1]), R=[f"th_{e_}", "c1", "c2"], W=["g1"])
                    S.op("dve", lambda e, e_=e_, hd=hd: e.tensor_scalar(th[e_][:], th[e_][:], nc1[:, hd:hd + 1], c1[:, hd:hd + 1],
                                                                       op0=ALU.mult, op1=ALU.add),
                         R=[f"th_{e_}", "c1", "nc1", "g1"], W=[f"th_{e_}"])
                    sE = 1.0
                yield
                S.op("dve", lambda e: e.tensor_tensor_scan(g1f, mskf, g1f, 0.0, op0=ALU.mult, op1=ALU.add),
                     R=["g1", "msk"], W=["g1"])
                S.op("act", lambda e, sE=sE: e.activation(out=Ebf, in_=g1f, func=AF.Exp, scale=sE), R=["g1"], W=["Eb"])
                S.op("act", lambda e, sE=sE: e.activation(out=g1f, in_=g1f, func=AF.Exp, scale=-sE), R=["g1"], W=["g1"])
                yield
                if it.gla:
                    a = proj_fm(slot, 128, t0, BLK)
                    S.op("dve", lambda e, a=a: e.tensor_tensor(keTf, PA[a][:], g1f, op=ALU.mult),
                         R=[f"PA{a}", "g1"], W=["keT"])
                    a = proj_fm(slot, 0, t0, BLK)
                    S.op("dve", lambda e, a=a, e_=e_: e.scalar_tensor_tensor(
                        qeT[p][e_][:], PA[a][:], 128.0 ** -0.5, Ebf, op0=ALU.mult, op1=ALU.mult),
                         R=[f"PA{a}", "Eb"], W=[f"qeT_{p}{e_}"])
                else:
                    S.op("dve", lambda e, e_=e_: e.tensor_tensor(keTf, th[e_][:], g1f, op=ALU.mult),
                         R=[f"th_{e_}", "g1"], W=["keT"])
                    S.op("pool", lambda e, e_=e_: e.tensor_tensor(qeT[p][e_][:], sq[e_][:], Ebf, op=ALU.mult),
                         R=[f"sq_{e_}", "Eb"], W=[f"qeT_{p}{e_}"])
                yield
                S.op("pool", lambda e: e.tensor_tensor(kdT[:], keT[:], Eb[:, :, 127:128].broadcast_to([128, 4, 128]), op=ALU.mult),
                     R=["keT", "Eb"], W=["kdT"])
                S.op("pool", lambda e, e_=e_: e.tensor_copy(EbL[p][e_][:], Eb[:, :, 127]), R=["Eb"], W=[f"EbL_{p}{e_}"])
                a = nxt("A")
                def f(e, e_=e_, a=a):
                    for cc in range(4):
                        r = e.matmul(PA[a][:, cc * 128:(cc + 1) * 128], lhsT=keT[:, cc, :],
                                     rhs=qeT[p][e_][:, cc * 128:(cc + 1) * 128], start=True, stop=True)
                    return r
                S.op("pe", f, R=["keT", f"qeT_{p}{e_}"], W=[f"PA{a}"])
                S.op("dve", lambda e, e_=e_, a=a: e.tensor_tensor(ATb[p][e_][:].rearrange("p c t -> p (c t)"), PA[a][:],
                                                                 U4[:].rearrange("p c t -> p (c t)"), op=ALU.mult),
                     R=[f"PA{a}", "U4"], W=[f"ATb_{p}{e_}"])
                yield
                x = nxt("X")
                def f(e, x=x):
                    for cc in range(4):
                        r = e.matmul(PX[x][:, cc * 128:(cc + 1) * 128], lhsT=kdT[:, cc, :], rhs=ident_b[:], start=True, stop=True)
                    return r
                S.op("pe", f, R=["kdT", "identb"], W=[f"PX{x}"])
                S.copy(kd[p][e_][:], PX[x][:], R=[f"PX{x}"], W=[f"kd_{p}{e_}"])
                yield

        def stage2(it):
            slot, p, nh, DV, q3 = it.slot, it.p, it.nh, it.DV, it.q3
            if it.bi == "s":
                for n in range(NS):
                    sl = n % NSL
                    bsl = n % 2
                    S.copy(S0bf[bsl][:], S0b[sl][:, 0, :], R=[f"S0b{sl}"], W=[f"S0bf{bsl}"])
                    for e_ in range(nh):
                        vsl = vsl_of(it, e_)
                        d = e_
                        S.op("pe", lambda e, e_=e_, n=n, bsl=bsl, vsl=vsl, d=d: e.matmul(
                            PD[d][0:NS, 0:DV], lhsT=Qsel[:, e_, n, :], rhs=S0bf[bsl][:, vsl], start=(n == 0), stop=False),
                             R=["Qsel", f"S0bf{bsl}"], W=[f"PD{d}"])
                        kr = nxt("K")
                        S.op("dve", lambda e, e_=e_, n=n, kr=kr: e.tensor_scalar(
                            Ks[kr][:], ktok[:, e_, :], ident_f[0:NS, n:n + 1], None, op0=ALU.mult),
                             R=["ktok", "identf"], W=[f"Ks{kr}"])
                        x = nxt("X")
                        S.op("pe", lambda e, kr=kr, vsl=vsl, x=x: e.matmul(
                            PX[x][:, 0:DV], lhsT=Ks[kr][:], rhs=s_v[:, vsl], start=True, stop=True),
                             R=[f"Ks{kr}", "s_v"], W=[f"PX{x}"])
                        S.op("dve", lambda e, e_=e_, n=n, sl=sl, vsl=vsl, x=x: e.scalar_tensor_tensor(
                            S0b[sl][:, 0, vsl], S0b[sl][:, 0, vsl], s_e[:, e_, n:n + 1], PX[x][:, 0:DV],
                            op0=ALU.mult, op1=ALU.add), R=[f"PX{x}", f"S0b{sl}", "s_e"], W=[f"S0b{sl}"])
                    if it.gla:
                        S.dma("pool", gs_d[n:n + 1, it.u].rearrange("n k v -> k n v"), S0b[sl][:], f"st_Sn{sl}", R=[f"S0b{sl}"])
                    else:
                        j = it.u - 4
                        for hh in range(2):
                            S.dma("pool", hs_d[n:n + 1, 2 * j + hh].rearrange("n k v -> k n v"),
                                  S0b[sl][:, :, hh * 128:(hh + 1) * 128], f"st_Sn{sl}", R=[f"S0b{sl}"])
                    if n + NSL < NS:
                        load_S0(it, n + NSL)
                    yield
                for e_ in range(nh):
                    vsl = vsl_of(it, e_)
                    d = e_
                    S.op("pe", lambda e, e_=e_, vsl=vsl, d=d: e.matmul(PD[d][0:NS, 0:DV], lhsT=qkd[:, e_, :], rhs=s_v[:, vsl],
                                                                     start=False, stop=True),
                         R=["qkd", "s_v"], W=[f"PD{d}"])
                    col = e_
                    S.op("act", lambda e, d=d, col=col: e.activation(out=junk[0:NS, 0:DV], in_=PD[d][0:NS, 0:DV], func=AF.Square,
                                                                     accum_out=ssq[0:NS, col:col + 1]),
                         R=[f"PD{d}"], W=["junk", f"ssq{col}"])
                    rstd_from(ssq[0:NS, col:col + 1], rstd[0:NS, col:col + 1], NS, DV, [f"ssq{col}", "eps_t"], [f"rstd{col}"])
                    S.op("dve", lambda e, d=d, col=col, vsl=vsl: e.scalar_tensor_tensor(
                        s_on[:, vsl], PD[d][0:NS, 0:DV], rstd[0:NS, col:col + 1], s_u[:, vsl], op0=ALU.mult, op1=ALU.mult),
                         R=[f"PD{d}", f"rstd{col}", "s_u"], W=["s_on"])
                x = nxt("X")
                def f(e, x=x):
                    for jj in range(2):
                        r = e.matmul(PX[x][:, jj * NS:(jj + 1) * NS], lhsT=s_on[:, jj * 128:(jj + 1) * 128],
                                     rhs=ident_b[0:NS, 0:NS], start=True, stop=True)
                    return r
                S.op("pe", f, R=["s_on", "identb"], W=[f"PX{x}"])
                vc0 = it.vc0
                S.op("act", lambda e, x=x, vc0=vc0: e.activation(
                    out=onT[:, vc0:vc0 + 2, T:T + NS], in_=PX[x][:, 0:2 * NS].rearrange("p (j t) -> p j t", t=NS),
                    func=AF.Copy), R=[f"PX{x}"], W=[f"onT{vc0}_s"])
                yield
                return
            bi = it.bi
            t0 = bi * BLK
            pb = bi % 2
            for e_ in range(nh):
                vsl = vsl_of(it, e_)
                sks = ["Sst0", "Sst1"] if it.gla else [f"Sst{e_}"]
                sbk = (lambda q: [f"Sbf{q}_0", f"Sbf{q}_1"]) if it.gla else (lambda q, e_=e_: [f"Sbf{q}_{e_}"])
                Sv = Sst[:, vsl]
                cpb = 512 // DV
                nbk = 4 // cpb
                xb = [nxt("X") for _ in range(nbk)]
                for bk in range(nbk):
                    def f(e, e_=e_, bk=bk, vsl=vsl):
                        for j in range(cpb):
                            cc = bk * cpb + j
                            r = e.matmul(PX[xb[bk]][:, j * DV:(j + 1) * DV], lhsT=kd[p][e_][:, cc * 128:(cc + 1) * 128],
                                         rhs=vbf[q3][:, cc, vsl], start=True, stop=True)
                        return r
                    S.op("pe", f, R=[f"kd_{p}{e_}"] + [f"vbf{q3}_{bk * cpb + j}" for j in range(cpb)], W=[f"PX{xb[bk]}"])
                for cc in range(4):
                    gc = bi * 4 + cc
                    bk, j = cc // cpb, cc % cpb
                    usl = PX[xb[bk]][:, j * DV:(j + 1) * DV]
                    if gc == 0:
                        S.op("dve", lambda e, usl=usl: e.tensor_copy(Sv, usl), R=[f"PX{xb[bk]}"], W=sks)
                    else:
                        S.op("dve", lambda e, usl=usl, cc=cc: e.scalar_tensor_tensor(
                            Sv, Sv, EbL[p][e_][:, cc:cc + 1], usl, op0=ALU.mult, op1=ALU.add),
                             R=[f"PX{xb[bk]}", f"EbL_{p}{e_}"] + sks, W=sks)
                    S.copy(Sbf[pb][:, cc, vsl], Sv, R=sks, W=sbk(pb))
                yield
                db = [nxt("D") for _ in range(nbk)]
                for bk in range(nbk):
                    def f(e, e_=e_, bk=bk, vsl=vsl):
                        for j in range(cpb):
                            cc = bk * cpb + j
                            gc = bi * 4 + cc
                            osl = PD[db[bk]][:, j * DV:(j + 1) * DV]
                            r = e.matmul(osl, lhsT=ATb[p][e_][:, cc, :], rhs=vbf[q3][:, cc, vsl], start=True, stop=(gc == 0))
                            if gc > 0:
                                prev = Sbf[1 - pb][:, 3, vsl] if cc == 0 else Sbf[pb][:, cc - 1, vsl]
                                r = e.matmul(osl, lhsT=qeT[p][e_][:, cc * 128:(cc + 1) * 128], rhs=prev, start=False, stop=True)
                        return r
                    S.op("pe", f, R=[f"ATb_{p}{e_}", f"qeT_{p}{e_}"] + sbk(pb) + sbk(1 - pb) +
                         [f"vbf{q3}_{bk * cpb + j}" for j in range(cpb)], W=[f"PD{db[bk]}"])
                yield
                for cc in range(4):
                    bk, j = cc // cpb, cc % cpb
                    col = e_ * 4 + cc
                    S.op("act", lambda e, bk=bk, j=j, col=col: e.activation(
                        out=junk[:, 0:DV], in_=PD[db[bk]][:, j * DV:(j + 1) * DV], func=AF.Square, accum_out=ssq[:, col:col + 1]),
                         R=[f"PD{db[bk]}"], W=["junk", f"ssq{e_}"])
                rstd_from(ssq[:, e_ * 4:e_ * 4 + 4], rstd[:, e_ * 4:e_ * 4 + 4], 128, DV, [f"ssq{e_}", "eps_t"], [f"rstd{e_}"])
                for cc in range(4):
                    bk, j = cc // cpb, cc % cpb
                    col = e_ * 4 + cc
                    S.op("dve", lambda e, bk=bk, j=j, col=col, cc=cc: e.scalar_tensor_tensor(
                        onb[e_][:, cc, 0:DV], PD[db[bk]][:, j * DV:(j + 1) * DV], rstd[:, col:col + 1], ug[q3][:, cc, vsl],
                        op0=ALU.mult, op1=ALU.mult),
                         R=[f"PD{db[bk]}", f"rstd{e_}", f"ug{q3}_{cc}"], W=[f"onb{e_}"])
                yield
                nv = DV // 128
                for jj in range(nv):
                    x2 = nxt("X")
                    def f(e, e_=e_, jj=jj, x2=x2):
                        for cc in range(4):
                            r = e.matmul(PX[x2][:, cc * 128:(cc + 1) * 128], lhsT=onb[e_][:, cc, jj * 128:(jj + 1) * 128],
                                         rhs=ident_b[:], start=True, stop=True)
                        return r
                    S.op("pe", f, R=[f"onb{e_}", "identb"], W=[f"PX{x2}"])
                    vc = it.vc0 + (jj if it.gla else e_)
                    S.copy(onT[:, vc, t0:t0 + BLK], PX[x2][:], R=[f"PX{x2}"], W=[f"onT{vc}_{bi}"])
                yield
            if bi == NBLK - 1:
                for e_ in range(nh):
                    if it.gla:
                        S.dma("sp", gp_d[it.u], Sst[:], "st_gp", R=["Sst0", "Sst1"])
                    else:
                        S.dma("sp", hp_d[hd_of(it, e_)], Sst[:, e_ * 128:(e_ + 1) * 128], f"st_hp{e_}", R=[f"Sst{e_}"])

        def run_all(g):
            for _ in g:
                pass

        def interleave(gens, beta_idx=None):
            alive = [True] * len(gens)
            paused = [False] * len(gens)
            while any(alive):
                for k in range(len(gens)):
                    if not alive[k]:
                        continue
                    if paused[k]:
                        if beta_idx is not None and alive[beta_idx]:
                            continue
                        paused[k] = False
                    try:
                        r = next(gens[k])
                        if r == "WAIT_BETA":
                            paused[k] = True
                    except StopIteration:
                        alive[k] = False

        nit = len(items)
        if "lsched" not in cache:
            cache["lsched"] = ListScheduler()
            cache["orders"] = {}
        lsched = cache["lsched"]

        def flush(tag):
            recs = S.capture
            S.capture = None
            if tag not in cache["orders"]:
                cache["orders"][tag] = lsched.schedule(recs)
            order_, choice_ = cache["orders"][tag]
            for i in order_:
                S.emit(recs[i], choice_.get(i))

        WIN = 8
        S.capture = []
        run_all(alpha1(items[0]))
        alpha2(items[0])
        run_all(beta(items[0]))
        run_all(alpha1(items[1]))
        for k, it in enumerate(items):
            if k + 1 < nit:
                alpha2(items[k + 1])
            gens = [stage2(it)]
            bidx = None
            if k + 1 < nit:
                gens.append(beta(items[k + 1]))
                bidx = 1
            if k + 2 < nit:
                ga1 = alpha1(items[k + 2])
                if items[k + 2].bi != "s":
                    for r in ga1:
                        if r == "TILES_DONE":
                            break
                gens.append(ga1)
            interleave(gens, bidx)
            if k + 1 < nit:
                nx = items[k + 1]
                if nx.bi == "s" and nx.ui + 2 < len(order):
                    load_unit_weights(order[nx.ui + 2], nx.slot)
            if k % WIN == WIN - 1 or k == nit - 1:
                flush(f"step{k}")
                S.capture = []
        S.capture = None

        ONT_KEYS = [k for k in list(S.last_w.keys()) if k.startswith("onT")]
        sb.cur = region_mark + 64
        ALLU = [k for k in list(S.last_w.keys()) + list(S.readers.keys())
                if not (k.startswith("onT") or k.startswith("xT") or k in ("identf", "identb", "eps_t", "one_t"))]
        ALLU = sorted(set(ALLU))
        wo = sb.alloc("wo", [128, 8, 1024], BF16)
        sb.cur = region_mark + 64
        fw = [sb.alloc(f"fw{i}", [128, 8, 1024], BF16) for i in range(4)]
        wo_r = wout_d.rearrange("(c p) n -> p c n", p=128)
        mT = sb.alloc("mT", [128, 8, TT], BF16)
        tha = sb.alloc("tha", [128, 512], BF16)
        thb = sb.alloc("thb", [128, 512], F32)
        m1 = sb.alloc("m1", [128, 512], F32)
        f1_end = sb.cur
        assert f1_end <= sb.top
        for q in ("pool", "sp", "pe", "act", "dve"):
            S._wait(q, S._deps([], ALLU))
        srcs = [win_r[:, :, MGA_OFF:MGA_OFF + 1024], win_r[:, :, MGB_OFF:MGB_OFF + 1024],
                wbg_d.rearrange("(c p) n -> p c n", p=128), wbh_d.rearrange("(c p) n -> p c n", p=128)]
        for dc in range(8):
            for i in range(4):
                S.dma("pool", fw[i][:, :, dc * 128:(dc + 1) * 128], srcs[i][:, :, dc * 128:(dc + 1) * 128],
                      f"ld_fw{i}_{dc}", W=[f"fw{i}_{dc}"])

        S.capture = []
        for dc in range(8):
            for bi in range(NBLK + 1):
                t0 = bi * BLK
                n = BLK if bi < NBLK else NS
                onk = [k for k in ONT_KEYS if k.endswith(f"_{bi}" if bi < NBLK else "_s")]
                def mm(bank, wi, src_is_x, base):
                    def f(e):
                        for c in range(8):
                            rhs = xT[:, c, t0:t0 + n] if src_is_x else onT[:, base + c, t0:t0 + n]
                            r = e.matmul(bank[:, 0:n], lhsT=fw[wi][:, c, dc * 128:(dc + 1) * 128], rhs=rhs,
                                         start=(c == 0), stop=(c == 7))
                        return r
                    return f
                a = nxt("A")
                S.op("pe", mm(PA[a], 0, True, 0), R=[f"fw0_{dc}"] + XK(t0), W=[f"PA{a}"])
                S.op("act", lambda e, a=a: e.activation(out=tha[:, 0:n], in_=PA[a][:, 0:n], func=AF.Tanh, scale=0.5),
                     R=[f"PA{a}"], W=["tha"])
                a = nxt("A")
                S.op("pe", mm(PA[a], 1, True, 0), R=[f"fw1_{dc}"] + XK(t0), W=[f"PA{a}"])
                S.op("act", lambda e, a=a: e.activation(out=thb[:, 0:n], in_=PA[a][:, 0:n], func=AF.Tanh, scale=0.5),
                     R=[f"PA{a}"], W=["thb"])
                b = nxt("B")
                S.op("pe", mm(PB[b], 2, False, 0), R=[f"fw2_{dc}"] + onk, W=[f"PB{b}"])
                S.op("dve", lambda e, b=b: e.scalar_tensor_tensor(m1[:, 0:n], tha[:, 0:n], 1.0, PB[b][:, 0:n],
                                                                  op0=ALU.add, op1=ALU.mult), R=[f"PB{b}", "tha"], W=["m1"])
                b = nxt("B")
                S.op("pe", mm(PB[b], 3, False, 8), R=[f"fw3_{dc}"] + onk, W=[f"PB{b}"])
                S.op("dve", lambda e, b=b: e.scalar_tensor_tensor(thb[:, 0:n], thb[:, 0:n], 1.0, PB[b][:, 0:n],
                                                                  op0=ALU.add, op1=ALU.mult), R=[f"PB{b}", "thb"], W=["thb"])
                S.op("pool", lambda e, dc=dc: e.tensor_tensor(mT[:, dc, t0:t0 + n], m1[:, 0:n], thb[:, 0:n], op=ALU.add),
                     R=["m1", "thb"], W=[f"mT_{bi}"])
            S.dma("pool", wo[:, :, dc * 128:(dc + 1) * 128], wo_r[:, :, dc * 128:(dc + 1) * 128], f"ld_wo_{dc}", W=[f"fw0_{dc}"])

        flush("F1")
        for q in ("pool", "sp", "pe", "act", "dve"):
            S._wait(q, S._deps([], [f"fw{i}_{dc}" for i in (1, 2) for dc in range(8)]))
        sb.cur = region_mark + 64 + 16384
        lng = sb.alloc("lng", [128, D], F32)
        lnb = sb.alloc("lnb", [128, D], F32)
        xt = [sb.alloc(f"xt{i}", [128, D], F32) for i in range(4)]
        stt = sb.alloc("stt", [128, 12], F32)
        junk2 = sb.alloc("junk2", [128, D], BF16)
        mv = sb.alloc("mv", [128, 2], F32)
        rs2 = sb.alloc("rs2", [128, 2], F32)
        eps2 = sb.alloc("eps2", [128, 1], F32)
        assert sb.cur <= region_mark + 64 + 3 * 16384
        S.capture = []
        S.dma("sp", lng[:], lng_d, "ld_ln", W=["lng"])
        S.dma("sp", lnb[:], lnb_d, "ld_lnb", W=["lnb"])
        S.op("dve", lambda e: e.memset(eps2[:], EPS / (ALPHA * ALPHA)), W=["eps2"])
        CY = 0.5 / ALPHA
        ntile = T // 128 + 1
        for ti in range(ntile):
            r0 = ti * 128
            m = 128 if ti < T // 128 else NS
            sl = ti % 4
            bi = min(ti // 4, NBLK)
            if ti == 0:
                for tj in range(min(2, ntile)):
                    mj = 128 if tj < T // 128 else NS
                    S.dma("sp", xt[tj % 4][0:mj, :], xtok_d[tj * 128:tj * 128 + mj, :], f"ld_xt{tj % 4}", W=[f"xt{tj % 4}"])
            if ti + 2 < ntile:
                tj = ti + 2
                mj = 128 if tj < T // 128 else NS
                S.dma("sp", xt[tj % 4][0:mj, :], xtok_d[tj * 128:tj * 128 + mj, :], f"ld_xt{tj % 4}", W=[f"xt{tj % 4}"])
            for hh in range(2):
                bq = (2 * ti + hh) % 4
                bank, bkey = ((PA, "PA") if bq < 2 else (PB, "PB"))
                bank = bank[bq % 2]
                bkey = f"{bkey}{bq % 2}"
                def f(e, bank=bank, hh=hh, r0=r0, m=m):
                    for c in range(8):
                        r = e.matmul(bank[0:m, :], lhsT=mT[:, c, r0:r0 + m], rhs=wo[:, c, hh * 512:(hh + 1) * 512],
                                     start=(c == 0), stop=(c == 7))
                    return r
                S.op("pe", f, R=[f"mT_{bi}"] + [f"fw0_{dc}" for dc in range(4 * hh, 4 * hh + 4)], W=[bkey])
                S.op("dve", lambda e, bank=bank, hh=hh, sl=sl, m=m: e.scalar_tensor_tensor(
                    xt[sl][0:m, hh * 512:(hh + 1) * 512], bank[0:m, :], CY, xt[sl][0:m, hh * 512:(hh + 1) * 512],
                    op0=ALU.mult, op1=ALU.add), R=[bkey, f"xt{sl}"], W=[f"xt{sl}"])
            S.op("act", lambda e, sl=sl, m=m: e.activation(out=junk2[0:m, :], in_=xt[sl][0:m, :], func=AF.Copy,
                                                           accum_out=stt[0:m, 0:1]), R=[f"xt{sl}"], W=["junk2", "stt0"])
            S.op("act", lambda e, sl=sl, m=m: e.activation(out=junk2[0:m, :], in_=xt[sl][0:m, :], func=AF.Square,
                                                           accum_out=stt[0:m, 1:2]), R=[f"xt{sl}"], W=["junk2", "stt1"])
            S.op("dve", lambda e, m=m: e.tensor_scalar(mv[0:m, 0:1], stt[0:m, 0:1], 1.0 / D, None, op0=ALU.mult),
                 R=["stt0"], W=["mv"])
            S.op("dve", lambda e, m=m: e.tensor_tensor(mv[0:m, 1:2], mv[0:m, 0:1], mv[0:m, 0:1], op=ALU.mult),
                 R=["mv"], W=["mv"])
            S.op("dve", lambda e, m=m: e.scalar_tensor_tensor(mv[0:m, 1:2], stt[0:m, 1:2], 1.0 / D, mv[0:m, 1:2],
                                                              op0=ALU.mult, op1=ALU.subtract), R=["stt1", "mv"], W=["mv"])
            S.op("act", lambda e, m=m: e.activation(out=rs2[0:m, 0:1], in_=mv[0:m, 1:2], func=AF.Ln, scale=1.0, bias=eps2[0:m, :]),
                 R=["mv", "eps2"], W=["rs2"])
            S.op("act", lambda e, m=m: e.activation(out=rs2[0:m, 0:1], in_=rs2[0:m, 0:1], func=AF.Exp, scale=-0.5),
                 R=["rs2"], W=["rs2"])
            S.op("dve", lambda e, m=m: e.scalar_tensor_tensor(rs2[0:m, 1:2], mv[0:m, 0:1], -1.0, rs2[0:m, 0:1],
                                                              op0=ALU.mult, op1=ALU.mult), R=["rs2", "mv"], W=["rs2"])
            S.op("act", lambda e, sl=sl, m=m: e.activation(out=xt[sl][0:m, :], in_=xt[sl][0:m, :], func=AF.Identity,
                                                           scale=rs2[0:m, 0:1], bias=rs2[0:m, 1:2]),
                 R=[f"xt{sl}", "rs2"], W=[f"xt{sl}"])
            S.op("dve", lambda e, sl=sl, m=m: e.tensor_tensor(xt[sl][0:m, :], xt[sl][0:m, :], lng[0:m, :], op=ALU.mult),
                 R=[f"xt{sl}", "lng"], W=[f"xt{sl}"])
            S.op("pool", lambda e, sl=sl, m=m: e.tensor_tensor(xt[sl][0:m, :], xt[sl][0:m, :], lnb[0:m, :], op=ALU.add),
                 R=[f"xt{sl}", "lnb"], W=[f"xt{sl}"])
            S.dma("sp", y_d[r0:r0 + m, :], xt[sl][0:m, :], f"st_y{sl}", R=[f"xt{sl}"])

        flush("F2")
        S.final_wait("sp", [k for k in S.dcnt if k.startswith("st_")])

    with nc.Block() as block:
        @block.sync
        def _(e):
            program("sp", e)

        @block.gpsimd
        def _(e):
            program("pool", e)

        @block.tensor
        def _(e):
            program("pe", e)

        @block.scalar
        def _(e):
            program("act", e)

        @block.vector
        def _(e):
            program("dve", e)
    return nc


_NC_CACHE = {}


def kernel(x_prompt, x_sample, state_gla, state_hgrn, w_in, w_gate_lr, b_gate_lr, gla_norm_g, w_br_gla,
           hgrn_lb_param, hgrn_norm_g, w_br_hgrn, w_out, ln_g, ln_b):
    f = lambda a: np.ascontiguousarray(np.asarray(a, dtype=np.float32))
    x_prompt, x_sample = f(x_prompt), f(x_sample)
    state_gla, state_hgrn = f(state_gla), f(state_hgrn)
    if "nc" not in _NC_CACHE:
        _NC_CACHE["nc"] = build_nc()
    nc = _NC_CACHE["nc"]
    shared = {
        "w_in": f(w_in)[0],
        "wlr": f(w_gate_lr)[0],
        "bgl": f(f(b_gate_lr)[0].reshape(4, 128).T),
        "glag": f(np.broadcast_to(f(gla_norm_g)[0].reshape(1, 1024), (128, 1024))),
        "lbp": f(f(hgrn_lb_param).reshape(2, 8, 128).transpose(2, 0, 1).reshape(128, 16)),
        "hgg": f(np.broadcast_to(f(hgrn_norm_g)[0].reshape(1, 1024), (128, 1024))),
        "wbg": f(w_br_gla)[0],
        "wbh": f(w_br_hgrn)[0],
        "wout": f(w_out)[0],
        "lng": f(np.broadcast_to(f(ln_g)[0].reshape(1, D), (128, D))),
        "lnb": f(np.broadcast_to(f(ln_b)[0].reshape(1, D), (128, D))),
    }
    in_maps = []
    for b in range(NCORES):
        xs = x_sample[b * NS:(b + 1) * NS, 0, :]
        xtok = np.concatenate([x_prompt[b], xs], axis=0)
        m = dict(shared)
        m["xtok"] = f(xtok)
        m["xT"] = f(xtok.T)
        m["sg"] = f(state_gla[0, b * NS:(b + 1) * NS])
        m["sh"] = f(state_hgrn[0, b * NS:(b + 1) * NS])
        in_maps.append(m)
    res = run_bass_kernel_spmd(nc, in_maps, core_ids=list(range(NCORES)))
    rs = res.results
    y_prompt = np.stack([r["y"][:T] for r in rs], axis=0)
    y_sample = np.concatenate([r["y"][T:TT] for r in rs], axis=0)[:, None, :]
    gp = np.stack([r["gp"] for r in rs], axis=0)[None]
    hp = np.stack([r["hp"] for r in rs], axis=0)[None]
    gs = np.concatenate([r["gs"] for r in rs], axis=0)[None]
    hs = np.concatenate([r["hs"] for r in rs], axis=0)[None]
    return (y_prompt.astype(np.float32), y_sample.astype(np.float32), gp.astype(np.float32),
            hp.astype(np.float32), gs.astype(np.float32), hs.astype(np.float32))
```

```python
import numpy as np
import concourse.bass as bass
import concourse.mybir as mybir
from concourse.bass_utils import run_bass_kernel_spmd

F32 = mybir.dt.float32
BF16 = mybir.dt.bfloat16
AF = mybir.ActivationFunctionType
ALU = mybir.AluOpType

NCORES = 8
D = 1024
T = 2048
NS = 16
TT = T + NS
NBLK = 4
BLK = 512
IN_DIM = 9232
GA_OFF = 3072
HQ_OFF = 3088
HF_OFF = HQ_OFF + 1024
HI_OFF = HQ_OFF + 2048
HR_OFF = HQ_OFF + 3072
MGA_OFF = HQ_OFF + 4096
MGB_OFF = MGA_OFF + 1024
ALPHA = 2.0 ** 0.25
EPS = 1e-5


class Sched:
    def __init__(self, nc, cache, me, eobj):
        self.nc = nc
        self.cache = cache
        self.me = me
        self.e = eobj
        self.cnt = {k: 0 for k in ("pe", "act", "dve", "pool")}
        self.waited = {}
        self.last_w = {}
        self.readers = {}
        self.dcnt = {}
        self.capture = None

    def emit(self, rec, eng=None):
        if rec[0] == "op":
            self.op(rec[1], rec[2], rec[3], rec[4])
        elif rec[0] == "flex":
            out, in_ = rec[2]
            if eng == "act":
                self.op("act", lambda e: e.activation(out=out, in_=in_, func=AF.Copy), rec[3], rec[4])
            else:
                self.op("dve", lambda e: e.tensor_copy(out, in_), rec[3], rec[4])
        else:
            self.dma(rec[1], rec[2][0], rec[2][1], rec[2][2], rec[3], rec[4])

    def copy(self, out, in_, R=(), W=()):
        W = list(W) + [k for k in R if len(k) == 3 and k[0] == "P" and k[1] in "ABDX"]
        if self.capture is None:
            self.op("dve", lambda e: e.tensor_copy(out, in_), R, W)
            return
        size = 1
        for v in out.shape[1:]:
            size *= v
        self.capture.append(("flex", "dve", (out, in_), list(R), list(W), 230 + 0.83 * size, 120 + 1.12 * size))

    def _semh(self, s):
        k = "sem_" + s
        if k not in self.cache:
            self.cache[k] = self.nc.alloc_semaphore("s_" + s)
        if s not in self.cnt and s not in self.dcnt:
            self.dcnt[s] = 0
        return self.cache[k]

    def _deps(self, R, W):
        best = {}
        def add(sv):
            s, v = sv
            if v > best.get(s, 0):
                best[s] = v
        for b in R:
            if b in self.last_w:
                add(self.last_w[b])
        for b in W:
            if b in self.last_w:
                add(self.last_w[b])
            for sv in self.readers.get(b, {}).items():
                add(sv)
        return best

    def _wait(self, eng, best):
        for s, v in best.items():
            if s == "pe" and eng == "pe":
                continue
            if self.waited.get((eng, s), 0) >= v:
                continue
            h = self._semh(s)
            if eng == self.me:
                self.e.wait_ge(h, v)
            self.waited[(eng, s)] = v

    def _book(self, me, R, W):
        s, v = me
        for b in R:
            d = self.readers.setdefault(b, {})
            if v > d.get(s, 0):
                d[s] = v
        for b in W:
            self.last_w[b] = me
            self.readers[b] = {}

    def op(self, eng, fn, R=(), W=()):
        W = list(W) + [k for k in R if len(k) == 3 and k[0] == "P" and k[1] in "ABDX"]
        if self.capture is not None:
            fe = _FakeEng(eng)
            fn(fe)
            calls = fe.calls

            def replay(e, calls=calls):
                r = None
                for name, args, kw in calls:
                    r = getattr(e, name)(*args, **kw)
                return r
            self.capture.append(("op", eng, replay, list(R), list(W), fe.dur, fe.tset))
            return
        self._wait(eng, self._deps(R, W))
        self.cnt[eng] += 1
        h = self._semh(eng)
        if eng == self.me:
            fn(self.e).then_inc(h, 1)
        self._book((eng, self.cnt[eng]), R, W)

    def dma(self, q, out, in_, sem, R=(), W=()):
        if self.capture is not None:
            self.capture.append(("dma", q, (out, in_, sem), list(R), list(W)))
            return
        self._wait(q, self._deps(R, W))
        h = self._semh(sem)
        self.dcnt[sem] += 16
        if q == self.me:
            self.e.dma_start(out=out, in_=in_).then_inc(h, 16)
        self._book((sem, self.dcnt[sem]), R, W)

    def final_wait(self, eng, sems):
        for s in sems:
            if self.dcnt.get(s, 0) > 0 and eng == self.me:
                self.e.wait_ge(self._semh(s), self.dcnt[s])


class _FakeIns:
    def then_inc(self, *a, **k):
        return self


class _FakeEng:
    def __init__(self, kind):
        self.kind = kind
        self.dur = 0.0
        self.tset = None
        self.calls = []

    def __getattr__(self, name):
        def call(*args, **kw):
            self.calls.append((name, args, kw))
            out = kw.get("out", args[0] if args else None)
            size = 1
            try:
                for v in out.shape[1:]:
                    size *= v
            except Exception:
                size = 256
            if name == "matmul":
                self.dur += 70 + 0.62 * size
            elif self.kind == "act":
                self.dur += 230 + 0.83 * size
                f = kw.get("func")
                if f in (AF.Silu, AF.Tanh):
                    self.tset = 18
                elif f in (AF.Exp, AF.Ln):
                    self.tset = 6
            elif self.kind == "dve":
                self.dur += (120 + 1.12 * size) * (2.0 if name == "tensor_tensor_scan" else 1.0)
            elif self.kind == "pool":
                self.dur += 320 + 1.6 * size
            else:
                self.dur += 100
            return _FakeIns()
        return call


class ListScheduler:
    def __init__(self):
        self.free = {k: 0.0 for k in ("pe", "act", "dve", "pool", "sp")}
        self.wfin = {}
        self.rfin = {}
        self.tset = None

    def schedule(self, recs):
        n = len(recs)
        dur = [0.0] * n
        lat = [0.0] * n
        tset = [None] * n
        for i, r in enumerate(recs):
            if r[0] == "op":
                dur[i] = r[5] + 60.0
                lat[i] = dur[i]
                tset[i] = r[6]
            elif r[0] == "flex":
                dur[i] = min(r[5], r[6]) + 60.0
                lat[i] = dur[i]
            else:
                dur[i] = 1000.0 if r[1] == "pool" else 150.0
                lat[i] = dur[i] + 2600.0
        preds = [set() for _ in range(n)]
        lw, rd = {}, {}
        for i, r in enumerate(recs):
            R, W = r[3], r[4]
            for k in R:
                if k in lw:
                    preds[i].add(lw[k])
            for k in W:
                if k in lw:
                    preds[i].add(lw[k])
                for j in rd.get(k, ()):
                    preds[i].add(j)
            for k in R:
                rd.setdefault(k, []).append(i)
            for k in W:
                lw[k] = i
                rd[k] = []
            preds[i].discard(i)
        succs = [[] for _ in range(n)]
        for i in range(n):
            for j in preds[i]:
                succs[j].append(i)
        prio = [0.0] * n
        for i in range(n - 1, -1, -1):
            m = 0.0
            for j in succs[i]:
                if prio[j] > m:
                    m = prio[j]
            prio[i] = lat[i] + m
        base = [0.0] * n
        for i, r in enumerate(recs):
            b = 0.0
            for k in r[3]:
                b = max(b, self.wfin.get(k, 0.0))
            for k in r[4]:
                b = max(b, self.wfin.get(k, 0.0), self.rfin.get(k, 0.0))
            base[i] = b
        npred = [len(p) for p in preds]
        fin = [0.0] * n
        ready = [i for i in range(n) if npred[i] == 0]
        order = []
        choice = {}
        while ready:
            best, bkey = None, None
            for i in ready:
                dep = base[i]
                for j in preds[i]:
                    if fin[j] + 80.0 > dep:
                        dep = fin[j] + 80.0
                if recs[i][0] == "flex":
                    sa = max(self.free["act"], dep)
                    sd = max(self.free["dve"], dep)
                    if sa + recs[i][5] < sd + recs[i][6]:
                        eng, st, du = "act", sa, recs[i][5] + 60.0
                    else:
                        eng, st, du = "dve", sd, recs[i][6] + 60.0
                else:
                    eng = recs[i][1]
                    st = max(self.free[eng], dep)
                    du = dur[i]
                    if eng == "act" and tset[i] is not None and self.tset is not None and tset[i] != self.tset:
                        st += 2600.0
                key = (round(st / 150.0), -prio[i], i)
                if bkey is None or key < bkey:
                    best, bkey, bst, beng, bdu = i, key, st, eng, du
            i = best
            eng = beng
            if recs[i][0] == "flex":
                choice[i] = eng
                lat[i] = bdu
            if eng == "act" and tset[i] is not None:
                self.tset = tset[i]
            self.free[eng] = bst + bdu
            fin[i] = bst + lat[i]
            order.append(i)
            ready.remove(i)
            for j in succs[i]:
                npred[j] -= 1
                if npred[j] == 0:
                    ready.append(j)
        assert len(order) == n
        for i, r in enumerate(recs):
            for k in r[3]:
                self.rfin[k] = max(self.rfin.get(k, 0.0), fin[i])
            for k in r[4]:
                self.wfin[k] = fin[i]
                self.rfin[k] = 0.0
        return order, choice


class SBAlloc:
    def __init__(self, nc, cache):
        self.nc = nc
        self.cache = cache
        self.cur = (nc.sbuf_base + 63) // 64 * 64
        self.top = nc.sbuf_top
        self.n = 0

    def alloc(self, name, shape, dtype):
        isz = 2 if dtype == BF16 else 4
        size = isz
        for s in shape[1:]:
            size *= s
        size = (size + 63) // 64 * 64
        assert self.cur + size <= self.top, (name, self.cur, size, self.top)
        self.n += 1
        k = f"sb_{name}_{self.n}"
        if k not in self.cache:
            self.cache[k] = self.nc.alloc_sbuf_tensor_at(f"{name}_{self.n}", list(shape), dtype, offset=self.cur)
        self.cur += size
        return self.cache[k]


def build_nc():
    nc = bass.Bass("TRN2", target_bir_lowering=False)
    dt_in = lambda n, s: nc.dram_tensor(n, list(s), F32, kind="ExternalInput").ap()
    dt_out = lambda n, s: nc.dram_tensor(n, list(s), F32, kind="ExternalOutput").ap()
    xT_d = dt_in("xT", (D, TT))
    xtok_d = dt_in("xtok", (TT, D))
    sg_d = dt_in("sg", (NS, 4, 128, 256))
    sh_d = dt_in("sh", (NS, 8, 128, 128))
    win_d = dt_in("w_in", (D, IN_DIM))
    wlr_d = dt_in("wlr", (16, 512))
    bgl_d = dt_in("bgl", (128, 4))
    glag_d = dt_in("glag", (128, 1024))
    lbp_d = dt_in("lbp", (128, 16))
    hgg_d = dt_in("hgg", (128, 1024))
    wbg_d = dt_in("wbg", (D, D))
    wbh_d = dt_in("wbh", (D, D))
    wout_d = dt_in("wout", (D, D))
    lng_d = dt_in("lng", (128, D))
    lnb_d = dt_in("lnb", (128, D))
    y_d = dt_out("y", (TT, D))
    gp_d = dt_out("gp", (4, 128, 256))
    hp_d = dt_out("hp", (8, 128, 128))
    gs_d = dt_out("gs", (NS, 4, 128, 256))
    hs_d = dt_out("hs", (NS, 8, 128, 128))

    PA = [nc.alloc_psum_tensor(f"PA{i}", [128, 512], F32) for i in range(2)]
    PB = [nc.alloc_psum_tensor(f"PB{i}", [128, 512], F32) for i in range(2)]
    PD = [nc.alloc_psum_tensor(f"PD{i}", [128, 512], F32) for i in range(2)]
    PX = [nc.alloc_psum_tensor(f"PX{i}", [128, 512], F32) for i in range(2)]
    cache = {}

    def program(me, eobj):
        S = Sched(nc, cache, me, eobj)
        sb = SBAlloc(nc, cache)
        win_r = win_d.rearrange("(c p) n -> p c n", p=128)

        xT = sb.alloc("xT", [128, 8, TT], BF16)
        onT = sb.alloc("onT", [128, 16, TT], BF16)
        ident_f = sb.alloc("identf", [128, 128], F32)
        ident_b = sb.alloc("identb", [128, 128], BF16)
        U4 = sb.alloc("U4", [128, 4, 128], F32)
        msk = sb.alloc("msk", [128, 4, 128], F32)
        idrow = sb.alloc("idrow", [128, 16, 16], F32)
        negb = sb.alloc("negb", [128, 4], F32)
        lbp = sb.alloc("lbp", [128, 16], F32)
        c1 = sb.alloc("c1", [128, 8], F32)
        nc1 = sb.alloc("nc1", [128, 8], F32)
        c2 = sb.alloc("c2", [128, 8], F32)
        lnc1 = sb.alloc("lnc1", [128, 8], F32)
        region_mark = sb.cur
        wga = sb.alloc("wga", [128, 8, 16], BF16)
        wlr = sb.alloc("wlr", [16, 512], BF16)
        gaT = sb.alloc("gaT", [16, TT], BF16)

        rot = {"A": 0, "B": 0, "D": 0, "X": 0, "K": 0, "O": 0}

        def nxt(k):
            rot[k] ^= 1
            return rot[k]

        S.op("pool", lambda e: e.memset(ident_f[:], 1.0), W=["identf"])
        S.op("pool", lambda e: e.affine_select(out=ident_f[:], in_=ident_f[:], pattern=[[1, 128]],
                                               compare_op=ALU.is_equal, fill=0.0, base=0,
                                               channel_multiplier=-1), R=["identf"], W=["identf"])
        S.op("pool", lambda e: e.memset(U4[:], 1.0), W=["U4"])
        S.op("pool", lambda e: e.affine_select(out=U4[:], in_=U4[:], pattern=[[0, 4], [1, 128]],
                                               compare_op=ALU.is_ge, fill=0.0, base=0,
                                               channel_multiplier=-1), R=["U4"], W=["U4"])
        S.op("pool", lambda e: e.memset(msk[:], 1.0), W=["msk"])
        S.op("pool", lambda e: e.memset(msk[:, :, 0:1], 0.0), R=["msk"], W=["msk"])
        S.op("pool", lambda e: e.memset(idrow[:], 1.0), W=["idrow"])
        S.op("pool", lambda e: e.affine_select(out=idrow[:], in_=idrow[:], pattern=[[1, 16], [-1, 16]],
                                               compare_op=ALU.is_equal, fill=0.0, base=0,
                                               channel_multiplier=0), R=["idrow"], W=["idrow"])
        S.op("dve", lambda e: e.tensor_copy(ident_b[:], ident_f[:]), R=["identf"], W=["identb"])

        S.dma("sp", negb[:], bgl_d, "ld_negb", W=["negb"])
        S.dma("sp", lbp[:], lbp_d, "ld_lbp", W=["lbp"])
        S.op("dve", lambda e: e.tensor_scalar(negb[:], negb[:], -1.0, None, op0=ALU.mult), R=["negb"], W=["negb"])
        S.op("dve", lambda e: e.tensor_tensor(c2[:], lbp[:, 0:8], lbp[:, 8:16], op=ALU.subtract), R=["lbp"], W=["c2"])
        S.op("act", lambda e: e.activation(out=c2[:], in_=c2[:], func=AF.Tanh, scale=0.5), R=["c2"], W=["c2"])
        S.op("dve", lambda e: e.tensor_scalar(c1[:], c2[:], -0.25, 0.25, op0=ALU.mult, op1=ALU.add), R=["c2"], W=["c1"])
        S.op("dve", lambda e: e.tensor_scalar(nc1[:], c2[:], 0.25, -0.25, op0=ALU.mult, op1=ALU.add), R=["c2"], W=["nc1"])
        S.op("act", lambda e: e.activation(out=lnc1[:], in_=c1[:], func=AF.Ln), R=["c1"], W=["lnc1"])
        S.op("dve", lambda e: e.tensor_scalar(c2[:], c2[:], 0.25, 0.75, op0=ALU.mult, op1=ALU.add), R=["c2", "c1", "nc1"], W=["c2"])

        S.dma("pool", wga[:], win_r[:, :, GA_OFF:GA_OFF + 16], "ld_wga", W=["wga"])
        S.dma("pool", wlr[:], wlr_d, "ld_wlr", W=["wlr"])
        xT_r = xT_d.rearrange("(c p) n -> p c n", p=128)
        def XK(t0):
            return [f"xT{c}_{min(t0 // BLK, NBLK)}" for c in range(8)]

        def load_xT(bi):
            c0 = bi * BLK
            n = BLK if bi < NBLK else NS
            for c in range(8):
                S.dma("pool", xT[:, c, c0:c0 + n], xT_r[:, c, c0:c0 + n], f"ld_xT_{bi}", W=[f"xT{c}_{bi}"])
        load_xT(0)

        wu = [sb.alloc(f"wu{i}", [128, 8, 1024], BF16) for i in range(2)]
        g_u = [sb.alloc(f"g_u{i}", [128, 256], F32) for i in range(2)]
        GT = 1
        NSL = 6
        S0b = [sb.alloc(f"S0b{i}", [128, GT, 256], F32) for i in range(NSL)]
        S0bf = [sb.alloc(f"S0bf{i}", [128, 256], BF16) for i in range(2)]
        th = [sb.alloc(f"th_{i}", [128, 512], F32) for i in range(2)]
        sq = [sb.alloc(f"sq_{i}", [128, 512], F32) for i in range(2)]
        g1 = sb.alloc("g1", [128, 4, 128], F32)
        Eb = sb.alloc("Eb", [128, 4, 128], F32)
        keT = sb.alloc("keT", [128, 4, 128], BF16)
        kdT = sb.alloc("kdT", [128, 4, 128], BF16)
        qeT = [[sb.alloc(f"qeT_{p}{i}", [128, 512], BF16) for i in range(2)] for p in range(2)]
        kd = [[sb.alloc(f"kd_{p}{i}", [128, 512], BF16) for i in range(2)] for p in range(2)]
        ATb = [[sb.alloc(f"ATb_{p}{i}", [128, 4, 128], BF16) for i in range(2)] for p in range(2)]
        EbL = [[sb.alloc(f"EbL_{p}{i}", [128, 4], F32) for i in range(2)] for p in range(2)]
        vbf = [sb.alloc(f"vbf{p}", [128, 4, 256], BF16) for p in range(3)]
        ug = [sb.alloc(f"ug{p}", [128, 4, 256], F32) for p in range(3)]
        Sst = sb.alloc("Sst", [128, 256], F32)
        Sbf = [sb.alloc(f"Sbf{i}", [128, 4, 256], BF16) for i in range(2)]
        onb = [sb.alloc("onb0", [128, 4, 256], BF16), sb.alloc("onb1", [128, 4, 128], BF16)]
        junk = sb.alloc("junk", [128, 256], BF16)
        ssq = sb.alloc("ssq", [128, 8], F32)
        rstd = sb.alloc("rstd", [128, 8], F32)
        eps_t = sb.alloc("eps_t", [128, 1], F32)
        one_t = sb.alloc("one_t", [128, 1], F32)
        s_e = sb.alloc("s_e", [128, 2, NS], F32)
        s_g = sb.alloc("s_g", [128, 2, NS], F32)
        s_q = sb.alloc("s_q", [128, 2, NS], F32)
        s_qb = sb.alloc("s_qb", [128, 2, NS], BF16)
        s_qe = sb.alloc("s_qe", [128, 2, NS], F32)
        s_k = sb.alloc("s_k", [128, 2, NS], F32)
        s_kb = sb.alloc("s_kb", [128, 2, NS], BF16)
        ktok = sb.alloc("ktok", [16, 2, 128], F32)
        Ks = [sb.alloc(f"Ks{i}", [16, 128], BF16) for i in range(2)]
        Qsel = sb.alloc("Qsel", [128, 2, NS, NS], BF16)
        qkd = sb.alloc("qkd", [16, 2, NS], BF16)
        s_v = sb.alloc("s_v", [16, 256], BF16)
        s_u = sb.alloc("s_u", [16, 256], F32)
        s_on = sb.alloc("s_on", [16, 256], BF16)
        S.op("dve", lambda e: e.memset(eps_t[:], EPS), W=["eps_t"])
        S.op("dve", lambda e: e.memset(one_t[:], 1.0), W=["one_t"])

        def load_unit_weights(u, slot):
            w = wu[slot]
            key = f"wu{slot}"
            if u < 4:
                h = u
                segs = [(0, h * 128, 128), (128, 512 + h * 128, 128), (256, 1024 + h * 256, 256),
                        (512, 2048 + h * 256, 256)]
            else:
                j = u - 4
                segs = [(0, HQ_OFF + j * 256, 256), (256, HF_OFF + j * 256, 256),
                        (512, HI_OFF + j * 256, 256), (768, HR_OFF + j * 256, 256)]
            for (o, c0, n) in segs:
                S.dma("pool", w[:, :, o:o + n], win_r[:, :, c0:c0 + n], f"ld_wu{slot}", W=[key])

        order = [0, 1, 2, 3, 4, 5, 6, 7]
        load_unit_weights(order[0], 0)
        for bi_ in range(1, NBLK + 1):
            load_xT(bi_)
        load_unit_weights(order[1], 1)

        for bi in range(NBLK + 1):
            t0 = bi * BLK
            n = BLK if bi < NBLK else NS
            a = nxt("A")
            def f(e, a=a, t0=t0, n=n):
                for c in range(8):
                    r = e.matmul(PA[a][0:16, 0:n], lhsT=wga[:, c, :], rhs=xT[:, c, t0:t0 + n],
                                 start=(c == 0), stop=(c == 7))
                return r
            S.op("pe", f, R=["wga"] + XK(t0), W=[f"PA{a}"])
            S.op("act", lambda e, a=a, t0=t0, n=n: e.activation(out=gaT[:, t0:t0 + n], in_=PA[a][0:16, 0:n], func=AF.Copy),
                 R=[f"PA{a}"], W=["gaT"])

        def proj_fm(slot, woff, t0, n):
            a = nxt("A")
            def f(e):
                for c in range(8):
                    r = e.matmul(PA[a][:, 0:n], lhsT=wu[slot][:, c, woff:woff + 128], rhs=xT[:, c, t0:t0 + n],
                                 start=(c == 0), stop=(c == 7))
                return r
            S.op("pe", f, R=[f"wu{slot}"] + XK(t0), W=[f"PA{a}"])
            return a

        def proj_tm(slot, woff, t0, m):
            b = nxt("B")
            def f(e):
                for c in range(8):
                    r = e.matmul(PB[b][0:m, :], lhsT=xT[:, c, t0:t0 + m], rhs=wu[slot][:, c, woff:woff + 512],
                                 start=(c == 0), stop=(c == 7))
                return r
            S.op("pe", f, R=[f"wu{slot}"] + XK(t0), W=[f"PB{b}"])
            return b

        def rstd_from(ssq_ap, rstd_ap, m, dv, keys_r, keys_w):
            S.op("act", lambda e: e.activation(out=rstd_ap, in_=ssq_ap, func=AF.Ln, scale=1.0 / dv, bias=eps_t[0:m, :]),
                 R=keys_r, W=keys_w)
            S.op("act", lambda e: e.activation(out=rstd_ap, in_=rstd_ap, func=AF.Exp, scale=-0.5),
                 R=keys_w, W=keys_w)

        class Item:
            pass

        items = []
        for ui, u in enumerate(order):
            for bi in list(range(NBLK)) + ["s"]:
                it = Item()
                it.ui, it.u, it.slot, it.bi = ui, u, ui % 2, bi
                it.p = len(items) % 2
                it.q3 = len(items) % 3
                it.gla = u < 4
                it.nh = 1 if it.gla else 2
                it.DV = 256 if it.gla else 128
                it.vr_off = 256 if it.gla else 512
                it.vc0 = 2 * u if it.gla else 8 + 2 * (u - 4)
                it.gk = f"g_u{ui % 2}"
                items.append(it)

        def hd_of(it, e_):
            return 2 * (it.u - 4) + e_

        def vsl_of(it, e_):
            return slice(0, 256) if it.gla else slice(e_ * 128, (e_ + 1) * 128)

        def alpha1(it):
            slot, q3 = it.slot, it.q3
            gu = g_u[it.ui % 2]
            if it.bi == 0:
                src = glag_d[:, it.u * 256:(it.u + 1) * 256] if it.gla else hgg_d[:, (it.u - 4) * 256:(it.u - 3) * 256]
                S.dma("sp", gu[:], src, f"ld_gu{it.ui % 2}", W=[it.gk])
            if it.bi == "s":
                b = proj_tm(slot, it.vr_off, T, NS)
                S.op("act", lambda e: e.activation(out=s_v[:], in_=PB[b][0:NS, 0:256], func=AF.Copy), R=[f"PB{b}"], W=["s_v"])
                S.op("act", lambda e: e.activation(out=s_u[:], in_=PB[b][0:NS, 256:512], func=AF.Copy), R=[f"PB{b}"], W=["s_u"])
                if not it.gla:
                    for e_ in range(2):
                        a = proj_fm(slot, e_ * 128, T, NS)
                        S.op("act", lambda e, a=a, e_=e_: e.activation(out=s_q[:, e_, :], in_=PA[a][:, 0:NS], func=AF.Copy),
                             R=[f"PA{a}"], W=["s_q"])
                        a = proj_fm(slot, 256 + e_ * 128, T, NS)
                        S.op("dve", lambda e, a=a, e_=e_: e.tensor_copy(s_k[:, e_, :], PA[a][:, 0:NS]),
                             R=[f"PA{a}"], W=["s_k"])
                for g in range(NSL):
                    load_S0(it, g)
                yield
                return
            t0 = it.bi * BLK
            for i in range(4):
                b = proj_tm(slot, it.vr_off, t0 + i * 128, 128)
                S.copy(vbf[q3][:, i, :], PB[b][:, 0:256], R=[f"PB{b}"], W=[f"vbf{q3}_{i}"])
                S.copy(ug[q3][:, i, :], PB[b][:, 256:512], R=[f"PB{b}"], W=[f"ug{q3}_{i}"])
                yield
            yield "TILES_DONE"
            if not it.gla:
                yield "WAIT_BETA"
                for e_ in range(2):
                    a = proj_fm(slot, e_ * 128, t0, BLK)
                    S.copy(sq[e_][:], PA[a][:], R=[f"PA{a}"], W=[f"sq_{e_}"])
                    yield
                    a = proj_fm(slot, 256 + e_ * 128, t0, BLK)
                    S.copy(th[e_][:], PA[a][:], R=[f"PA{a}"], W=[f"th_{e_}"])
                    yield

        def alpha2(it):
            q3 = it.q3
            gu = g_u[it.ui % 2]
            if it.bi == "s":
                S.op("act", lambda e: e.activation(out=s_u[:], in_=s_u[:], func=AF.Silu), R=["s_u"], W=["s_u"])
                S.op("pool", lambda e: e.tensor_tensor(s_u[:], s_u[:], gu[0:NS, :], op=ALU.mult), R=["s_u", it.gk], W=["s_u"])
                if not it.gla:
                    S.op("act", lambda e: e.activation(out=s_q[:], in_=s_q[:], func=AF.Silu), R=["s_q"], W=["s_q"])
                    S.op("act", lambda e: e.activation(out=s_k[:], in_=s_k[:], func=AF.Tanh, scale=0.5), R=["s_k"], W=["s_k"])
                return
            ugk = [f"ug{q3}_{i}" for i in range(4)]
            S.op("act", lambda e: e.activation(out=ug[q3][:], in_=ug[q3][:], func=AF.Silu), R=ugk, W=ugk)
            for i in range(4):
                S.op("pool", lambda e, i=i: e.tensor_tensor(ug[q3][:, i, :], ug[q3][:, i, :], gu[:], op=ALU.mult),
                     R=[f"ug{q3}_{i}", it.gk], W=[f"ug{q3}_{i}"])
            if not it.gla:
                for e_ in range(2):
                    S.op("act", lambda e, e_=e_: e.activation(out=sq[e_][:], in_=sq[e_][:], func=AF.Silu),
                         R=[f"sq_{e_}"], W=[f"sq_{e_}"])
                    S.op("act", lambda e, e_=e_: e.activation(out=th[e_][:], in_=th[e_][:], func=AF.Tanh, scale=-0.5),
                         R=[f"th_{e_}"], W=[f"th_{e_}"])

        def load_S0(it, g):
            sl = g % NSL
            n0 = g * GT
            if it.gla:
                S.dma("sp", S0b[sl][:], sg_d[n0:n0 + GT, it.u].rearrange("n k v -> k n v"), f"ld_S0{sl}", W=[f"S0b{sl}"])
            else:
                j = it.u - 4
                for hh in range(2):
                    S.dma("sp", S0b[sl][:, :, hh * 128:(hh + 1) * 128],
                          sh_d[n0:n0 + GT, 2 * j + hh].rearrange("n k v -> k n v"), f"ld_S0{sl}", W=[f"S0b{sl}"])

        def beta(it):
            slot, p = it.slot, it.p
            g1f = g1[:].rearrange("p c t -> p (c t)")
            Ebf = Eb[:].rearrange("p c t -> p (c t)")
            keTf = keT[:].rearrange("p c t -> p (c t)")
            mskf = msk[:].rearrange("p c t -> p (c t)")
            if it.bi == "s":
                nh = it.nh
                for e_ in range(nh):
                    if it.gla:
                        h = it.u
                        a = nxt("A")
                        S.op("pe", lambda e, a=a, h=h: e.matmul(PA[a][:, 0:NS], lhsT=wlr[:, h * 128:(h + 1) * 128],
                                                              rhs=gaT[:, T:T + NS], start=True, stop=True),
                             R=["wlr", "gaT"], W=[f"PA{a}"])
                        S.op("act", lambda e, a=a, h=h, e_=e_: e.activation(out=s_g[:, e_, :], in_=PA[a][:, 0:NS], func=AF.Exp,
                                                                          scale=-1.0, bias=negb[:, h:h + 1]),
                             R=[f"PA{a}", "negb"], W=["s_g"])
                        S.op("act", lambda e, e_=e_: e.activation(out=s_g[:, e_, :], in_=s_g[:, e_, :], func=AF.Ln, scale=1.0,
                                                                  bias=one_t[:]), R=["s_g", "one_t"], W=["s_g"])
                        sE = -1.0 / 16.0
                        a = proj_fm(slot, 128, T, NS)
                        S.op("act", lambda e, a=a, e_=e_: e.activation(out=s_k[:, e_, :], in_=PA[a][:, 0:NS], func=AF.Copy),
                             R=[f"PA{a}"], W=["s_k"])
                        a = proj_fm(slot, 0, T, NS)
                        S.op("act", lambda e, a=a, e_=e_: e.activation(out=s_q[:, e_, :], in_=PA[a][:, 0:NS], func=AF.Identity,
                                                                      scale=128.0 ** -0.5), R=[f"PA{a}"], W=["s_q"])
                    else:
                        hd = hd_of(it, e_)
                        S.op("act", lambda e, e_=e_, hd=hd: e.activation(out=s_g[:, e_, :], in_=s_k[:, e_, :], func=AF.Ln,
                                                                        scale=c1[:, hd:hd + 1], bias=c2[:, hd:hd + 1]),
                             R=["s_k", "c1", "c2"], W=["s_g"])
                        S.op("dve", lambda e, e_=e_, hd=hd: e.tensor_scalar(s_k[:, e_, :], s_k[:, e_, :], nc1[:, hd:hd + 1],
                                                                           c1[:, hd:hd + 1], op0=ALU.mult, op1=ALU.add),
                             R=["s_k", "c1", "nc1", "s_g"], W=["s_k"])
                        sE = 1.0
                    S.op("act", lambda e, e_=e_, sE=sE: e.activation(out=s_e[:, e_, :], in_=s_g[:, e_, :], func=AF.Exp, scale=sE),
                         R=["s_g"], W=["s_e"])
                    yield
                S.op("dve", lambda e: e.tensor_copy(s_kb[:, 0:nh, :], s_k[:, 0:nh, :]), R=["s_k"], W=["s_kb"])
                S.op("dve", lambda e: e.tensor_copy(s_qb[:, 0:nh, :], s_q[:, 0:nh, :]), R=["s_q"], W=["s_qb"])
                S.op("dve", lambda e: e.tensor_tensor(s_qe[:, 0:nh, :], s_q[:, 0:nh, :], s_e[:, 0:nh, :], op=ALU.mult),
                     R=["s_q", "s_e"], W=["s_qe"])
                S.op("dve", lambda e: e.tensor_tensor(
                    Qsel[:, 0:nh, :, :], s_qe[:, 0:nh, :].unsqueeze(3).broadcast_to([128, nh, NS, NS]),
                    idrow[:].unsqueeze(1).broadcast_to([128, nh, NS, NS]), op=ALU.mult), R=["s_qe", "idrow"], W=["Qsel"])
                yield
                for e_ in range(nh):
                    x = nxt("X")
                    S.op("pe", lambda e, e_=e_, x=x: e.matmul(PX[x][0:NS, 0:128], lhsT=s_kb[:, e_, :], rhs=ident_b[:],
                                                            start=True, stop=True), R=["s_kb", "identb"], W=[f"PX{x}"])
                    S.op("dve", lambda e, e_=e_, x=x: e.tensor_copy(ktok[:, e_, :], PX[x][0:NS, 0:128]),
                         R=[f"PX{x}"], W=["ktok"])
                    x = nxt("X")
                    S.op("pe", lambda e, e_=e_, x=x: e.matmul(PX[x][0:NS, 0:NS], lhsT=s_qb[:, e_, :], rhs=s_kb[:, e_, :],
                                                            start=True, stop=True), R=["s_kb", "s_qb"], W=[f"PX{x}"])
                    S.op("dve", lambda e, e_=e_, x=x: e.tensor_tensor(qkd[:, e_, :], PX[x][0:NS, 0:NS], ident_f[0:NS, 0:NS],
                                                                    op=ALU.mult), R=[f"PX{x}", "identf"], W=["qkd"])
                    yield
                return
            t0 = it.bi * BLK
            for e_ in range(it.nh):
                if it.gla:
                    h = it.u
                    a = nxt("A")
                    S.op("pe", lambda e, a=a, h=h: e.matmul(PA[a][:], lhsT=wlr[:, h * 128:(h + 1) * 128],
                                                          rhs=gaT[:, t0:t0 + BLK], start=True, stop=True),
                         R=["wlr", "gaT"], W=[f"PA{a}"])
                    S.op("act", lambda e, a=a, h=h: e.activation(out=g1f, in_=PA[a][:], func=AF.Exp, scale=-1.0,
                                                               bias=negb[:, h:h + 1]), R=[f"PA{a}", "negb"], W=["g1"])
                    S.op("act", lambda e: e.activation(out=g1f, in_=g1f, func=AF.Ln, scale=1.0, bias=one_t[:]),
                         R=["g1", "one_t"], W=["g1"])
                    sE = -1.0 / 16.0
                else:
                    hd = hd_of(it, e_)
                    S.op("act", lambda e, e_=e_, hd=hd: e.activation(out=g1f, in_=th[e_][:], func=AF.Ln, scale=nc1[:, hd:hd + 1],
                                                                    bias=c2[:, hd:hd + 1]), R=[f"th_{e_}", "nc1", "c2"], W=["g1"])
                    sE = 1.0
                yield
                S.op("dve", lambda e: e.tensor_tensor_scan(g1f, mskf, g1f, 0.0, op0=ALU.mult, op1=ALU.add),
                     R=["g1", "msk"], W=["g1"])
                S.op("act", lambda e, sE=sE: e.activation(out=Ebf, in_=g1f, func=AF.Exp, scale=sE), R=["g1"], W=["Eb"])
                if it.gla:
                    S.op("act", lambda e, sE=sE: e.activation(out=g1f, in_=g1f, func=AF.Exp, scale=-sE), R=["g1"], W=["g1"])
                else:
                    S.op("act", lambda e, sE=sE, hd=hd: e.activation(out=g1f, in_=g1f, func=AF.Exp, scale=-sE,
                                                                    bias=lnc1[:, hd:hd + 1]), R=["g1", "lnc1"], W=["g1"])
                yield
                if it.gla:
                    a = proj_fm(slot, 128, t0, BLK)
                    S.op("dve", lambda e, a=a: e.tensor_tensor(keTf, PA[a][:], g1f, op=ALU.mult),
                         R=[f"PA{a}", "g1"], W=["keT"])
                    a = proj_fm(slot, 0, t0, BLK)
                    S.op("dve", lambda e, a=a, e_=e_: e.scalar_tensor_tensor(
                        qeT[p][e_][:], PA[a][:], 128.0 ** -0.5, Ebf, op0=ALU.mult, op1=ALU.mult),
                         R=[f"PA{a}", "Eb"], W=[f"qeT_{p}{e_}"])
                else:
                    S.op("dve", lambda e, e_=e_: e.scalar_tensor_tensor(keTf, th[e_][:], 1.0, g1f, op0=ALU.add, op1=ALU.mult),
                         R=[f"th_{e_}", "g1"], W=["keT"])
                    S.op("pool", lambda e, e_=e_: e.tensor_tensor(qeT[p][e_][:], sq[e_][:], Ebf, op=ALU.mult),
                         R=[f"sq_{e_}", "Eb"], W=[f"qeT_{p}{e_}"])
                yield
                S.op("pool", lambda e: e.tensor_tensor(kdT[:], keT[:], Eb[:, :, 127:128].broadcast_to([128, 4, 128]), op=ALU.mult),
                     R=["keT", "Eb"], W=["kdT"])
                S.op("pool", lambda e, e_=e_: e.tensor_copy(EbL[p][e_][:], Eb[:, :, 127]), R=["Eb"], W=[f"EbL_{p}{e_}"])
                a = nxt("A")
                def f(e, e_=e_, a=a):
                    for cc in range(4):
                        r = e.matmul(PA[a][:, cc * 128:(cc + 1) * 128], lhsT=keT[:, cc, :],
                                     rhs=qeT[p][e_][:, cc * 128:(cc + 1) * 128], start=True, stop=True)
                    return r
                S.op("pe", f, R=["keT", f"qeT_{p}{e_}"], W=[f"PA{a}"])
                S.op("dve", lambda e, e_=e_, a=a: e.tensor_tensor(ATb[p][e_][:].rearrange("p c t -> p (c t)"), PA[a][:],
                                                                 U4[:].rearrange("p c t -> p (c t)"), op=ALU.mult),
                     R=[f"PA{a}", "U4"], W=[f"ATb_{p}{e_}"])
                yield
                x = nxt("X")
                def f(e, x=x):
                    for cc in range(4):
                        r = e.matmul(PX[x][:, cc * 128:(cc + 1) * 128], lhsT=kdT[:, cc, :], rhs=ident_b[:], start=True, stop=True)
                    return r
                S.op("pe", f, R=["kdT", "identb"], W=[f"PX{x}"])
                S.copy(kd[p][e_][:], PX[x][:], R=[f"PX{x}"], W=[f"kd_{p}{e_}"])
                yield

        def stage2(it):
            slot, p, nh, DV, q3 = it.slot, it.p, it.nh, it.DV, it.q3
            if it.bi == "s":
                for n in range(NS):
                    sl = n % NSL
                    bsl = n % 2
                    S.copy(S0bf[bsl][:], S0b[sl][:, 0, :], R=[f"S0b{sl}"], W=[f"S0bf{bsl}"])
                    for e_ in range(nh):
                        vsl = vsl_of(it, e_)
                        d = e_
                        S.op("pe", lambda e, e_=e_, n=n, bsl=bsl, vsl=vsl, d=d: e.matmul(
                            PD[d][0:NS, 0:DV], lhsT=Qsel[:, e_, n, :], rhs=S0bf[bsl][:, vsl], start=(n == 0), stop=False),
                             R=["Qsel", f"S0bf{bsl}"], W=[f"PD{d}"])
                        kr = nxt("K")
                        S.op("dve", lambda e, e_=e_, n=n, kr=kr: e.tensor_scalar(
                            Ks[kr][:], ktok[:, e_, :], ident_f[0:NS, n:n + 1], None, op0=ALU.mult),
                             R=["ktok", "identf"], W=[f"Ks{kr}"])
                        x = nxt("X")
                        S.op("pe", lambda e, kr=kr, vsl=vsl, x=x: e.matmul(
                            PX[x][:, 0:DV], lhsT=Ks[kr][:], rhs=s_v[:, vsl], start=True, stop=True),
                             R=[f"Ks{kr}", "s_v"], W=[f"PX{x}"])
                        S.op("dve", lambda e, e_=e_, n=n, sl=sl, vsl=vsl, x=x: e.scalar_tensor_tensor(
                            S0b[sl][:, 0, vsl], S0b[sl][:, 0, vsl], s_e[:, e_, n:n + 1], PX[x][:, 0:DV],
                            op0=ALU.mult, op1=ALU.add), R=[f"PX{x}", f"S0b{sl}", "s_e"], W=[f"S0b{sl}"])
                    if it.gla:
                        S.dma("pool", gs_d[n:n + 1, it.u].rearrange("n k v -> k n v"), S0b[sl][:], f"st_Sn{sl}", R=[f"S0b{sl}"])
                    else:
                        j = it.u - 4
                        for hh in range(2):
                            S.dma("pool", hs_d[n:n + 1, 2 * j + hh].rearrange("n k v -> k n v"),
                                  S0b[sl][:, :, hh * 128:(hh + 1) * 128], f"st_Sn{sl}", R=[f"S0b{sl}"])
                    if n + NSL < NS:
                        load_S0(it, n + NSL)
                    yield
                for e_ in range(nh):
                    vsl = vsl_of(it, e_)
                    d = e_
                    S.op("pe", lambda e, e_=e_, vsl=vsl, d=d: e.matmul(PD[d][0:NS, 0:DV], lhsT=qkd[:, e_, :], rhs=s_v[:, vsl],
                                                                     start=False, stop=True),
                         R=["qkd", "s_v"], W=[f"PD{d}"])
                    col = e_
                    S.op("act", lambda e, d=d, col=col: e.activation(out=junk[0:NS, 0:DV], in_=PD[d][0:NS, 0:DV], func=AF.Square,
                                                                     accum_out=ssq[0:NS, col:col + 1]),
                         R=[f"PD{d}"], W=["junk", f"ssq{col}"])
                    rstd_from(ssq[0:NS, col:col + 1], rstd[0:NS, col:col + 1], NS, DV, [f"ssq{col}", "eps_t"], [f"rstd{col}"])
                    S.op("dve", lambda e, d=d, col=col, vsl=vsl: e.scalar_tensor_tensor(
                        s_on[:, vsl], PD[d][0:NS, 0:DV], rstd[0:NS, col:col + 1], s_u[:, vsl], op0=ALU.mult, op1=ALU.mult),
                         R=[f"PD{d}", f"rstd{col}", "s_u"], W=["s_on"])
                x = nxt("X")
                def f(e, x=x):
                    for jj in range(2):
                        r = e.matmul(PX[x][:, jj * NS:(jj + 1) * NS], lhsT=s_on[:, jj * 128:(jj + 1) * 128],
                                     rhs=ident_b[0:NS, 0:NS], start=True, stop=True)
                    return r
                S.op("pe", f, R=["s_on", "identb"], W=[f"PX{x}"])
                vc0 = it.vc0
                S.op("act", lambda e, x=x, vc0=vc0: e.activation(
                    out=onT[:, vc0:vc0 + 2, T:T + NS], in_=PX[x][:, 0:2 * NS].rearrange("p (j t) -> p j t", t=NS),
                    func=AF.Copy), R=[f"PX{x}"], W=[f"onT{vc0}_s"])
                yield
                return
            bi = it.bi
            t0 = bi * BLK
            pb = bi % 2
            for e_ in range(nh):
                vsl = vsl_of(it, e_)
                sks = ["Sst0", "Sst1"] if it.gla else [f"Sst{e_}"]
                sbk = (lambda q: [f"Sbf{q}_0", f"Sbf{q}_1"]) if it.gla else (lambda q, e_=e_: [f"Sbf{q}_{e_}"])
                Sv = Sst[:, vsl]
                cpb = 512 // DV
                nbk = 4 // cpb
                xb = [nxt("X") for _ in range(nbk)]
                for bk in range(nbk):
                    def f(e, e_=e_, bk=bk, vsl=vsl):
                        for j in range(cpb):
                            cc = bk * cpb + j
                            r = e.matmul(PX[xb[bk]][:, j * DV:(j + 1) * DV], lhsT=kd[p][e_][:, cc * 128:(cc + 1) * 128],
                                         rhs=vbf[q3][:, cc, vsl], start=True, stop=True)
                        return r
                    S.op("pe", f, R=[f"kd_{p}{e_}"] + [f"vbf{q3}_{bk * cpb + j}" for j in range(cpb)], W=[f"PX{xb[bk]}"])
                for cc in range(4):
                    gc = bi * 4 + cc
                    bk, j = cc // cpb, cc % cpb
                    usl = PX[xb[bk]][:, j * DV:(j + 1) * DV]
                    if gc == 0:
                        S.op("dve", lambda e, usl=usl: e.tensor_copy(Sv, usl), R=[f"PX{xb[bk]}"], W=sks)
                    else:
                        S.op("dve", lambda e, usl=usl, cc=cc: e.scalar_tensor_tensor(
                            Sv, Sv, EbL[p][e_][:, cc:cc + 1], usl, op0=ALU.mult, op1=ALU.add),
                             R=[f"PX{xb[bk]}", f"EbL_{p}{e_}"] + sks, W=sks)
                    S.copy(Sbf[pb][:, cc, vsl], Sv, R=sks, W=sbk(pb))
                yield
                db = [nxt("D") for _ in range(nbk)]
                for bk in range(nbk):
                    def f(e, e_=e_, bk=bk, vsl=vsl):
                        for j in range(cpb):
                            cc = bk * cpb + j
                            gc = bi * 4 + cc
                            osl = PD[db[bk]][:, j * DV:(j + 1) * DV]
                            r = e.matmul(osl, lhsT=ATb[p][e_][:, cc, :], rhs=vbf[q3][:, cc, vsl], start=True, stop=(gc == 0))
                            if gc > 0:
                                prev = Sbf[1 - pb][:, 3, vsl] if cc == 0 else Sbf[pb][:, cc - 1, vsl]
                                r = e.matmul(osl, lhsT=qeT[p][e_][:, cc * 128:(cc + 1) * 128], rhs=prev, start=False, stop=True)
                        return r
                    S.op("pe", f, R=[f"ATb_{p}{e_}", f"qeT_{p}{e_}"] + sbk(pb) + sbk(1 - pb) +
                         [f"vbf{q3}_{bk * cpb + j}" for j in range(cpb)], W=[f"PD{db[bk]}"])
                yield
                for cc in range(4):
                    bk, j = cc // cpb, cc % cpb
                    col = e_ * 4 + cc
                    S.op("act", lambda e, bk=bk, j=j, col=col: e.activation(
                        out=junk[:, 0:DV], in_=PD[db[bk]][:, j * DV:(j + 1) * DV], func=AF.Square, accum_out=ssq[:, col:col + 1]),
                         R=[f"PD{db[bk]}"], W=["junk", f"ssq{e_}"])
                rstd_from(ssq[:, e_ * 4:e_ * 4 + 4], rstd[:, e_ * 4:e_ * 4 + 4], 128, DV, [f"ssq{e_}", "eps_t"], [f"rstd{e_}"])
                for cc in range(4):
                    bk, j = cc // cpb, cc % cpb
                    col = e_ * 4 + cc
                    S.op("dve", lambda e, bk=bk, j=j, col=col, cc=cc: e.scalar_tensor_tensor(
                        onb[e_][:, cc, 0:DV], PD[db[bk]][:, j * DV:(j + 1) * DV], rstd[:, col:col + 1], ug[q3][:, cc, vsl],
                        op0=ALU.mult, op1=ALU.mult),
                         R=[f"PD{db[bk]}", f"rstd{e_}", f"ug{q3}_{cc}"], W=[f"onb{e_}"])
                yield
                nv = DV // 128
                for jj in range(nv):
                    x2 = nxt("X")
                    def f(e, e_=e_, jj=jj, x2=x2):
                        for cc in range(4):
                            r = e.matmul(PX[x2][:, cc * 128:(cc + 1) * 128], lhsT=onb[e_][:, cc, jj * 128:(jj + 1) * 128],
                                         rhs=ident_b[:], start=True, stop=True)
                        return r
                    S.op("pe", f, R=[f"onb{e_}", "identb"], W=[f"PX{x2}"])
                    vc = it.vc0 + (jj if it.gla else e_)
                    S.copy(onT[:, vc, t0:t0 + BLK], PX[x2][:], R=[f"PX{x2}"], W=[f"onT{vc}_{bi}"])
                yield
            if bi == NBLK - 1:
                for e_ in range(nh):
                    if it.gla:
                        S.dma("sp", gp_d[it.u], Sst[:], "st_gp", R=["Sst0", "Sst1"])
                    else:
                        S.dma("sp", hp_d[hd_of(it, e_)], Sst[:, e_ * 128:(e_ + 1) * 128], f"st_hp{e_}", R=[f"Sst{e_}"])

        def run_all(g):
            for _ in g:
                pass

        def interleave(gens, beta_idx=None):
            alive = [True] * len(gens)
            paused = [False] * len(gens)
            while any(alive):
                for k in range(len(gens)):
                    if not alive[k]:
                        continue
                    if paused[k]:
                        if beta_idx is not None and alive[beta_idx]:
                            continue
                        paused[k] = False
                    try:
                        r = next(gens[k])
                        if r == "WAIT_BETA":
                            paused[k] = True
                    except StopIteration:
                        alive[k] = False

        nit = len(items)
        if "lsched" not in cache:
            cache["lsched"] = ListScheduler()
            cache["orders"] = {}
        lsched = cache["lsched"]

        def flush(tag):
            recs = S.capture
            S.capture = None
            if tag not in cache["orders"]:
                cache["orders"][tag] = lsched.schedule(recs)
            order_, choice_ = cache["orders"][tag]
            for i in order_:
                S.emit(recs[i], choice_.get(i))

        WIN = 8
        S.capture = []
        run_all(alpha1(items[0]))
        alpha2(items[0])
        run_all(beta(items[0]))
        run_all(alpha1(items[1]))
        for k, it in enumerate(items):
            if k + 1 < nit:
                alpha2(items[k + 1])
            gens = [stage2(it)]
            bidx = None
            if k + 1 < nit:
                gens.append(beta(items[k + 1]))
                bidx = 1
            if k + 2 < nit:
                ga1 = alpha1(items[k + 2])
                if items[k + 2].bi != "s":
                    for r in ga1:
                        if r == "TILES_DONE":
                            break
                gens.append(ga1)
            interleave(gens, bidx)
            if k + 1 < nit:
                nx = items[k + 1]
                if nx.bi == "s" and nx.ui + 2 < len(order):
                    load_unit_weights(order[nx.ui + 2], nx.slot)
            if k % WIN == WIN - 1 or k == nit - 1:
                flush(f"step{k}")
                S.capture = []
        S.capture = None

        ONT_KEYS = [k for k in list(S.last_w.keys()) if k.startswith("onT")]
        sb.cur = region_mark + 64
        ALLU = [k for k in list(S.last_w.keys()) + list(S.readers.keys())
                if not (k.startswith("onT") or k.startswith("xT") or k in ("identf", "identb", "eps_t", "one_t"))]
        ALLU = sorted(set(ALLU))
        wo = sb.alloc("wo", [128, 8, 1024], BF16)
        sb.cur = region_mark + 64
        fw = [sb.alloc(f"fw{i}", [128, 8, 1024], BF16) for i in range(4)]
        wo_r = wout_d.rearrange("(c p) n -> p c n", p=128)
        mT = sb.alloc("mT", [128, 8, TT], BF16)
        tha = sb.alloc("tha", [128, 512], BF16)
        thb = sb.alloc("thb", [128, 512], F32)
        m1 = sb.alloc("m1", [128, 512], F32)
        f1_end = sb.cur
        assert f1_end <= sb.top
        for q in ("pool", "sp", "pe", "act", "dve"):
            S._wait(q, S._deps([], ALLU))
        srcs = [win_r[:, :, MGA_OFF:MGA_OFF + 1024], win_r[:, :, MGB_OFF:MGB_OFF + 1024],
                wbg_d.rearrange("(c p) n -> p c n", p=128), wbh_d.rearrange("(c p) n -> p c n", p=128)]
        for dc in range(8):
            for i in range(4):
                S.dma("pool", fw[i][:, :, dc * 128:(dc + 1) * 128], srcs[i][:, :, dc * 128:(dc + 1) * 128],
                      f"ld_fw{i}_{dc}", W=[f"fw{i}_{dc}"])

        S.capture = []
        for dc in range(8):
            for bi in range(NBLK + 1):
                t0 = bi * BLK
                n = BLK if bi < NBLK else NS
                onk = [k for k in ONT_KEYS if k.endswith(f"_{bi}" if bi < NBLK else "_s")]
                def mm(bank, wi, src_is_x, base):
                    def f(e):
                        for c in range(8):
                            rhs = xT[:, c, t0:t0 + n] if src_is_x else onT[:, base + c, t0:t0 + n]
                            r = e.matmul(bank[:, 0:n], lhsT=fw[wi][:, c, dc * 128:(dc + 1) * 128], rhs=rhs,
                                         start=(c == 0), stop=(c == 7))
                        return r
                    return f
                a = nxt("A")
                S.op("pe", mm(PA[a], 0, True, 0), R=[f"fw0_{dc}"] + XK(t0), W=[f"PA{a}"])
                S.op("act", lambda e, a=a: e.activation(out=tha[:, 0:n], in_=PA[a][:, 0:n], func=AF.Tanh, scale=0.5),
                     R=[f"PA{a}"], W=["tha"])
                a = nxt("A")
                S.op("pe", mm(PA[a], 1, True, 0), R=[f"fw1_{dc}"] + XK(t0), W=[f"PA{a}"])
                S.op("act", lambda e, a=a: e.activation(out=thb[:, 0:n], in_=PA[a][:, 0:n], func=AF.Tanh, scale=0.5),
                     R=[f"PA{a}"], W=["thb"])
                b = nxt("B")
                S.op("pe", mm(PB[b], 2, False, 0), R=[f"fw2_{dc}"] + onk, W=[f"PB{b}"])
                S.op("dve", lambda e, b=b: e.scalar_tensor_tensor(m1[:, 0:n], tha[:, 0:n], 1.0, PB[b][:, 0:n],
                                                                  op0=ALU.add, op1=ALU.mult), R=[f"PB{b}", "tha"], W=["m1"])
                b = nxt("B")
                S.op("pe", mm(PB[b], 3, False, 8), R=[f"fw3_{dc}"] + onk, W=[f"PB{b}"])
                S.op("dve", lambda e, b=b: e.scalar_tensor_tensor(thb[:, 0:n], thb[:, 0:n], 1.0, PB[b][:, 0:n],
                                                                  op0=ALU.add, op1=ALU.mult), R=[f"PB{b}", "thb"], W=["thb"])
                S.op("pool", lambda e, dc=dc: e.tensor_tensor(mT[:, dc, t0:t0 + n], m1[:, 0:n], thb[:, 0:n], op=ALU.add),
                     R=["m1", "thb"], W=[f"mT_{bi}"])
            S.dma("pool", wo[:, :, dc * 128:(dc + 1) * 128], wo_r[:, :, dc * 128:(dc + 1) * 128], f"ld_wo_{dc}", W=[f"fw0_{dc}"])

        flush("F1")
        for q in ("pool", "sp", "pe", "act", "dve"):
            S._wait(q, S._deps([], [f"fw{i}_{dc}" for i in (1, 2) for dc in range(8)]))
        sb.cur = region_mark + 64 + 16384
        lng = sb.alloc("lng", [128, D], F32)
        lnb = sb.alloc("lnb", [128, D], F32)
        xt = [sb.alloc(f"xt{i}", [128, D], F32) for i in range(4)]
        stt = sb.alloc("stt", [128, 12], F32)
        junk2 = sb.alloc("junk2", [128, D], BF16)
        mv = sb.alloc("mv", [128, 2], F32)
        rs2 = sb.alloc("rs2", [128, 2], F32)
        eps2 = sb.alloc("eps2", [128, 1], F32)
        assert sb.cur <= region_mark + 64 + 3 * 16384
        S.capture = []
        S.dma("sp", lng[:], lng_d, "ld_ln", W=["lng"])
        S.dma("sp", lnb[:], lnb_d, "ld_lnb", W=["lnb"])
        S.op("dve", lambda e: e.memset(eps2[:], EPS / (ALPHA * ALPHA)), W=["eps2"])
        CY = 0.5 / ALPHA
        ntile = T // 128 + 1
        for ti in range(ntile):
            r0 = ti * 128
            m = 128 if ti < T // 128 else NS
            sl = ti % 4
            bi = min(ti // 4, NBLK)
            if ti == 0:
                for tj in range(min(2, ntile)):
                    mj = 128 if tj < T // 128 else NS
                    S.dma("sp", xt[tj % 4][0:mj, :], xtok_d[tj * 128:tj * 128 + mj, :], f"ld_xt{tj % 4}", W=[f"xt{tj % 4}"])
            if ti + 2 < ntile:
                tj = ti + 2
                mj = 128 if tj < T // 128 else NS
                S.dma("sp", xt[tj % 4][0:mj, :], xtok_d[tj * 128:tj * 128 + mj, :], f"ld_xt{tj % 4}", W=[f"xt{tj % 4}"])
            for hh in range(2):
                bq = (2 * ti + hh) % 4
                bank, bkey = ((PA, "PA") if bq < 2 else (PB, "PB"))
                bank = bank[bq % 2]
                bkey = f"{bkey}{bq % 2}"
                def f(e, bank=bank, hh=hh, r0=r0, m=m):
                    for c in range(8):
                        r = e.matmul(bank[0:m, :], lhsT=mT[:, c, r0:r0 + m], rhs=wo[:, c, hh * 512:(hh + 1) * 512],
                                     start=(c == 0), stop=(c == 7))
                    return r
                S.op("pe", f, R=[f"mT_{bi}"] + [f"fw0_{dc}" for dc in range(4 * hh, 4 * hh + 4)], W=[bkey])
                S.op("dve", lambda e, bank=bank, hh=hh, sl=sl, m=m: e.scalar_tensor_tensor(
                    xt[sl][0:m, hh * 512:(hh + 1) * 512], bank[0:m, :], CY, xt[sl][0:m, hh * 512:(hh + 1) * 512],
                    op0=ALU.mult, op1=ALU.add), R=[bkey, f"xt{sl}"], W=[f"xt{sl}"])
            S.op("act", lambda e, sl=sl, m=m: e.activation(out=junk2[0:m, :], in_=xt[sl][0:m, :], func=AF.Copy,
                                                           accum_out=stt[0:m, 0:1]), R=[f"xt{sl}"], W=["junk2", "stt0"])
            S.op("act", lambda e, sl=sl, m=m: e.activation(out=junk2[0:m, :], in_=xt[sl][0:m, :], func=AF.Square,
                                                           accum_out=stt[0:m, 1:2]), R=[f"xt{sl}"], W=["junk2", "stt1"])
            S.op("dve", lambda e, m=m: e.tensor_scalar(mv[0:m, 0:1], stt[0:m, 0:1], 1.0 / D, None, op0=ALU.mult),
                 R=["stt0"], W=["mv"])
            S.op("dve", lambda e, m=m: e.tensor_tensor(mv[0:m, 1:2], mv[0:m, 0:1], mv[0:m, 0:1], op=ALU.mult),
                 R=["mv"], W=["mv"])
            S.op("dve", lambda e, m=m: e.scalar_tensor_tensor(mv[0:m, 1:2], stt[0:m, 1:2], 1.0 / D, mv[0:m, 1:2],
                                                              op0=ALU.mult, op1=ALU.subtract), R=["stt1", "mv"], W=["mv"])
            S.op("act", lambda e, m=m: e.activation(out=rs2[0:m, 0:1], in_=mv[0:m, 1:2], func=AF.Ln, scale=1.0, bias=eps2[0:m, :]),
                 R=["mv", "eps2"], W=["rs2"])
            S.op("act", lambda e, m=m: e.activation(out=rs2[0:m, 0:1], in_=rs2[0:m, 0:1], func=AF.Exp, scale=-0.5),
                 R=["rs2"], W=["rs2"])
            S.op("dve", lambda e, m=m: e.scalar_tensor_tensor(rs2[0:m, 1:2], mv[0:m, 0:1], -1.0, rs2[0:m, 0:1],
                                                              op0=ALU.mult, op1=ALU.mult), R=["rs2", "mv"], W=["rs2"])
            S.op("act", lambda e, sl=sl, m=m: e.activation(out=xt[sl][0:m, :], in_=xt[sl][0:m, :], func=AF.Identity,
                                                           scale=rs2[0:m, 0:1], bias=rs2[0:m, 1:2]),
                 R=[f"xt{sl}", "rs2"], W=[f"xt{sl}"])
            S.op("dve", lambda e, sl=sl, m=m: e.tensor_tensor(xt[sl][0:m, :], xt[sl][0:m, :], lng[0:m, :], op=ALU.mult),
                 R=[f"xt{sl}", "lng"], W=[f"xt{sl}"])
            S.op("pool", lambda e, sl=sl, m=m: e.tensor_tensor(xt[sl][0:m, :], xt[sl][0:m, :], lnb[0:m, :], op=ALU.add),
                 R=[f"xt{sl}", "lnb"], W=[f"xt{sl}"])
            S.dma("sp", y_d[r0:r0 + m, :], xt[sl][0:m, :], f"st_y{sl}", R=[f"xt{sl}"])

        flush("F2")
        S.final_wait("sp", [k for k in S.dcnt if k.startswith("st_")])

    with nc.Block() as block:
        @block.sync
        def _(e):
            program("sp", e)

        @block.gpsimd
        def _(e):
            program("pool", e)

        @block.tensor
        def _(e):
            program("pe", e)

        @block.scalar
        def _(e):
            program("act", e)

        @block.vector
        def _(e):
            program("dve", e)
    return nc


_NC_CACHE = {}


def kernel(x_prompt, x_sample, state_gla, state_hgrn, w_in, w_gate_lr, b_gate_lr, gla_norm_g, w_br_gla,
           hgrn_lb_param, hgrn_norm_g, w_br_hgrn, w_out, ln_g, ln_b):
    f = lambda a: np.ascontiguousarray(np.asarray(a, dtype=np.float32))
    x_prompt, x_sample = f(x_prompt), f(x_sample)
    state_gla, state_hgrn = f(state_gla), f(state_hgrn)
    if "nc" not in _NC_CACHE:
        _NC_CACHE["nc"] = build_nc()
    nc = _NC_CACHE["nc"]
    shared = {
        "w_in": f(w_in)[0],
        "wlr": f(w_gate_lr)[0],
        "bgl": f(f(b_gate_lr)[0].reshape(4, 128).T),
        "glag": f(np.broadcast_to(f(gla_norm_g)[0].reshape(1, 1024), (128, 1024))),
        "lbp": f(f(hgrn_lb_param).reshape(2, 8, 128).transpose(2, 0, 1).reshape(128, 16)),
        "hgg": f(np.broadcast_to(f(hgrn_norm_g)[0].reshape(1, 1024), (128, 1024))),
        "wbg": f(w_br_gla)[0],
        "wbh": f(w_br_hgrn)[0],
        "wout": f(w_out)[0],
        "lng": f(np.broadcast_to(f(ln_g)[0].reshape(1, D), (128, D))),
        "lnb": f(np.broadcast_to(f(ln_b)[0].reshape(1, D), (128, D))),
    }
    in_maps = []
    for b in range(NCORES):
        xs = x_sample[b * NS:(b + 1) * NS, 0, :]
        xtok = np.concatenate([x_prompt[b], xs], axis=0)
        m = dict(shared)
        m["xtok"] = f(xtok)
        m["xT"] = f(xtok.T)
        m["sg"] = f(state_gla[0, b * NS:(b + 1) * NS])
        m["sh"] = f(state_hgrn[0, b * NS:(b + 1) * NS])
        in_maps.append(m)
    res = run_bass_kernel_spmd(nc, in_maps, core_ids=list(range(NCORES)))
    rs = res.results
    y_prompt = np.stack([r["y"][:T] for r in rs], axis=0)
    y_sample = np.concatenate([r["y"][T:TT] for r in rs], axis=0)[:, None, :]
    gp = np.stack([r["gp"] for r in rs], axis=0)[None]
    hp = np.stack([r["hp"] for r in rs], axis=0)[None]
    gs = np.concatenate([r["gs"] for r in rs], axis=0)[None]
    hs = np.concatenate([r["hs"] for r in rs], axis=0)[None]
    return (y_prompt.astype(np.float32), y_sample.astype(np.float32), gp.astype(np.float32),
            hp.astype(np.float32), gs.astype(np.float32), hs.astype(np.float32))
```

```python
import numpy as np
import concourse.bass as bass
import concourse.mybir as mybir
from concourse.bass_utils import run_bass_kernel_spmd

F32 = mybir.dt.float32
BF16 = mybir.dt.bfloat16
AF = mybir.ActivationFunctionType
ALU = mybir.AluOpType

NCORES = 8
D = 1024
T = 2048
NS = 16
TT = T + NS
NBLK = 4
BLK = 512
IN_DIM = 9232
GA_OFF = 3072
HQ_OFF = 3088
HF_OFF = HQ_OFF + 1024
HI_OFF = HQ_OFF + 2048
HR_OFF = HQ_OFF + 3072
MGA_OFF = HQ_OFF + 4096
MGB_OFF = MGA_OFF + 1024
ALPHA = 2.0 ** 0.25
EPS = 1e-5


class Sched:
    def __init__(self, nc, cache, me, eobj):
        self.nc = nc
        self.cache = cache
        self.me = me
        self.e = eobj
        self.cnt = {k: 0 for k in ("pe", "act", "dve", "pool")}
        self.waited = {}
        self.last_w = {}
        self.readers = {}
        self.dcnt = {}
        self.capture = None

    def emit(self, rec, eng=None):
        if rec[0] == "op":
            self.op(rec[1], rec[2], rec[3], rec[4])
        elif rec[0] == "flex":
            out, in_ = rec[2]
            if eng == "act":
                self.op("act", lambda e: e.activation(out=out, in_=in_, func=AF.Copy), rec[3], rec[4])
            else:
                self.op("dve", lambda e: e.tensor_copy(out, in_), rec[3], rec[4])
        else:
            self.dma(rec[1], rec[2][0], rec[2][1], rec[2][2], rec[3], rec[4])

    def copy(self, out, in_, R=(), W=()):
        W = list(W) + [k for k in R if len(k) == 3 and k[0] == "P" and k[1] in "ABDX"]
        if self.capture is None:
            self.op("dve", lambda e: e.tensor_copy(out, in_), R, W)
            return
        size = 1
        for v in out.shape[1:]:
            size *= v
        self.capture.append(("flex", "dve", (out, in_), list(R), list(W), 230 + 0.83 * size, 120 + 1.12 * size))

    def _semh(self, s):
        k = "sem_" + s
        if k not in self.cache:
            self.cache[k] = self.nc.alloc_semaphore("s_" + s)
        if s not in self.cnt and s not in self.dcnt:
            self.dcnt[s] = 0
        return self.cache[k]

    def _deps(self, R, W):
        best = {}
        def add(sv):
            s, v = sv
            if v > best.get(s, 0):
                best[s] = v
        for b in R:
            if b in self.last_w:
                add(self.last_w[b])
        for b in W:
            if b in self.last_w:
                add(self.last_w[b])
            for sv in self.readers.get(b, {}).items():
                add(sv)
        return best

    def _wait(self, eng, best):
        for s, v in best.items():
            if s == "pe" and eng == "pe":
                continue
            if self.waited.get((eng, s), 0) >= v:
                continue
            h = self._semh(s)
            if eng == self.me:
                self.e.wait_ge(h, v)
            self.waited[(eng, s)] = v

    def _book(self, me, R, W):
        s, v = me
        for b in R:
            d = self.readers.setdefault(b, {})
            if v > d.get(s, 0):
                d[s] = v
        for b in W:
            self.last_w[b] = me
            self.readers[b] = {}

    def op(self, eng, fn, R=(), W=()):
        W = list(W) + [k for k in R if len(k) == 3 and k[0] == "P" and k[1] in "ABDX"]
        if self.capture is not None:
            fe = _FakeEng(eng)
            fn(fe)
            calls = fe.calls

            def replay(e, calls=calls):
                r = None
                for name, args, kw in calls:
                    r = getattr(e, name)(*args, **kw)
                return r
            self.capture.append(("op", eng, replay, list(R), list(W), fe.dur, fe.tset))
            return
        self._wait(eng, self._deps(R, W))
        self.cnt[eng] += 1
        h = self._semh(eng)
        if eng == self.me:
            fn(self.e).then_inc(h, 1)
        self._book((eng, self.cnt[eng]), R, W)

    def dma(self, q, out, in_, sem, R=(), W=()):
        if self.capture is not None:
            self.capture.append(("dma", q, (out, in_, sem), list(R), list(W)))
            return
        self._wait(q, self._deps(R, W))
        h = self._semh(sem)
        self.dcnt[sem] += 16
        if q == self.me:
            self.e.dma_start(out=out, in_=in_).then_inc(h, 16)
        self._book((sem, self.dcnt[sem]), R, W)

    def final_wait(self, eng, sems):
        for s in sems:
            if self.dcnt.get(s, 0) > 0 and eng == self.me:
                self.e.wait_ge(self._semh(s), self.dcnt[s])


class _FakeIns:
    def then_inc(self, *a, **k):
        return self


class _FakeEng:
    def __init__(self, kind):
        self.kind = kind
        self.dur = 0.0
        self.tset = None
        self.calls = []

    def __getattr__(self, name):
        def call(*args, **kw):
            self.calls.append((name, args, kw))
            out = kw.get("out", args[0] if args else None)
            size = 1
            try:
                for v in out.shape[1:]:
                    size *= v
            except Exception:
                size = 256
            if name == "matmul":
                self.dur += 70 + 0.62 * size
            elif self.kind == "act":
                self.dur += 230 + 0.83 * size
                f = kw.get("func")
                if f in (AF.Silu, AF.Tanh):
                    self.tset = 18
                elif f in (AF.Exp, AF.Ln):
                    self.tset = 6
            elif self.kind == "dve":
                self.dur += (120 + 1.12 * size) * (2.0 if name == "tensor_tensor_scan" else 1.0)
            elif self.kind == "pool":
                self.dur += 320 + 1.6 * size
            else:
                self.dur += 100
            return _FakeIns()
        return call


class ListScheduler:
    def __init__(self):
        self.free = {k: 0.0 for k in ("pe", "act", "dve", "pool", "sp")}
        self.wfin = {}
        self.rfin = {}
        self.tset = None

    def schedule(self, recs):
        n = len(recs)
        dur = [0.0] * n
        lat = [0.0] * n
        tset = [None] * n
        for i, r in enumerate(recs):
            if r[0] == "op":
                dur[i] = r[5] + 60.0
                lat[i] = dur[i]
                tset[i] = r[6]
            elif r[0] == "flex":
                dur[i] = min(r[5], r[6]) + 60.0
                lat[i] = dur[i]
            else:
                dur[i] = 1000.0 if r[1] == "pool" else 150.0
                lat[i] = dur[i] + 2600.0
        preds = [set() for _ in range(n)]
        lw, rd = {}, {}
        for i, r in enumerate(recs):
            R, W = r[3], r[4]
            for k in R:
                if k in lw:
                    preds[i].add(lw[k])
            for k in W:
                if k in lw:
                    preds[i].add(lw[k])
                for j in rd.get(k, ()):
                    preds[i].add(j)
            for k in R:
                rd.setdefault(k, []).append(i)
            for k in W:
                lw[k] = i
                rd[k] = []
            preds[i].discard(i)
        succs = [[] for _ in range(n)]
        for i in range(n):
            for j in preds[i]:
                succs[j].append(i)
        prio = [0.0] * n
        for i in range(n - 1, -1, -1):
            m = 0.0
            for j in succs[i]:
                if prio[j] > m:
                    m = prio[j]
            prio[i] = lat[i] + m
        base = [0.0] * n
        for i, r in enumerate(recs):
            b = 0.0
            for k in r[3]:
                b = max(b, self.wfin.get(k, 0.0))
            for k in r[4]:
                b = max(b, self.wfin.get(k, 0.0), self.rfin.get(k, 0.0))
            base[i] = b
        npred = [len(p) for p in preds]
        fin = [0.0] * n
        ready = [i for i in range(n) if npred[i] == 0]
        order = []
        choice = {}
        while ready:
            best, bkey = None, None
            for i in ready:
                dep = base[i]
                for j in preds[i]:
                    if fin[j] + 80.0 > dep:
                        dep = fin[j] + 80.0
                if recs[i][0] == "flex":
                    sa = max(self.free["act"], dep)
                    sd = max(self.free["dve"], dep)
                    if sa + recs[i][5] < sd + recs[i][6]:
                        eng, st, du = "act", sa, recs[i][5] + 60.0
                    else:
                        eng, st, du = "dve", sd, recs[i][6] + 60.0
                else:
                    eng = recs[i][1]
                    st = max(self.free[eng], dep)
                    du = dur[i]
                    if eng == "act" and tset[i] is not None and self.tset is not None and tset[i] != self.tset:
                        st += 1300.0
                key = (round(st / 150.0), -prio[i], i)
                if bkey is None or key < bkey:
                    best, bkey, bst, beng, bdu = i, key, st, eng, du
            i = best
            eng = beng
            if recs[i][0] == "flex":
                choice[i] = eng
                lat[i] = bdu
            if eng == "act" and tset[i] is not None:
                self.tset = tset[i]
            self.free[eng] = bst + bdu
            fin[i] = bst + lat[i]
            order.append(i)
            ready.remove(i)
            for j in succs[i]:
                npred[j] -= 1
                if npred[j] == 0:
                    ready.append(j)
        assert len(order) == n
        for i, r in enumerate(recs):
            for k in r[3]:
                self.rfin[k] = max(self.rfin.get(k, 0.0), fin[i])
            for k in r[4]:
                self.wfin[k] = fin[i]
                self.rfin[k] = 0.0
        return order, choice


class SBAlloc:
    def __init__(self, nc, cache):
        self.nc = nc
        self.cache = cache
        self.cur = (nc.sbuf_base + 63) // 64 * 64
        self.top = nc.sbuf_top
        self.n = 0

    def alloc(self, name, shape, dtype):
        isz = 2 if dtype == BF16 else 4
        size = isz
        for s in shape[1:]:
            size *= s
        size = (size + 63) // 64 * 64
        assert self.cur + size <= self.top, (name, self.cur, size, self.top)
        self.n += 1
        k = f"sb_{name}_{self.n}"
        if k not in self.cache:
            self.cache[k] = self.nc.alloc_sbuf_tensor_at(f"{name}_{self.n}", list(shape), dtype, offset=self.cur)
        self.cur += size
        return self.cache[k]


def build_nc():
    nc = bass.Bass("TRN2", target_bir_lowering=False)
    dt_in = lambda n, s: nc.dram_tensor(n, list(s), F32, kind="ExternalInput").ap()
    dt_out = lambda n, s: nc.dram_tensor(n, list(s), F32, kind="ExternalOutput").ap()
    xT_d = dt_in("xT", (D, TT))
    xtok_d = dt_in("xtok", (TT, D))
    sg_d = dt_in("sg", (NS, 4, 128, 256))
    sh_d = dt_in("sh", (NS, 8, 128, 128))
    win_d = dt_in("w_in", (D, IN_DIM))
    wlr_d = dt_in("wlr", (16, 512))
    bgl_d = dt_in("bgl", (128, 4))
    glag_d = dt_in("glag", (128, 1024))
    lbp_d = dt_in("lbp", (128, 16))
    hgg_d = dt_in("hgg", (128, 1024))
    wbg_d = dt_in("wbg", (D, D))
    wbh_d = dt_in("wbh", (D, D))
    wout_d = dt_in("wout", (D, D))
    lng_d = dt_in("lng", (128, D))
    lnb_d = dt_in("lnb", (128, D))
    y_d = dt_out("y", (TT, D))
    gp_d = dt_out("gp", (4, 128, 256))
    hp_d = dt_out("hp", (8, 128, 128))
    gs_d = dt_out("gs", (NS, 4, 128, 256))
    hs_d = dt_out("hs", (NS, 8, 128, 128))

    PA = [nc.alloc_psum_tensor(f"PA{i}", [128, 512], F32) for i in range(2)]
    PB = [nc.alloc_psum_tensor(f"PB{i}", [128, 512], F32) for i in range(2)]
    PD = [nc.alloc_psum_tensor(f"PD{i}", [128, 512], F32) for i in range(2)]
    PX = [nc.alloc_psum_tensor(f"PX{i}", [128, 512], F32) for i in range(2)]
    cache = {}

    def program(me, eobj):
        S = Sched(nc, cache, me, eobj)
        sb = SBAlloc(nc, cache)
        win_r = win_d.rearrange("(c p) n -> p c n", p=128)

        xT = sb.alloc("xT", [128, 8, TT], BF16)
        onT = sb.alloc("onT", [128, 16, TT], BF16)
        ident_f = sb.alloc("identf", [128, 128], F32)
        ident_b = sb.alloc("identb", [128, 128], BF16)
        U4 = sb.alloc("U4", [128, 4, 128], F32)
        msk = sb.alloc("msk", [128, 4, 128], F32)
        idrow = sb.alloc("idrow", [128, 16, 16], F32)
        negb = sb.alloc("negb", [128, 4], F32)
        lbp = sb.alloc("lbp", [128, 16], F32)
        c1 = sb.alloc("c1", [128, 8], F32)
        nc1 = sb.alloc("nc1", [128, 8], F32)
        c2 = sb.alloc("c2", [128, 8], F32)
        region_mark = sb.cur
        wga = sb.alloc("wga", [128, 8, 16], BF16)
        wlr = sb.alloc("wlr", [16, 512], BF16)
        gaT = sb.alloc("gaT", [16, TT], BF16)

        rot = {"A": 0, "B": 0, "D": 0, "X": 0, "K": 0, "O": 0}

        def nxt(k):
            rot[k] ^= 1
            return rot[k]

        S.op("pool", lambda e: e.memset(ident_f[:], 1.0), W=["identf"])
        S.op("pool", lambda e: e.affine_select(out=ident_f[:], in_=ident_f[:], pattern=[[1, 128]],
                                               compare_op=ALU.is_equal, fill=0.0, base=0,
                                               channel_multiplier=-1), R=["identf"], W=["identf"])
        S.op("pool", lambda e: e.memset(U4[:], 1.0), W=["U4"])
        S.op("pool", lambda e: e.affine_select(out=U4[:], in_=U4[:], pattern=[[0, 4], [1, 128]],
                                               compare_op=ALU.is_ge, fill=0.0, base=0,
                                               channel_multiplier=-1), R=["U4"], W=["U4"])
        S.op("pool", lambda e: e.memset(msk[:], 1.0), W=["msk"])
        S.op("pool", lambda e: e.memset(msk[:, :, 0:1], 0.0), R=["msk"], W=["msk"])
        S.op("pool", lambda e: e.memset(idrow[:], 1.0), W=["idrow"])
        S.op("pool", lambda e: e.affine_select(out=idrow[:], in_=idrow[:], pattern=[[1, 16], [-1, 16]],
                                               compare_op=ALU.is_equal, fill=0.0, base=0,
                                               channel_multiplier=0), R=["idrow"], W=["idrow"])
        S.op("dve", lambda e: e.tensor_copy(ident_b[:], ident_f[:]), R=["identf"], W=["identb"])

        S.dma("sp", negb[:], bgl_d, "ld_negb", W=["negb"])
        S.dma("sp", lbp[:], lbp_d, "ld_lbp", W=["lbp"])
        S.op("dve", lambda e: e.tensor_scalar(negb[:], negb[:], -1.0, None, op0=ALU.mult), R=["negb"], W=["negb"])
        S.op("dve", lambda e: e.tensor_tensor(c2[:], lbp[:, 0:8], lbp[:, 8:16], op=ALU.subtract), R=["lbp"], W=["c2"])
        S.op("act", lambda e: e.activation(out=c2[:], in_=c2[:], func=AF.Tanh, scale=0.5), R=["c2"], W=["c2"])
        S.op("dve", lambda e: e.tensor_scalar(c1[:], c2[:], -0.25, 0.25, op0=ALU.mult, op1=ALU.add), R=["c2"], W=["c1"])
        S.op("dve", lambda e: e.tensor_scalar(nc1[:], c2[:], 0.25, -0.25, op0=ALU.mult, op1=ALU.add), R=["c2"], W=["nc1"])
        S.op("dve", lambda e: e.tensor_scalar(c2[:], c2[:], 0.25, 0.75, op0=ALU.mult, op1=ALU.add), R=["c2", "c1", "nc1"], W=["c2"])

        S.dma("pool", wga[:], win_r[:, :, GA_OFF:GA_OFF + 16], "ld_wga", W=["wga"])
        S.dma("pool", wlr[:], wlr_d, "ld_wlr", W=["wlr"])
        xT_r = xT_d.rearrange("(c p) n -> p c n", p=128)
        def XK(t0):
            return [f"xT{c}_{min(t0 // BLK, NBLK)}" for c in range(8)]

        def load_xT(bi):
            c0 = bi * BLK
            n = BLK if bi < NBLK else NS
            for c in range(8):
                S.dma("pool", xT[:, c, c0:c0 + n], xT_r[:, c, c0:c0 + n], f"ld_xT_{bi}", W=[f"xT{c}_{bi}"])
        load_xT(0)

        wu = [sb.alloc(f"wu{i}", [128, 8, 1024], BF16) for i in range(2)]
        g_u = [sb.alloc(f"g_u{i}", [128, 256], F32) for i in range(2)]
        GT = 1
        NSL = 6
        S0b = [sb.alloc(f"S0b{i}", [128, GT, 256], F32) for i in range(NSL)]
        S0bf = [sb.alloc(f"S0bf{i}", [128, 256], BF16) for i in range(2)]
        th = [sb.alloc(f"th_{i}", [128, 512], F32) for i in range(2)]
        sq = [sb.alloc(f"sq_{i}", [128, 512], F32) for i in range(2)]
        g1 = sb.alloc("g1", [128, 4, 128], F32)
        Eb = sb.alloc("Eb", [128, 4, 128], F32)
        keT = sb.alloc("keT", [128, 4, 128], BF16)
        kdT = sb.alloc("kdT", [128, 4, 128], BF16)
        qeT = [[sb.alloc(f"qeT_{p}{i}", [128, 512], BF16) for i in range(2)] for p in range(2)]
        kd = [[sb.alloc(f"kd_{p}{i}", [128, 512], BF16) for i in range(2)] for p in range(2)]
        ATb = [[sb.alloc(f"ATb_{p}{i}", [128, 4, 128], BF16) for i in range(2)] for p in range(2)]
        EbL = [[sb.alloc(f"EbL_{p}{i}", [128, 4], F32) for i in range(2)] for p in range(2)]
        vbf = [sb.alloc(f"vbf{p}", [128, 4, 256], BF16) for p in range(3)]
        ug = [sb.alloc(f"ug{p}", [128, 4, 256], F32) for p in range(3)]
        Sst = sb.alloc("Sst", [128, 256], F32)
        Sbf = [sb.alloc(f"Sbf{i}", [128, 4, 256], BF16) for i in range(2)]
        onb = [sb.alloc("onb0", [128, 4, 256], BF16), sb.alloc("onb1", [128, 4, 128], BF16)]
        junk = sb.alloc("junk", [128, 256], BF16)
        ssq = sb.alloc("ssq", [128, 8], F32)
        rstd = sb.alloc("rstd", [128, 8], F32)
        eps_t = sb.alloc("eps_t", [128, 1], F32)
        one_t = sb.alloc("one_t", [128, 1], F32)
        s_e = sb.alloc("s_e", [128, 2, NS], F32)
        s_g = sb.alloc("s_g", [128, 2, NS], F32)
        s_q = sb.alloc("s_q", [128, 2, NS], F32)
        s_qb = sb.alloc("s_qb", [128, 2, NS], BF16)
        s_qe = sb.alloc("s_qe", [128, 2, NS], F32)
        s_k = sb.alloc("s_k", [128, 2, NS], F32)
        s_kb = sb.alloc("s_kb", [128, 2, NS], BF16)
        ktok = sb.alloc("ktok", [16, 2, 128], F32)
        Ks = [sb.alloc(f"Ks{i}", [16, 128], BF16) for i in range(2)]
        Qsel = sb.alloc("Qsel", [128, 2, NS, NS], BF16)
        qkd = sb.alloc("qkd", [16, 2, NS], BF16)
        s_v = sb.alloc("s_v", [16, 256], BF16)
        s_u = sb.alloc("s_u", [16, 256], F32)
        s_on = sb.alloc("s_on", [16, 256], BF16)
        S.op("dve", lambda e: e.memset(eps_t[:], EPS), W=["eps_t"])
        S.op("dve", lambda e: e.memset(one_t[:], 1.0), W=["one_t"])

        def load_unit_weights(u, slot):
            w = wu[slot]
            key = f"wu{slot}"
            if u < 4:
                h = u
                segs = [(0, h * 128, 128), (128, 512 + h * 128, 128), (256, 1024 + h * 256, 256),
                        (512, 2048 + h * 256, 256)]
            else:
                j = u - 4
                segs = [(0, HQ_OFF + j * 256, 256), (256, HF_OFF + j * 256, 256),
                        (512, HI_OFF + j * 256, 256), (768, HR_OFF + j * 256, 256)]
            for (o, c0, n) in segs:
                S.dma("pool", w[:, :, o:o + n], win_r[:, :, c0:c0 + n], f"ld_wu{slot}", W=[key])

        order = [0, 1, 2, 3, 4, 5, 6, 7]
        load_unit_weights(order[0], 0)
        for bi_ in range(1, NBLK + 1):
            load_xT(bi_)
        load_unit_weights(order[1], 1)

        for bi in range(NBLK + 1):
            t0 = bi * BLK
            n = BLK if bi < NBLK else NS
            a = nxt("A")
            def f(e, a=a, t0=t0, n=n):
                for c in range(8):
                    r = e.matmul(PA[a][0:16, 0:n], lhsT=wga[:, c, :], rhs=xT[:, c, t0:t0 + n],
                                 start=(c == 0), stop=(c == 7))
                return r
            S.op("pe", f, R=["wga"] + XK(t0), W=[f"PA{a}"])
            S.op("act", lambda e, a=a, t0=t0, n=n: e.activation(out=gaT[:, t0:t0 + n], in_=PA[a][0:16, 0:n], func=AF.Copy),
                 R=[f"PA{a}"], W=["gaT"])

        def proj_fm(slot, woff, t0, n):
            a = nxt("A")
            def f(e):
                for c in range(8):
                    r = e.matmul(PA[a][:, 0:n], lhsT=wu[slot][:, c, woff:woff + 128], rhs=xT[:, c, t0:t0 + n],
                                 start=(c == 0), stop=(c == 7))
                return r
            S.op("pe", f, R=[f"wu{slot}"] + XK(t0), W=[f"PA{a}"])
            return a

        def proj_tm(slot, woff, t0, m):
            b = nxt("B")
            def f(e):
                for c in range(8):
                    r = e.matmul(PB[b][0:m, :], lhsT=xT[:, c, t0:t0 + m], rhs=wu[slot][:, c, woff:woff + 512],
                                 start=(c == 0), stop=(c == 7))
                return r
            S.op("pe", f, R=[f"wu{slot}"] + XK(t0), W=[f"PB{b}"])
            return b

        def rstd_from(ssq_ap, rstd_ap, m, dv, keys_r, keys_w):
            S.op("act", lambda e: e.activation(out=rstd_ap, in_=ssq_ap, func=AF.Ln, scale=1.0 / dv, bias=eps_t[0:m, :]),
                 R=keys_r, W=keys_w)
            S.op("act", lambda e: e.activation(out=rstd_ap, in_=rstd_ap, func=AF.Exp, scale=-0.5),
                 R=keys_w, W=keys_w)

        class Item:
            pass

        items = []
        for ui, u in enumerate(order):
            for bi in list(range(NBLK)) + ["s"]:
                it = Item()
                it.ui, it.u, it.slot, it.bi = ui, u, ui % 2, bi
                it.p = len(items) % 2
                it.q3 = len(items) % 3
                it.gla = u < 4
                it.nh = 1 if it.gla else 2
                it.DV = 256 if it.gla else 128
                it.vr_off = 256 if it.gla else 512
                it.vc0 = 2 * u if it.gla else 8 + 2 * (u - 4)
                it.gk = f"g_u{ui % 2}"
                items.append(it)

        def hd_of(it, e_):
            return 2 * (it.u - 4) + e_

        def vsl_of(it, e_):
            return slice(0, 256) if it.gla else slice(e_ * 128, (e_ + 1) * 128)

        def alpha1(it):
            slot, q3 = it.slot, it.q3
            gu = g_u[it.ui % 2]
            if it.bi == 0:
                src = glag_d[:, it.u * 256:(it.u + 1) * 256] if it.gla else hgg_d[:, (it.u - 4) * 256:(it.u - 3) * 256]
                S.dma("sp", gu[:], src, f"ld_gu{it.ui % 2}", W=[it.gk])
            if it.bi == "s":
                b = proj_tm(slot, it.vr_off, T, NS)
                S.op("act", lambda e: e.activation(out=s_v[:], in_=PB[b][0:NS, 0:256], func=AF.Copy), R=[f"PB{b}"], W=["s_v"])
                S.op("act", lambda e: e.activation(out=s_u[:], in_=PB[b][0:NS, 256:512], func=AF.Copy), R=[f"PB{b}"], W=["s_u"])
                if not it.gla:
                    for e_ in range(2):
                        a = proj_fm(slot, e_ * 128, T, NS)
                        S.op("act", lambda e, a=a, e_=e_: e.activation(out=s_q[:, e_, :], in_=PA[a][:, 0:NS], func=AF.Copy),
                             R=[f"PA{a}"], W=["s_q"])
                        a = proj_fm(slot, 256 + e_ * 128, T, NS)
                        S.op("dve", lambda e, a=a, e_=e_: e.tensor_copy(s_k[:, e_, :], PA[a][:, 0:NS]),
                             R=[f"PA{a}"], W=["s_k"])
                for g in range(NSL):
                    load_S0(it, g)
                yield
                return
            t0 = it.bi * BLK
            for i in range(4):
                b = proj_tm(slot, it.vr_off, t0 + i * 128, 128)
                S.copy(vbf[q3][:, i, :], PB[b][:, 0:256], R=[f"PB{b}"], W=[f"vbf{q3}_{i}"])
                S.copy(ug[q3][:, i, :], PB[b][:, 256:512], R=[f"PB{b}"], W=[f"ug{q3}_{i}"])
                yield
            yield "TILES_DONE"
            if not it.gla:
                yield "WAIT_BETA"
                for e_ in range(2):
                    a = proj_fm(slot, e_ * 128, t0, BLK)
                    S.copy(sq[e_][:], PA[a][:], R=[f"PA{a}"], W=[f"sq_{e_}"])
                    yield
                    a = proj_fm(slot, 256 + e_ * 128, t0, BLK)
                    S.copy(th[e_][:], PA[a][:], R=[f"PA{a}"], W=[f"th_{e_}"])
                    yield

        def alpha2(it):
            q3 = it.q3
            gu = g_u[it.ui % 2]
            if it.bi == "s":
                S.op("act", lambda e: e.activation(out=s_u[:], in_=s_u[:], func=AF.Silu), R=["s_u"], W=["s_u"])
                S.op("pool", lambda e: e.tensor_tensor(s_u[:], s_u[:], gu[0:NS, :], op=ALU.mult), R=["s_u", it.gk], W=["s_u"])
                if not it.gla:
                    S.op("act", lambda e: e.activation(out=s_q[:], in_=s_q[:], func=AF.Silu), R=["s_q"], W=["s_q"])
                    S.op("act", lambda e: e.activation(out=s_k[:], in_=s_k[:], func=AF.Tanh, scale=0.5), R=["s_k"], W=["s_k"])
                return
            ugk = [f"ug{q3}_{i}" for i in range(4)]
            S.op("act", lambda e: e.activation(out=ug[q3][:], in_=ug[q3][:], func=AF.Silu), R=ugk, W=ugk)
            for i in range(4):
                S.op("pool", lambda e, i=i: e.tensor_tensor(ug[q3][:, i, :], ug[q3][:, i, :], gu[:], op=ALU.mult),
                     R=[f"ug{q3}_{i}", it.gk], W=[f"ug{q3}_{i}"])
            if not it.gla:
                for e_ in range(2):
                    S.op("act", lambda e, e_=e_: e.activation(out=sq[e_][:], in_=sq[e_][:], func=AF.Silu),
                         R=[f"sq_{e_}"], W=[f"sq_{e_}"])
                    S.op("act", lambda e, e_=e_: e.activation(out=th[e_][:], in_=th[e_][:], func=AF.Tanh, scale=0.5),
                         R=[f"th_{e_}"], W=[f"th_{e_}"])

        def load_S0(it, g):
            sl = g % NSL
            n0 = g * GT
            if it.gla:
                S.dma("sp", S0b[sl][:], sg_d[n0:n0 + GT, it.u].rearrange("n k v -> k n v"), f"ld_S0{sl}", W=[f"S0b{sl}"])
            else:
                j = it.u - 4
                for hh in range(2):
                    S.dma("sp", S0b[sl][:, :, hh * 128:(hh + 1) * 128],
                          sh_d[n0:n0 + GT, 2 * j + hh].rearrange("n k v -> k n v"), f"ld_S0{sl}", W=[f"S0b{sl}"])

        def beta(it):
            slot, p = it.slot, it.p
            g1f = g1[:].rearrange("p c t -> p (c t)")
            Ebf = Eb[:].rearrange("p c t -> p (c t)")
            keTf = keT[:].rearrange("p c t -> p (c t)")
            mskf = msk[:].rearrange("p c t -> p (c t)")
            if it.bi == "s":
                nh = it.nh
                for e_ in range(nh):
                    if it.gla:
                        h = it.u
                        a = nxt("A")
                        S.op("pe", lambda e, a=a, h=h: e.matmul(PA[a][:, 0:NS], lhsT=wlr[:, h * 128:(h + 1) * 128],
                                                              rhs=gaT[:, T:T + NS], start=True, stop=True),
                             R=["wlr", "gaT"], W=[f"PA{a}"])
                        S.op("act", lambda e, a=a, h=h, e_=e_: e.activation(out=s_g[:, e_, :], in_=PA[a][:, 0:NS], func=AF.Exp,
                                                                          scale=-1.0, bias=negb[:, h:h + 1]),
                             R=[f"PA{a}", "negb"], W=["s_g"])
                        S.op("act", lambda e, e_=e_: e.activation(out=s_g[:, e_, :], in_=s_g[:, e_, :], func=AF.Ln, scale=1.0,
                                                                  bias=one_t[:]), R=["s_g", "one_t"], W=["s_g"])
                        sE = -1.0 / 16.0
                        a = proj_fm(slot, 128, T, NS)
                        S.op("act", lambda e, a=a, e_=e_: e.activation(out=s_k[:, e_, :], in_=PA[a][:, 0:NS], func=AF.Copy),
                             R=[f"PA{a}"], W=["s_k"])
                        a = proj_fm(slot, 0, T, NS)
                        S.op("act", lambda e, a=a, e_=e_: e.activation(out=s_q[:, e_, :], in_=PA[a][:, 0:NS], func=AF.Identity,
                                                                      scale=128.0 ** -0.5), R=[f"PA{a}"], W=["s_q"])
                    else:
                        hd = hd_of(it, e_)
                        S.op("act", lambda e, e_=e_, hd=hd: e.activation(out=s_g[:, e_, :], in_=s_k[:, e_, :], func=AF.Ln,
                                                                        scale=c1[:, hd:hd + 1], bias=c2[:, hd:hd + 1]),
                             R=["s_k", "c1", "c2"], W=["s_g"])
                        S.op("dve", lambda e, e_=e_, hd=hd: e.tensor_scalar(s_k[:, e_, :], s_k[:, e_, :], nc1[:, hd:hd + 1],
                                                                           c1[:, hd:hd + 1], op0=ALU.mult, op1=ALU.add),
                             R=["s_k", "c1", "nc1", "s_g"], W=["s_k"])
                        sE = 1.0
                    S.op("act", lambda e, e_=e_, sE=sE: e.activation(out=s_e[:, e_, :], in_=s_g[:, e_, :], func=AF.Exp, scale=sE),
                         R=["s_g"], W=["s_e"])
                    yield
                S.op("dve", lambda e: e.tensor_copy(s_kb[:, 0:nh, :], s_k[:, 0:nh, :]), R=["s_k"], W=["s_kb"])
                S.op("dve", lambda e: e.tensor_copy(s_qb[:, 0:nh, :], s_q[:, 0:nh, :]), R=["s_q"], W=["s_qb"])
                S.op("dve", lambda e: e.tensor_tensor(s_qe[:, 0:nh, :], s_q[:, 0:nh, :], s_e[:, 0:nh, :], op=ALU.mult),
                     R=["s_q", "s_e"], W=["s_qe"])
                S.op("dve", lambda e: e.tensor_tensor(
                    Qsel[:, 0:nh, :, :], s_qe[:, 0:nh, :].unsqueeze(3).broadcast_to([128, nh, NS, NS]),
                    idrow[:].unsqueeze(1).broadcast_to([128, nh, NS, NS]), op=ALU.mult), R=["s_qe", "idrow"], W=["Qsel"])
                yield
                for e_ in range(nh):
                    x = nxt("X")
                    S.op("pe", lambda e, e_=e_, x=x: e.matmul(PX[x][0:NS, 0:128], lhsT=s_kb[:, e_, :], rhs=ident_b[:],
                                                            start=True, stop=True), R=["s_kb", "identb"], W=[f"PX{x}"])
                    S.op("dve", lambda e, e_=e_, x=x: e.tensor_copy(ktok[:, e_, :], PX[x][0:NS, 0:128]),
                         R=[f"PX{x}"], W=["ktok"])
                    x = nxt("X")
                    S.op("pe", lambda e, e_=e_, x=x: e.matmul(PX[x][0:NS, 0:NS], lhsT=s_qb[:, e_, :], rhs=s_kb[:, e_, :],
                                                            start=True, stop=True), R=["s_kb", "s_qb"], W=[f"PX{x}"])
                    S.op("dve", lambda e, e_=e_, x=x: e.tensor_tensor(qkd[:, e_, :], PX[x][0:NS, 0:NS], ident_f[0:NS, 0:NS],
                                                                    op=ALU.mult), R=[f"PX{x}", "identf"], W=["qkd"])
                    yield
                return
            t0 = it.bi * BLK
            for e_ in range(it.nh):
                if it.gla:
                    h = it.u
                    a = nxt("A")
                    S.op("pe", lambda e, a=a, h=h: e.matmul(PA[a][:], lhsT=wlr[:, h * 128:(h + 1) * 128],
                                                          rhs=gaT[:, t0:t0 + BLK], start=True, stop=True),
                         R=["wlr", "gaT"], W=[f"PA{a}"])
                    S.op("act", lambda e, a=a, h=h: e.activation(out=g1f, in_=PA[a][:], func=AF.Exp, scale=-1.0,
                                                               bias=negb[:, h:h + 1]), R=[f"PA{a}", "negb"], W=["g1"])
                    S.op("act", lambda e: e.activation(out=g1f, in_=g1f, func=AF.Ln, scale=1.0, bias=one_t[:]),
                         R=["g1", "one_t"], W=["g1"])
                    sE = -1.0 / 16.0
                else:
                    hd = hd_of(it, e_)
                    S.op("act", lambda e, e_=e_, hd=hd: e.activation(out=g1f, in_=th[e_][:], func=AF.Ln, scale=c1[:, hd:hd + 1],
                                                                    bias=c2[:, hd:hd + 1]), R=[f"th_{e_}", "c1", "c2"], W=["g1"])
                    S.op("dve", lambda e, e_=e_, hd=hd: e.tensor_scalar(th[e_][:], th[e_][:], nc1[:, hd:hd + 1], c1[:, hd:hd + 1],
                                                                       op0=ALU.mult, op1=ALU.add),
                         R=[f"th_{e_}", "c1", "nc1", "g1"], W=[f"th_{e_}"])
                    sE = 1.0
                yield
                S.op("dve", lambda e: e.tensor_tensor_scan(g1f, mskf, g1f, 0.0, op0=ALU.mult, op1=ALU.add),
                     R=["g1", "msk"], W=["g1"])
                S.op("act", lambda e, sE=sE: e.activation(out=Ebf, in_=g1f, func=AF.Exp, scale=sE), R=["g1"], W=["Eb"])
                S.op("act", lambda e, sE=sE: e.activation(out=g1f, in_=g1f, func=AF.Exp, scale=-sE), R=["g1"], W=["g1"])
                yield
                if it.gla:
                    a = proj_fm(slot, 128, t0, BLK)
                    S.op("dve", lambda e, a=a: e.tensor_tensor(keTf, PA[a][:], g1f, op=ALU.mult),
                         R=[f"PA{a}", "g1"], W=["keT"])
                    a = proj_fm(slot, 0, t0, BLK)
                    S.op("dve", lambda e, a=a, e_=e_: e.scalar_tensor_tensor(
                        qeT[p][e_][:], PA[a][:], 128.0 ** -0.5, Ebf, op0=ALU.mult, op1=ALU.mult),
                         R=[f"PA{a}", "Eb"], W=[f"qeT_{p}{e_}"])
                else:
                    S.op("dve", lambda e, e_=e_: e.tensor_tensor(keTf, th[e_][:], g1f, op=ALU.mult),
                         R=[f"th_{e_}", "g1"], W=["keT"])
                    S.op("pool", lambda e, e_=e_: e.tensor_tensor(qeT[p][e_][:], sq[e_][:], Ebf, op=ALU.mult),
                         R=[f"sq_{e_}", "Eb"], W=[f"qeT_{p}{e_}"])
                yield
                S.op("pool", lambda e: e.tensor_tensor(kdT[:], keT[:], Eb[:, :, 127:128].broadcast_to([128, 4, 128]), op=ALU.mult),
                     R=["keT", "Eb"], W=["kdT"])
                S.op("pool", lambda e, e_=e_: e.tensor_copy(EbL[p][e_][:], Eb[:, :, 127]), R=["Eb"], W=[f"EbL_{p}{e_}"])
                a = nxt("A")
                def f(e, e_=e_, a=a):
                    for cc in range(4):
                        r = e.matmul(PA[a][:, cc * 128:(cc + 1) * 128], lhsT=keT[:, cc, :],
                                     rhs=qeT[p][e_][:, cc * 128:(cc + 1) * 128], start=True, stop=True)
                    return r
                S.op("pe", f, R=["keT", f"qeT_{p}{e_}"], W=[f"PA{a}"])
                S.op("dve", lambda e, e_=e_, a=a: e.tensor_tensor(ATb[p][e_][:].rearrange("p c t -> p (c t)"), PA[a][:],
                                                                 U4[:].rearrange("p c t -> p (c t)"), op=ALU.mult),
                     R=[f"PA{a}", "U4"], W=[f"ATb_{p}{e_}"])
                yield
                x = nxt("X")
                def f(e, x=x):
                    for cc in range(4):
                        r = e.matmul(PX[x][:, cc * 128:(cc + 1) * 128], lhsT=kdT[:, cc, :], rhs=ident_b[:], start=True, stop=True)
                    return r
                S.op("pe", f, R=["kdT", "identb"], W=[f"PX{x}"])
                S.copy(kd[p][e_][:], PX[x][:], R=[f"PX{x}"], W=[f"kd_{p}{e_}"])
                yield

        def stage2(it):
            slot, p, nh, DV, q3 = it.slot, it.p, it.nh, it.DV, it.q3
            if it.bi == "s":
                for n in range(NS):
                    sl = n % NSL
                    bsl = n % 2
                    S.copy(S0bf[bsl][:], S0b[sl][:, 0, :], R=[f"S0b{sl}"], W=[f"S0bf{bsl}"])
                    for e_ in range(nh):
                        vsl = vsl_of(it, e_)
                        d = e_
                        S.op("pe", lambda e, e_=e_, n=n, bsl=bsl, vsl=vsl, d=d: e.matmul(
                            PD[d][0:NS, 0:DV], lhsT=Qsel[:, e_, n, :], rhs=S0bf[bsl][:, vsl], start=(n == 0), stop=False),
                             R=["Qsel", f"S0bf{bsl}"], W=[f"PD{d}"])
                        kr = nxt("K")
                        S.op("dve", lambda e, e_=e_, n=n, kr=kr: e.tensor_scalar(
                            Ks[kr][:], ktok[:, e_, :], ident_f[0:NS, n:n + 1], None, op0=ALU.mult),
                             R=["ktok", "identf"], W=[f"Ks{kr}"])
                        x = nxt("X")
                        S.op("pe", lambda e, kr=kr, vsl=vsl, x=x: e.matmul(
                            PX[x][:, 0:DV], lhsT=Ks[kr][:], rhs=s_v[:, vsl], start=True, stop=True),
                             R=[f"Ks{kr}", "s_v"], W=[f"PX{x}"])
                        S.op("dve", lambda e, e_=e_, n=n, sl=sl, vsl=vsl, x=x: e.scalar_tensor_tensor(
                            S0b[sl][:, 0, vsl], S0b[sl][:, 0, vsl], s_e[:, e_, n:n + 1], PX[x][:, 0:DV],
                            op0=ALU.mult, op1=ALU.add), R=[f"PX{x}", f"S0b{sl}", "s_e"], W=[f"S0b{sl}"])
                    if it.gla:
                        S.dma("pool", gs_d[n:n + 1, it.u].rearrange("n k v -> k n v"), S0b[sl][:], f"st_Sn{sl}", R=[f"S0b{sl}"])
                    else:
                        j = it.u - 4
                        for hh in range(2):
                            S.dma("pool", hs_d[n:n + 1, 2 * j + hh].rearrange("n k v -> k n v"),
                                  S0b[sl][:, :, hh * 128:(hh + 1) * 128], f"st_Sn{sl}", R=[f"S0b{sl}"])
                    if n + NSL < NS:
                        load_S0(it, n + NSL)
                    yield
                for e_ in range(nh):
                    vsl = vsl_of(it, e_)
                    d = e_
                    S.op("pe", lambda e, e_=e_, vsl=vsl, d=d: e.matmul(PD[d][0:NS, 0:DV], lhsT=qkd[:, e_, :], rhs=s_v[:, vsl],
                                                                     start=False, stop=True),
                         R=["qkd", "s_v"], W=[f"PD{d}"])
                    col = e_
                    S.op("act", lambda e, d=d, col=col: e.activation(out=junk[0:NS, 0:DV], in_=PD[d][0:NS, 0:DV], func=AF.Square,
                                                                     accum_out=ssq[0:NS, col:col + 1]),
                         R=[f"PD{d}"], W=["junk", f"ssq{col}"])
                    rstd_from(ssq[0:NS, col:col + 1], rstd[0:NS, col:col + 1], NS, DV, [f"ssq{col}", "eps_t"], [f"rstd{col}"])
                    S.op("dve", lambda e, d=d, col=col, vsl=vsl: e.scalar_tensor_tensor(
                        s_on[:, vsl], PD[d][0:NS, 0:DV], rstd[0:NS, col:col + 1], s_u[:, vsl], op0=ALU.mult, op1=ALU.mult),
                         R=[f"PD{d}", f"rstd{col}", "s_u"], W=["s_on"])
                x = nxt("X")
                def f(e, x=x):
                    for jj in range(2):
                        r = e.matmul(PX[x][:, jj * NS:(jj + 1) * NS], lhsT=s_on[:, jj * 128:(jj + 1) * 128],
                                     rhs=ident_b[0:NS, 0:NS], start=True, stop=True)
                    return r
                S.op("pe", f, R=["s_on", "identb"], W=[f"PX{x}"])
                vc0 = it.vc0
                S.op("act", lambda e, x=x, vc0=vc0: e.activation(
                    out=onT[:, vc0:vc0 + 2, T:T + NS], in_=PX[x][:, 0:2 * NS].rearrange("p (j t) -> p j t", t=NS),
                    func=AF.Copy), R=[f"PX{x}"], W=[f"onT{vc0}_s"])
                yield
                return
            bi = it.bi
            t0 = bi * BLK
            pb = bi % 2
            for e_ in range(nh):
                vsl = vsl_of(it, e_)
                sks = ["Sst0", "Sst1"] if it.gla else [f"Sst{e_}"]
                sbk = (lambda q: [f"Sbf{q}_0", f"Sbf{q}_1"]) if it.gla else (lambda q, e_=e_: [f"Sbf{q}_{e_}"])
                Sv = Sst[:, vsl]
                cpb = 512 // DV
                nbk = 4 // cpb
                xb = [nxt("X") for _ in range(nbk)]
                for bk in range(nbk):
                    def f(e, e_=e_, bk=bk, vsl=vsl):
                        for j in range(cpb):
                            cc = bk * cpb + j
                            r = e.matmul(PX[xb[bk]][:, j * DV:(j + 1) * DV], lhsT=kd[p][e_][:, cc * 128:(cc + 1) * 128],
                                         rhs=vbf[q3][:, cc, vsl], start=True, stop=True)
                        return r
                    S.op("pe", f, R=[f"kd_{p}{e_}"] + [f"vbf{q3}_{bk * cpb + j}" for j in range(cpb)], W=[f"PX{xb[bk]}"])
                for cc in range(4):
                    gc = bi * 4 + cc
                    bk, j = cc // cpb, cc % cpb
                    usl = PX[xb[bk]][:, j * DV:(j + 1) * DV]
                    if gc == 0:
                        S.op("dve", lambda e, usl=usl: e.tensor_copy(Sv, usl), R=[f"PX{xb[bk]}"], W=sks)
                    else:
                        S.op("dve", lambda e, usl=usl, cc=cc: e.scalar_tensor_tensor(
                            Sv, Sv, EbL[p][e_][:, cc:cc + 1], usl, op0=ALU.mult, op1=ALU.add),
                             R=[f"PX{xb[bk]}", f"EbL_{p}{e_}"] + sks, W=sks)
                    S.copy(Sbf[pb][:, cc, vsl], Sv, R=sks, W=sbk(pb))
                yield
                db = [nxt("D") for _ in range(nbk)]
                for bk in range(nbk):
                    def f(e, e_=e_, bk=bk, vsl=vsl):
                        for j in range(cpb):
                            cc = bk * cpb + j
                            gc = bi * 4 + cc
                            osl = PD[db[bk]][:, j * DV:(j + 1) * DV]
                            r = e.matmul(osl, lhsT=ATb[p][e_][:, cc, :], rhs=vbf[q3][:, cc, vsl], start=True, stop=(gc == 0))
                            if gc > 0:
                                prev = Sbf[1 - pb][:, 3, vsl] if cc == 0 else Sbf[pb][:, cc - 1, vsl]
                                r = e.matmul(osl, lhsT=qeT[p][e_][:, cc * 128:(cc + 1) * 128], rhs=prev, start=False, stop=True)
                        return r
                    S.op("pe", f, R=[f"ATb_{p}{e_}", f"qeT_{p}{e_}"] + sbk(pb) + sbk(1 - pb) +
                         [f"vbf{q3}_{bk * cpb + j}" for j in range(cpb)], W=[f"PD{db[bk]}"])
                yield
                for cc in range(4):
                    bk, j = cc // cpb, cc % cpb
                    col = e_ * 4 + cc
                    S.op("act", lambda e, bk=bk, j=j, col=col: e.activation(
                        out=junk[:, 0:DV], in_=PD[db[bk]][:, j * DV:(j + 1) * DV], func=AF.Square, accum_out=ssq[:, col:col + 1]),
                         R=[f"PD{db[bk]}"], W=["junk", f"ssq{e_}"])
                rstd_from(ssq[:, e_ * 4:e_ * 4 + 4], rstd[:, e_ * 4:e_ * 4 + 4], 128, DV, [f"ssq{e_}", "eps_t"], [f"rstd{e_}"])
                for cc in range(4):
                    bk, j = cc // cpb, cc % cpb
                    col = e_ * 4 + cc
                    S.op("dve", lambda e, bk=bk, j=j, col=col, cc=cc: e.scalar_tensor_tensor(
                        onb[e_][:, cc, 0:DV], PD[db[bk]][:, j * DV:(j + 1) * DV], rstd[:, col:col + 1], ug[q3][:, cc, vsl],
                        op0=ALU.mult, op1=ALU.mult),
                         R=[f"PD{db[bk]}", f"rstd{e_}", f"ug{q3}_{cc}"], W=[f"onb{e_}"])
                yield
                nv = DV // 128
                for jj in range(nv):
                    x2 = nxt("X")
                    def f(e, e_=e_, jj=jj, x2=x2):
                        for cc in range(4):
                            r = e.matmul(PX[x2][:, cc * 128:(cc + 1) * 128], lhsT=onb[e_][:, cc, jj * 128:(jj + 1) * 128],
                                         rhs=ident_b[:], start=True, stop=True)
                        return r
                    S.op("pe", f, R=[f"onb{e_}", "identb"], W=[f"PX{x2}"])
                    vc = it.vc0 + (jj if it.gla else e_)
                    S.copy(onT[:, vc, t0:t0 + BLK], PX[x2][:], R=[f"PX{x2}"], W=[f"onT{vc}_{bi}"])
                yield
            if bi == NBLK - 1:
                for e_ in range(nh):
                    if it.gla:
                        S.dma("sp", gp_d[it.u], Sst[:], "st_gp", R=["Sst0", "Sst1"])
                    else:
                        S.dma("sp", hp_d[hd_of(it, e_)], Sst[:, e_ * 128:(e_ + 1) * 128], f"st_hp{e_}", R=[f"Sst{e_}"])

        def run_all(g):
            for _ in g:
                pass

        def interleave(gens, beta_idx=None):
            alive = [True] * len(gens)
            paused = [False] * len(gens)
            while any(alive):
                for k in range(len(gens)):
                    if not alive[k]:
                        continue
                    if paused[k]:
                        if beta_idx is not None and alive[beta_idx]:
                            continue
                        paused[k] = False
                    try:
                        r = next(gens[k])
                        if r == "WAIT_BETA":
                            paused[k] = True
                    except StopIteration:
                        alive[k] = False

        nit = len(items)
        if "lsched" not in cache:
            cache["lsched"] = ListScheduler()
            cache["orders"] = {}
        lsched = cache["lsched"]

        def flush(tag):
            recs = S.capture
            S.capture = None
            if tag not in cache["orders"]:
                cache["orders"][tag] = lsched.schedule(recs)
            order_, choice_ = cache["orders"][tag]
            for i in order_:
                S.emit(recs[i], choice_.get(i))

        WIN = 8
        S.capture = []
        run_all(alpha1(items[0]))
        alpha2(items[0])
        run_all(beta(items[0]))
        run_all(alpha1(items[1]))
        for k, it in enumerate(items):
            if k + 1 < nit:
                alpha2(items[k + 1])
            gens = [stage2(it)]
            bidx = None
            if k + 1 < nit:
                gens.append(beta(items[k + 1]))
                bidx = 1
            if k + 2 < nit:
                ga1 = alpha1(items[k + 2])
                if items[k + 2].bi != "s":
                    for r in ga1:
                        if r == "TILES_DONE":
                            break
                gens.append(ga1)
            interleave(gens, bidx)
            if k + 1 < nit:
                nx = items[k + 1]
                if nx.bi == "s" and nx.ui + 2 < len(order):
                    load_unit_weights(order[nx.ui + 2], nx.slot)
            if k % WIN == WIN - 1 or k == nit - 1:
                flush(f"step{k}")
                S.capture = []
        S.capture = None

        ONT_KEYS = [k for k in list(S.last_w.keys()) if k.startswith("onT")]
        sb.cur = region_mark + 64
        ALLU = [k for k in list(S.last_w.keys()) + list(S.readers.keys())
                if not (k.startswith("onT") or k.startswith("xT") or k in ("identf", "identb", "eps_t", "one_t"))]
        ALLU = sorted(set(ALLU))
        wo = sb.alloc("wo", [128, 8, 1024], BF16)
        sb.cur = region_mark + 64
        fw = [sb.alloc(f"fw{i}", [128, 8, 1024], BF16) for i in range(4)]
        wo_r = wout_d.rearrange("(c p) n -> p c n", p=128)
        mT = sb.alloc("mT", [128, 8, TT], BF16)
        tha = sb.alloc("tha", [128, 512], BF16)
        thb = sb.alloc("thb", [128, 512], F32)
        m1 = sb.alloc("m1", [128, 512], F32)
        f1_end = sb.cur
        assert f1_end <= sb.top
        for q in ("pool", "sp", "pe", "act", "dve"):
            S._wait(q, S._deps([], ALLU))
        srcs = [win_r[:, :, MGA_OFF:MGA_OFF + 1024], win_r[:, :, MGB_OFF:MGB_OFF + 1024],
                wbg_d.rearrange("(c p) n -> p c n", p=128), wbh_d.rearrange("(c p) n -> p c n", p=128)]
        for dc in range(8):
            for i in range(4):
                S.dma("pool", fw[i][:, :, dc * 128:(dc + 1) * 128], srcs[i][:, :, dc * 128:(dc + 1) * 128],
                      f"ld_fw{i}_{dc}", W=[f"fw{i}_{dc}"])

        S.capture = []
        for dc in range(8):
            for bi in range(NBLK + 1):
                t0 = bi * BLK
                n = BLK if bi < NBLK else NS
                onk = [k for k in ONT_KEYS if k.endswith(f"_{bi}" if bi < NBLK else "_s")]
                def mm(bank, wi, src_is_x, base):
                    def f(e):
                        for c in range(8):
                            rhs = xT[:, c, t0:t0 + n] if src_is_x else onT[:, base + c, t0:t0 + n]
                            r = e.matmul(bank[:, 0:n], lhsT=fw[wi][:, c, dc * 128:(dc + 1) * 128], rhs=rhs,
                                         start=(c == 0), stop=(c == 7))
                        return r
                    return f
                a = nxt("A")
                S.op("pe", mm(PA[a], 0, True, 0), R=[f"fw0_{dc}"] + XK(t0), W=[f"PA{a}"])
                S.op("act", lambda e, a=a: e.activation(out=tha[:, 0:n], in_=PA[a][:, 0:n], func=AF.Tanh, scale=0.5),
                     R=[f"PA{a}"], W=["tha"])
                a = nxt("A")
                S.op("pe", mm(PA[a], 1, True, 0), R=[f"fw1_{dc}"] + XK(t0), W=[f"PA{a}"])
                S.op("act", lambda e, a=a: e.activation(out=thb[:, 0:n], in_=PA[a][:, 0:n], func=AF.Tanh, scale=0.5),
                     R=[f"PA{a}"], W=["thb"])
                b = nxt("B")
                S.op("pe", mm(PB[b], 2, False, 0), R=[f"fw2_{dc}"] + onk, W=[f"PB{b}"])
                S.op("dve", lambda e, b=b: e.scalar_tensor_tensor(m1[:, 0:n], tha[:, 0:n], 1.0, PB[b][:, 0:n],
                                                                  op0=ALU.add, op1=ALU.mult), R=[f"PB{b}", "tha"], W=["m1"])
                b = nxt("B")
                S.op("pe", mm(PB[b], 3, False, 8), R=[f"fw3_{dc}"] + onk, W=[f"PB{b}"])
                S.op("dve", lambda e, b=b: e.scalar_tensor_tensor(thb[:, 0:n], thb[:, 0:n], 1.0, PB[b][:, 0:n],
                                                                  op0=ALU.add, op1=ALU.mult), R=[f"PB{b}", "thb"], W=["thb"])
                S.op("pool", lambda e, dc=dc: e.tensor_tensor(mT[:, dc, t0:t0 + n], m1[:, 0:n], thb[:, 0:n], op=ALU.add),
                     R=["m1", "thb"], W=[f"mT_{bi}"])
            S.dma("pool", wo[:, :, dc * 128:(dc + 1) * 128], wo_r[:, :, dc * 128:(dc + 1) * 128], f"ld_wo_{dc}", W=[f"fw0_{dc}"])

        flush("F1")
        for q in ("pool", "sp", "pe", "act", "dve"):
            S._wait(q, S._deps([], [f"fw{i}_{dc}" for i in (1, 2) for dc in range(8)]))
        sb.cur = region_mark + 64 + 16384
        lng = sb.alloc("lng", [128, D], F32)
        lnb = sb.alloc("lnb", [128, D], F32)
        xt = [sb.alloc(f"xt{i}", [128, D], F32) for i in range(4)]
        stt = sb.alloc("stt", [128, 12], F32)
        junk2 = sb.alloc("junk2", [128, D], BF16)
        mv = sb.alloc("mv", [128, 2], F32)
        rs2 = sb.alloc("rs2", [128, 2], F32)
        eps2 = sb.alloc("eps2", [128, 1], F32)
        assert sb.cur <= region_mark + 64 + 3 * 16384
        S.capture = []
        S.dma("sp", lng[:], lng_d, "ld_ln", W=["lng"])
        S.dma("sp", lnb[:], lnb_d, "ld_lnb", W=["lnb"])
        S.op("dve", lambda e: e.memset(eps2[:], EPS / (ALPHA * ALPHA)), W=["eps2"])
        CY = 0.5 / ALPHA
        ntile = T // 128 + 1
        for ti in range(ntile):
            r0 = ti * 128
            m = 128 if ti < T // 128 else NS
            sl = ti % 4
            bi = min(ti // 4, NBLK)
            if ti == 0:
                for tj in range(min(2, ntile)):
                    mj = 128 if tj < T // 128 else NS
                    S.dma("sp", xt[tj % 4][0:mj, :], xtok_d[tj * 128:tj * 128 + mj, :], f"ld_xt{tj % 4}", W=[f"xt{tj % 4}"])
            if ti + 2 < ntile:
                tj = ti + 2
                mj = 128 if tj < T // 128 else NS
                S.dma("sp", xt[tj % 4][0:mj, :], xtok_d[tj * 128:tj * 128 + mj, :], f"ld_xt{tj % 4}", W=[f"xt{tj % 4}"])
            for hh in range(2):
                bq = (2 * ti + hh) % 4
                bank, bkey = ((PA, "PA") if bq < 2 else (PB, "PB"))
                bank = bank[bq % 2]
                bkey = f"{bkey}{bq % 2}"
                def f(e, bank=bank, hh=hh, r0=r0, m=m):
                    for c in range(8):
                        r = e.matmul(bank[0:m, :], lhsT=mT[:, c, r0:r0 + m], rhs=wo[:, c, hh * 512:(hh + 1) * 512],
                                     start=(c == 0), stop=(c == 7))
                    return r
                S.op("pe", f, R=[f"mT_{bi}"] + [f"fw0_{dc}" for dc in range(4 * hh, 4 * hh + 4)], W=[bkey])
                S.op("dve", lambda e, bank=bank, hh=hh, sl=sl, m=m: e.scalar_tensor_tensor(
                    xt[sl][0:m, hh * 512:(hh + 1) * 512], bank[0:m, :], CY, xt[sl][0:m, hh * 512:(hh + 1) * 512],
                    op0=ALU.mult, op1=ALU.add), R=[bkey, f"xt{sl}"], W=[f"xt{sl}"])
            S.op("act", lambda e, sl=sl, m=m: e.activation(out=junk2[0:m, :], in_=xt[sl][0:m, :], func=AF.Copy,
                                                           accum_out=stt[0:m, 0:1]), R=[f"xt{sl}"], W=["junk2", "stt0"])
            S.op("act", lambda e, sl=sl, m=m: e.activation(out=junk2[0:m, :], in_=xt[sl][0:m, :], func=AF.Square,
                                                           accum_out=stt[0:m, 1:2]), R=[f"xt{sl}"], W=["junk2", "stt1"])
            S.op("dve", lambda e, m=m: e.tensor_scalar(mv[0:m, 0:1], stt[0:m, 0:1], 1.0 / D, None, op0=ALU.mult),
                 R=["stt0"], W=["mv"])
            S.op("dve", lambda e, m=m: e.tensor_tensor(mv[0:m, 1:2], mv[0:m, 0:1], mv[0:m, 0:1], op=ALU.mult),
                 R=["mv"], W=["mv"])
            S.op("dve", lambda e, m=m: e.scalar_tensor_tensor(mv[0:m, 1:2], stt[0:m, 1:2], 1.0 / D, mv[0:m, 1:2],
                                                              op0=ALU.mult, op1=ALU.subtract), R=["stt1", "mv"], W=["mv"])
            S.op("act", lambda e, m=m: e.activation(out=rs2[0:m, 0:1], in_=mv[0:m, 1:2], func=AF.Ln, scale=1.0, bias=eps2[0:m, :]),
                 R=["mv", "eps2"], W=["rs2"])
            S.op("act", lambda e, m=m: e.activation(out=rs2[0:m, 0:1], in_=rs2[0:m, 0:1], func=AF.Exp, scale=-0.5),
                 R=["rs2"], W=["rs2"])
            S.op("dve", lambda e, m=m: e.scalar_tensor_tensor(rs2[0:m, 1:2], mv[0:m, 0:1], -1.0, rs2[0:m, 0:1],
                                                              op0=ALU.mult, op1=ALU.mult), R=["rs2", "mv"], W=["rs2"])
            S.op("act", lambda e, sl=sl, m=m: e.activation(out=xt[sl][0:m, :], in_=xt[sl][0:m, :], func=AF.Identity,
                                                           scale=rs2[0:m, 0:1], bias=rs2[0:m, 1:2]),
                 R=[f"xt{sl}", "rs2"], W=[f"xt{sl}"])
            S.op("dve", lambda e, sl=sl, m=m: e.tensor_tensor(xt[sl][0:m, :], xt[sl][0:m, :], lng[0:m, :], op=ALU.mult),
                 R=[f"xt{sl}", "lng"], W=[f"xt{sl}"])
            S.op("pool", lambda e, sl=sl, m=m: e.tensor_tensor(xt[sl][0:m, :], xt[sl][0:m, :], lnb[0:m, :], op=ALU.add),
                 R=[f"xt{sl}", "lnb"], W=[f"xt{sl}"])
            S.dma("sp", y_d[r0:r0 + m, :], xt[sl][0:m, :], f"st_y{sl}", R=[f"xt{sl}"])

        flush("F2")
        S.final_wait("sp", [k for k in S.dcnt if k.startswith("st_")])

    with nc.Block() as block:
        @block.sync
        def _(e):
            program("sp", e)

        @block.gpsimd
        def _(e):
            program("pool", e)

        @block.tensor
        def _(e):
            program("pe", e)

        @block.scalar
        def _(e):
            program("act", e)

        @block.vector
        def _(e):
            program("dve", e)
    return nc


_NC_CACHE = {}


def kernel(x_prompt, x_sample, state_gla, state_hgrn, w_in, w_gate_lr, b_gate_lr, gla_norm_g, w_br_gla,
           hgrn_lb_param, hgrn_norm_g, w_br_hgrn, w_out, ln_g, ln_b):
    f = lambda a: np.ascontiguousarray(np.asarray(a, dtype=np.float32))
    x_prompt, x_sample = f(x_prompt), f(x_sample)
    state_gla, state_hgrn = f(state_gla), f(state_hgrn)
    if "nc" not in _NC_CACHE:
        _NC_CACHE["nc"] = build_nc()
    nc = _NC_CACHE["nc"]
    shared = {
        "w_in": f(w_in)[0],
        "wlr": f(w_gate_lr)[0],
        "bgl": f(f(b_gate_lr)[0].reshape(4, 128).T),
        "glag": f(np.broadcast_to(f(gla_norm_g)[0].reshape(1, 1024), (128, 1024))),
        "lbp": f(f(hgrn_lb_param).reshape(2, 8, 128).transpose(2, 0, 1).reshape(128, 16)),
        "hgg": f(np.broadcast_to(f(hgrn_norm_g)[0].reshape(1, 1024), (128, 1024))),
        "wbg": f(w_br_gla)[0],
        "wbh": f(w_br_hgrn)[0],
        "wout": f(w_out)[0],
        "lng": f(np.broadcast_to(f(ln_g)[0].reshape(1, D), (128, D))),
        "lnb": f(np.broadcast_to(f(ln_b)[0].reshape(1, D), (128, D))),
    }
    in_maps = []
    for b in range(NCORES):
        xs = x_sample[b * NS:(b + 1) * NS, 0, :]
        xtok = np.concatenate([x_prompt[b], xs], axis=0)
        m = dict(shared)
        m["xtok"] = f(xtok)
        m["xT"] = f(xtok.T)
        m["sg"] = f(state_gla[0, b * NS:(b + 1) * NS])
        m["sh"] = f(state_hgrn[0, b * NS:(b + 1) * NS])
        in_maps.append(m)
    res = run_bass_kernel_spmd(nc, in_maps, core_ids=list(range(NCORES)))
    rs = res.results
    y_prompt = np.stack([r["y"][:T] for r in rs], axis=0)
    y_sample = np.concatenate([r["y"][T:TT] for r in rs], axis=0)[:, None, :]
    gp = np.stack([r["gp"] for r in rs], axis=0)[None]
    hp = np.stack([r["hp"] for r in rs], axis=0)[None]
    gs = np.concatenate([r["gs"] for r in rs], axis=0)[None]
    hs = np.concatenate([r["hs"] for r in rs], axis=0)[None]
    return (y_prompt.astype(np.float32), y_sample.astype(np.float32), gp.astype(np.float32),
            hp.astype(np.float32), gs.astype(np.float32), hs.astype(np.float32))
```

```python
import numpy as np
import concourse.bass as bass
import concourse.mybir as mybir
from concourse.bass_utils import run_bass_kernel_spmd

F32 = mybir.dt.float32
BF16 = mybir.dt.bfloat16
AF = mybir.ActivationFunctionType
ALU = mybir.AluOpType

NCORES = 8
D = 1024
T = 2048
NS = 16
TT = T + NS
NBLK = 4
BLK = 512
IN_DIM = 9232
GA_OFF = 3072
HQ_OFF = 3088
HF_OFF = HQ_OFF + 1024
HI_OFF = HQ_OFF + 2048
HR_OFF = HQ_OFF + 3072
MGA_OFF = HQ_OFF + 4096
MGB_OFF = MGA_OFF + 1024
ALPHA = 2.0 ** 0.25
EPS = 1e-5


class Sched:
    def __init__(self, nc, cache, me, eobj):
        self.nc = nc
        self.cache = cache
        self.me = me
        self.e = eobj
        self.cnt = {k: 0 for k in ("pe", "act", "dve", "pool")}
        self.waited = {}
        self.last_w = {}
        self.readers = {}
        self.dcnt = {}
        self.capture = None

    def emit(self, rec, eng=None):
        if rec[0] == "op":
            self.op(rec[1], rec[2], rec[3], rec[4])
        elif rec[0] == "flex":
            out, in_ = rec[2]
            if eng == "act":
                self.op("act", lambda e: e.activation(out=out, in_=in_, func=AF.Copy), rec[3], rec[4])
            else:
                self.op("dve", lambda e: e.tensor_copy(out, in_), rec[3], rec[4])
        else:
            self.dma(rec[1], rec[2][0], rec[2][1], rec[2][2], rec[3], rec[4])

    def copy(self, out, in_, R=(), W=()):
        W = list(W) + [k for k in R if len(k) == 3 and k[0] == "P" and k[1] in "ABDX"]
        if self.capture is None:
            self.op("dve", lambda e: e.tensor_copy(out, in_), R, W)
            return
        size = 1
        for v in out.shape[1:]:
            size *= v
        self.capture.append(("flex", "dve", (out, in_), list(R), list(W), 230 + 0.83 * size, 120 + 1.12 * size))

    def _semh(self, s):
        k = "sem_" + s
        if k not in self.cache:
            self.cache[k] = self.nc.alloc_semaphore("s_" + s)
        if s not in self.cnt and s not in self.dcnt:
            self.dcnt[s] = 0
        return self.cache[k]

    def _deps(self, R, W):
        best = {}
        def add(sv):
            s, v = sv
            if v > best.get(s, 0):
                best[s] = v
        for b in R:
            if b in self.last_w:
                add(self.last_w[b])
        for b in W:
            if b in self.last_w:
                add(self.last_w[b])
            for sv in self.readers.get(b, {}).items():
                add(sv)
        return best

    def _wait(self, eng, best):
        for s, v in best.items():
            if s == "pe" and eng == "pe":
                continue
            if self.waited.get((eng, s), 0) >= v:
                continue
            h = self._semh(s)
            if eng == self.me:
                self.e.wait_ge(h, v)
            self.waited[(eng, s)] = v

    def _book(self, me, R, W):
        s, v = me
        for b in R:
            d = self.readers.setdefault(b, {})
            if v > d.get(s, 0):
                d[s] = v
        for b in W:
            self.last_w[b] = me
            self.readers[b] = {}

    def op(self, eng, fn, R=(), W=()):
        W = list(W) + [k for k in R if len(k) == 3 and k[0] == "P" and k[1] in "ABDX"]
        if self.capture is not None:
            fe = _FakeEng(eng)
            fn(fe)
            calls = fe.calls

            def replay(e, calls=calls):
                r = None
                for name, args, kw in calls:
                    r = getattr(e, name)(*args, **kw)
                return r
            self.capture.append(("op", eng, replay, list(R), list(W), fe.dur, fe.tset))
            return
        self._wait(eng, self._deps(R, W))
        self.cnt[eng] += 1
        h = self._semh(eng)
        if eng == self.me:
            fn(self.e).then_inc(h, 1)
        self._book((eng, self.cnt[eng]), R, W)

    def dma(self, q, out, in_, sem, R=(), W=()):
        if self.capture is not None:
            self.capture.append(("dma", q, (out, in_, sem), list(R), list(W)))
            return
        self._wait(q, self._deps(R, W))
        h = self._semh(sem)
        self.dcnt[sem] += 16
        if q == self.me:
            self.e.dma_start(out=out, in_=in_).then_inc(h, 16)
        self._book((sem, self.dcnt[sem]), R, W)

    def final_wait(self, eng, sems):
        for s in sems:
            if self.dcnt.get(s, 0) > 0 and eng == self.me:
                self.e.wait_ge(self._semh(s), self.dcnt[s])


class _FakeIns:
    def then_inc(self, *a, **k):
        return self


class _FakeEng:
    def __init__(self, kind):
        self.kind = kind
        self.dur = 0.0
        self.tset = None
        self.calls = []

    def __getattr__(self, name):
        def call(*args, **kw):
            self.calls.append((name, args, kw))
            out = kw.get("out", args[0] if args else None)
            size = 1
            try:
                for v in out.shape[1:]:
                    size *= v
            except Exception:
                size = 256
            if name == "matmul":
                self.dur += 70 + 0.62 * size
            elif self.kind == "act":
                self.dur += 230 + 0.83 * size
                f = kw.get("func")
                if f in (AF.Silu, AF.Tanh):
                    self.tset = 18
                elif f in (AF.Exp, AF.Ln):
                    self.tset = 6
            elif self.kind == "dve":
                self.dur += (120 + 1.12 * size) * (2.0 if name == "tensor_tensor_scan" else 1.0)
            elif self.kind == "pool":
                self.dur += 320 + 1.6 * size
            else:
                self.dur += 100
            return _FakeIns()
        return call


class ListScheduler:
    def __init__(self):
        self.free = {k: 0.0 for k in ("pe", "act", "dve", "pool", "sp")}
        self.wfin = {}
        self.rfin = {}
        self.tset = None

    def schedule(self, recs):
        n = len(recs)
        dur = [0.0] * n
        lat = [0.0] * n
        tset = [None] * n
        for i, r in enumerate(recs):
            if r[0] == "op":
                dur[i] = r[5] + 60.0
                lat[i] = dur[i]
                tset[i] = r[6]
            elif r[0] == "flex":
                dur[i] = min(r[5], r[6]) + 60.0
                lat[i] = dur[i]
            else:
                dur[i] = 1000.0 if r[1] == "pool" else 150.0
                lat[i] = dur[i] + 2600.0
        preds = [set() for _ in range(n)]
        lw, rd = {}, {}
        for i, r in enumerate(recs):
            R, W = r[3], r[4]
            for k in R:
                if k in lw:
                    preds[i].add(lw[k])
            for k in W:
                if k in lw:
                    preds[i].add(lw[k])
                for j in rd.get(k, ()):
                    preds[i].add(j)
            for k in R:
                rd.setdefault(k, []).append(i)
            for k in W:
                lw[k] = i
                rd[k] = []
            preds[i].discard(i)
        succs = [[] for _ in range(n)]
        for i in range(n):
            for j in preds[i]:
                succs[j].append(i)
        prio = [0.0] * n
        for i in range(n - 1, -1, -1):
            m = 0.0
            for j in succs[i]:
                if prio[j] > m:
                    m = prio[j]
            prio[i] = lat[i] + m
        base = [0.0] * n
        for i, r in enumerate(recs):
            b = 0.0
            for k in r[3]:
                b = max(b, self.wfin.get(k, 0.0))
            for k in r[4]:
                b = max(b, self.wfin.get(k, 0.0), self.rfin.get(k, 0.0))
            base[i] = b
        npred = [len(p) for p in preds]
        fin = [0.0] * n
        ready = [i for i in range(n) if npred[i] == 0]
        order = []
        choice = {}
        while ready:
            best, bkey = None, None
            for i in ready:
                dep = base[i]
                for j in preds[i]:
                    if fin[j] + 80.0 > dep:
                        dep = fin[j] + 80.0
                if recs[i][0] == "flex":
                    sa = max(self.free["act"], dep)
                    sd = max(self.free["dve"], dep)
                    if sa + recs[i][5] < sd + recs[i][6]:
                        eng, st, du = "act", sa, recs[i][5] + 60.0
                    else:
                        eng, st, du = "dve", sd, recs[i][6] + 60.0
                else:
                    eng = recs[i][1]
                    st = max(self.free[eng], dep)
                    du = dur[i]
                    if eng == "act" and tset[i] is not None and self.tset is not None and tset[i] != self.tset:
                        st += 1300.0
                key = (round(st / 150.0), -prio[i], i)
                if bkey is None or key < bkey:
                    best, bkey, bst, beng, bdu = i, key, st, eng, du
            i = best
            eng = beng
            if recs[i][0] == "flex":
                choice[i] = eng
                lat[i] = bdu
            if eng == "act" and tset[i] is not None:
                self.tset = tset[i]
            self.free[eng] = bst + bdu
            fin[i] = bst + lat[i]
            order.append(i)
            ready.remove(i)
            for j in succs[i]:
                npred[j] -= 1
                if npred[j] == 0:
                    ready.append(j)
        assert len(order) == n
        for i, r in enumerate(recs):
            for k in r[3]:
                self.rfin[k] = max(self.rfin.get(k, 0.0), fin[i])
            for k in r[4]:
                self.wfin[k] = fin[i]
                self.rfin[k] = 0.0
        return order, choice


class SBAlloc:
    def __init__(self, nc, cache):
        self.nc = nc
        self.cache = cache
        self.cur = (nc.sbuf_base + 63) // 64 * 64
        self.top = nc.sbuf_top
        self.n = 0

    def alloc(self, name, shape, dtype):
        isz = 2 if dtype == BF16 else 4
        size = isz
        for s in shape[1:]:
            size *= s
        size = (size + 63) // 64 * 64
        assert self.cur + size <= self.top, (name, self.cur, size, self.top)
        self.n += 1
        k = f"sb_{name}_{self.n}"
        if k not in self.cache:
            self.cache[k] = self.nc.alloc_sbuf_tensor_at(f"{name}_{self.n}", list(shape), dtype, offset=self.cur)
        self.cur += size
        return self.cache[k]


def build_nc():
    nc = bass.Bass("TRN2", target_bir_lowering=False)
    dt_in = lambda n, s: nc.dram_tensor(n, list(s), F32, kind="ExternalInput").ap()
    dt_out = lambda n, s: nc.dram_tensor(n, list(s), F32, kind="ExternalOutput").ap()
    xT_d = dt_in("xT", (D, TT))
    xtok_d = dt_in("xtok", (TT, D))
    sg_d = dt_in("sg", (NS, 4, 128, 256))
    sh_d = dt_in("sh", (NS, 8, 128, 128))
    win_d = dt_in("w_in", (D, IN_DIM))
    wlr_d = dt_in("wlr", (16, 512))
    bgl_d = dt_in("bgl", (128, 4))
    glag_d = dt_in("glag", (128, 1024))
    lbp_d = dt_in("lbp", (128, 16))
    hgg_d = dt_in("hgg", (128, 1024))
    wbg_d = dt_in("wbg", (D, D))
    wbh_d = dt_in("wbh", (D, D))
    wout_d = dt_in("wout", (D, D))
    lng_d = dt_in("lng", (128, D))
    lnb_d = dt_in("lnb", (128, D))
    y_d = dt_out("y", (TT, D))
    gp_d = dt_out("gp", (4, 128, 256))
    hp_d = dt_out("hp", (8, 128, 128))
    gs_d = dt_out("gs", (NS, 4, 128, 256))
    hs_d = dt_out("hs", (NS, 8, 128, 128))

    PA = [nc.alloc_psum_tensor(f"PA{i}", [128, 512], F32) for i in range(2)]
    PB = [nc.alloc_psum_tensor(f"PB{i}", [128, 512], F32) for i in range(2)]
    PD = [nc.alloc_psum_tensor(f"PD{i}", [128, 512], F32) for i in range(2)]
    PX = [nc.alloc_psum_tensor(f"PX{i}", [128, 512], F32) for i in range(2)]
    cache = {}

    def program(me, eobj):
        S = Sched(nc, cache, me, eobj)
        sb = SBAlloc(nc, cache)
        win_r = win_d.rearrange("(c p) n -> p c n", p=128)

        xT = sb.alloc("xT", [128, 8, TT], BF16)
        onT = sb.alloc("onT", [128, 16, TT], BF16)
        ident_f = sb.alloc("identf", [128, 128], F32)
        ident_b = sb.alloc("identb", [128, 128], BF16)
        U4 = sb.alloc("U4", [128, 4, 128], F32)
        msk = sb.alloc("msk", [128, 4, 128], F32)
        idrow = sb.alloc("idrow", [128, 16, 16], F32)
        negb = sb.alloc("negb", [128, 4], F32)
        lbp = sb.alloc("lbp", [128, 16], F32)
        c1 = sb.alloc("c1", [128, 8], F32)
        nc1 = sb.alloc("nc1", [128, 8], F32)
        c2 = sb.alloc("c2", [128, 8], F32)
        lnc1 = sb.alloc("lnc1", [128, 8], F32)
        region_mark = sb.cur
        wga = sb.alloc("wga", [128, 8, 16], BF16)
        wlr = sb.alloc("wlr", [16, 512], BF16)
        gaT = sb.alloc("gaT", [16, TT], BF16)

        rot = {"A": 0, "B": 0, "D": 0, "X": 0, "K": 0, "O": 0}

        def nxt(k):
            rot[k] ^= 1
            return rot[k]

        S.op("pool", lambda e: e.memset(ident_f[:], 1.0), W=["identf"])
        S.op("pool", lambda e: e.affine_select(out=ident_f[:], in_=ident_f[:], pattern=[[1, 128]],
                                               compare_op=ALU.is_equal, fill=0.0, base=0,
                                               channel_multiplier=-1), R=["identf"], W=["identf"])
        S.op("pool", lambda e: e.memset(U4[:], 1.0), W=["U4"])
        S.op("pool", lambda e: e.affine_select(out=U4[:], in_=U4[:], pattern=[[0, 4], [1, 128]],
                                               compare_op=ALU.is_ge, fill=0.0, base=0,
                                               channel_multiplier=-1), R=["U4"], W=["U4"])
        S.op("pool", lambda e: e.memset(msk[:], 1.0), W=["msk"])
        S.op("pool", lambda e: e.memset(msk[:, :, 0:1], 0.0), R=["msk"], W=["msk"])
        S.op("pool", lambda e: e.memset(idrow[:], 1.0), W=["idrow"])
        S.op("pool", lambda e: e.affine_select(out=idrow[:], in_=idrow[:], pattern=[[1, 16], [-1, 16]],
                                               compare_op=ALU.is_equal, fill=0.0, base=0,
                                               channel_multiplier=0), R=["idrow"], W=["idrow"])
        S.op("dve", lambda e: e.tensor_copy(ident_b[:], ident_f[:]), R=["identf"], W=["identb"])

        S.dma("sp", negb[:], bgl_d, "ld_negb", W=["negb"])
        S.dma("sp", lbp[:], lbp_d, "ld_lbp", W=["lbp"])
        S.op("dve", lambda e: e.tensor_scalar(negb[:], negb[:], -1.0, None, op0=ALU.mult), R=["negb"], W=["negb"])
        S.op("dve", lambda e: e.tensor_tensor(c2[:], lbp[:, 0:8], lbp[:, 8:16], op=ALU.subtract), R=["lbp"], W=["c2"])
        S.op("act", lambda e: e.activation(out=c2[:], in_=c2[:], func=AF.Tanh, scale=0.5), R=["c2"], W=["c2"])
        S.op("dve", lambda e: e.tensor_scalar(c1[:], c2[:], -0.25, 0.25, op0=ALU.mult, op1=ALU.add), R=["c2"], W=["c1"])
        S.op("dve", lambda e: e.tensor_scalar(nc1[:], c2[:], 0.25, -0.25, op0=ALU.mult, op1=ALU.add), R=["c2"], W=["nc1"])
        S.op("act", lambda e: e.activation(out=lnc1[:], in_=c1[:], func=AF.Ln), R=["c1"], W=["lnc1"])
        S.op("dve", lambda e: e.tensor_scalar(c2[:], c2[:], 0.25, 0.75, op0=ALU.mult, op1=ALU.add), R=["c2", "c1", "nc1"], W=["c2"])

        S.dma("pool", wga[:], win_r[:, :, GA_OFF:GA_OFF + 16], "ld_wga", W=["wga"])
        S.dma("pool", wlr[:], wlr_d, "ld_wlr", W=["wlr"])
        xT_r = xT_d.rearrange("(c p) n -> p c n", p=128)
        def XK(t0):
            return [f"xT{c}_{min(t0 // BLK, NBLK)}" for c in range(8)]

        def load_xT(bi):
            c0 = bi * BLK
            n = BLK if bi < NBLK else NS
            for c in range(8):
                S.dma("pool", xT[:, c, c0:c0 + n], xT_r[:, c, c0:c0 + n], f"ld_xT_{bi}", W=[f"xT{c}_{bi}"])
        load_xT(0)

        wu = [sb.alloc(f"wu{i}", [128, 8, 1024], BF16) for i in range(2)]
        g_u = [sb.alloc(f"g_u{i}", [128, 256], F32) for i in range(2)]
        GT = 1
        NSL = 6
        S0b = [sb.alloc(f"S0b{i}", [128, GT, 256], F32) for i in range(NSL)]
        S0bf = [sb.alloc(f"S0bf{i}", [128, 256], BF16) for i in range(2)]
        th = [sb.alloc(f"th_{i}", [128, 512], F32) for i in range(2)]
        sq = [sb.alloc(f"sq_{i}", [128, 512], F32) for i in range(2)]
        g1 = sb.alloc("g1", [128, 4, 128], F32)
        Eb = sb.alloc("Eb", [128, 4, 128], F32)
        keT = sb.alloc("keT", [128, 4, 128], BF16)
        kdT = sb.alloc("kdT", [128, 4, 128], BF16)
        qeT = [[sb.alloc(f"qeT_{p}{i}", [128, 512], BF16) for i in range(2)] for p in range(2)]
        kd = [[sb.alloc(f"kd_{p}{i}", [128, 512], BF16) for i in range(2)] for p in range(2)]
        ATb = [[sb.alloc(f"ATb_{p}{i}", [128, 4, 128], BF16) for i in range(2)] for p in range(2)]
        EbL = [[sb.alloc(f"EbL_{p}{i}", [128, 4], F32) for i in range(2)] for p in range(2)]
        vbf = [sb.alloc(f"vbf{p}", [128, 4, 256], BF16) for p in range(3)]
        ug = [sb.alloc(f"ug{p}", [128, 4, 256], F32) for p in range(3)]
        Sst = sb.alloc("Sst", [128, 256], F32)
        Sbf = [sb.alloc(f"Sbf{i}", [128, 4, 256], BF16) for i in range(2)]
        onb = [sb.alloc("onb0", [128, 4, 256], BF16), sb.alloc("onb1", [128, 4, 128], BF16)]
        junk = sb.alloc("junk", [128, 256], BF16)
        ssq = sb.alloc("ssq", [128, 8], F32)
        rstd = sb.alloc("rstd", [128, 8], F32)
        eps_t = sb.alloc("eps_t", [128, 1], F32)
        one_t = sb.alloc("one_t", [128, 1], F32)
        s_e = sb.alloc("s_e", [128, 2, NS], F32)
        s_g = sb.alloc("s_g", [128, 2, NS], F32)
        s_q = sb.alloc("s_q", [128, 2, NS], F32)
        s_qb = sb.alloc("s_qb", [128, 2, NS], BF16)
        s_qe = sb.alloc("s_qe", [128, 2, NS], F32)
        s_k = sb.alloc("s_k", [128, 2, NS], F32)
        s_kb = sb.alloc("s_kb", [128, 2, NS], BF16)
        ktok = sb.alloc("ktok", [16, 2, 128], F32)
        Ks = [sb.alloc(f"Ks{i}", [16, 128], BF16) for i in range(2)]
        Qsel = sb.alloc("Qsel", [128, 2, NS, NS], BF16)
        qkd = sb.alloc("qkd", [16, 2, NS], BF16)
        s_v = sb.alloc("s_v", [16, 256], BF16)
        s_u = sb.alloc("s_u", [16, 256], F32)
        s_on = sb.alloc("s_on", [16, 256], BF16)
        S.op("dve", lambda e: e.memset(eps_t[:], EPS), W=["eps_t"])
        S.op("dve", lambda e: e.memset(one_t[:], 1.0), W=["one_t"])

        def load_unit_weights(u, slot):
            w = wu[slot]
            key = f"wu{slot}"
            if u < 4:
                h = u
                segs = [(0, h * 128, 128), (128, 512 + h * 128, 128), (256, 1024 + h * 256, 256),
                        (512, 2048 + h * 256, 256)]
            else:
                j = u - 4
                segs = [(0, HQ_OFF + j * 256, 256), (256, HF_OFF + j * 256, 256),
                        (512, HI_OFF + j * 256, 256), (768, HR_OFF + j * 256, 256)]
            for (o, c0, n) in segs:
                S.dma("pool", w[:, :, o:o + n], win_r[:, :, c0:c0 + n], f"ld_wu{slot}", W=[key])

        order = [0, 1, 2, 3, 4, 5, 6, 7]
        load_unit_weights(order[0], 0)
        for bi_ in range(1, NBLK + 1):
            load_xT(bi_)
        load_unit_weights(order[1], 1)

        for bi in range(NBLK + 1):
            t0 = bi * BLK
            n = BLK if bi < NBLK else NS
            a = nxt("A")
            def f(e, a=a, t0=t0, n=n):
                for c in range(8):
                    r = e.matmul(PA[a][0:16, 0:n], lhsT=wga[:, c, :], rhs=xT[:, c, t0:t0 + n],
                                 start=(c == 0), stop=(c == 7))
                return r
            S.op("pe", f, R=["wga"] + XK(t0), W=[f"PA{a}"])
            S.op("act", lambda e, a=a, t0=t0, n=n: e.activation(out=gaT[:, t0:t0 + n], in_=PA[a][0:16, 0:n], func=AF.Copy),
                 R=[f"PA{a}"], W=["gaT"])

        def proj_fm(slot, woff, t0, n):
            a = nxt("A")
            def f(e):
                for c in range(8):
                    r = e.matmul(PA[a][:, 0:n], lhsT=wu[slot][:, c, woff:woff + 128], rhs=xT[:, c, t0:t0 + n],
                                 start=(c == 0), stop=(c == 7))
                return r
            S.op("pe", f, R=[f"wu{slot}"] + XK(t0), W=[f"PA{a}"])
            return a

        def proj_tm(slot, woff, t0, m):
            b = nxt("B")
            def f(e):
                for c in range(8):
                    r = e.matmul(PB[b][0:m, :], lhsT=xT[:, c, t0:t0 + m], rhs=wu[slot][:, c, woff:woff + 512],
                                 start=(c == 0), stop=(c == 7))
                return r
            S.op("pe", f, R=[f"wu{slot}"] + XK(t0), W=[f"PB{b}"])
            return b

        def rstd_from(ssq_ap, rstd_ap, m, dv, keys_r, keys_w):
            S.op("act", lambda e: e.activation(out=rstd_ap, in_=ssq_ap, func=AF.Ln, scale=1.0 / dv, bias=eps_t[0:m, :]),
                 R=keys_r, W=keys_w)
            S.op("act", lambda e: e.activation(out=rstd_ap, in_=rstd_ap, func=AF.Exp, scale=-0.5),
                 R=keys_w, W=keys_w)

        class Item:
            pass

        items = []
        for ui, u in enumerate(order):
            for bi in list(range(NBLK)) + ["s"]:
                it = Item()
                it.ui, it.u, it.slot, it.bi = ui, u, ui % 2, bi
                it.p = len(items) % 2
                it.q3 = len(items) % 3
                it.gla = u < 4
                it.nh = 1 if it.gla else 2
                it.DV = 256 if it.gla else 128
                it.vr_off = 256 if it.gla else 512
                it.vc0 = 2 * u if it.gla else 8 + 2 * (u - 4)
                it.gk = f"g_u{ui % 2}"
                items.append(it)

        def hd_of(it, e_):
            return 2 * (it.u - 4) + e_

        def vsl_of(it, e_):
            return slice(0, 256) if it.gla else slice(e_ * 128, (e_ + 1) * 128)

        def alpha1(it):
            slot, q3 = it.slot, it.q3
            gu = g_u[it.ui % 2]
            if it.bi == 0:
                src = glag_d[:, it.u * 256:(it.u + 1) * 256] if it.gla else hgg_d[:, (it.u - 4) * 256:(it.u - 3) * 256]
                S.dma("sp", gu[:], src, f"ld_gu{it.ui % 2}", W=[it.gk])
            if it.bi == "s":
                b = proj_tm(slot, it.vr_off, T, NS)
                S.op("act", lambda e: e.activation(out=s_v[:], in_=PB[b][0:NS, 0:256], func=AF.Copy), R=[f"PB{b}"], W=["s_v"])
                S.op("act", lambda e: e.activation(out=s_u[:], in_=PB[b][0:NS, 256:512], func=AF.Copy), R=[f"PB{b}"], W=["s_u"])
                if not it.gla:
                    for e_ in range(2):
                        a = proj_fm(slot, e_ * 128, T, NS)
                        S.op("act", lambda e, a=a, e_=e_: e.activation(out=s_q[:, e_, :], in_=PA[a][:, 0:NS], func=AF.Copy),
                             R=[f"PA{a}"], W=["s_q"])
                        a = proj_fm(slot, 256 + e_ * 128, T, NS)
                        S.op("dve", lambda e, a=a, e_=e_: e.tensor_copy(s_k[:, e_, :], PA[a][:, 0:NS]),
                             R=[f"PA{a}"], W=["s_k"])
                for g in range(NSL):
                    load_S0(it, g)
                yield
                return
            t0 = it.bi * BLK
            for i in range(4):
                b = proj_tm(slot, it.vr_off, t0 + i * 128, 128)
                S.copy(vbf[q3][:, i, :], PB[b][:, 0:256], R=[f"PB{b}"], W=[f"vbf{q3}_{i}"])
                S.copy(ug[q3][:, i, :], PB[b][:, 256:512], R=[f"PB{b}"], W=[f"ug{q3}_{i}"])
                yield
            yield "TILES_DONE"
            if not it.gla:
                yield "WAIT_BETA"
                for e_ in range(2):
                    a = proj_fm(slot, e_ * 128, t0, BLK)
                    S.copy(sq[e_][:], PA[a][:], R=[f"PA{a}"], W=[f"sq_{e_}"])
                    yield
                    a = proj_fm(slot, 256 + e_ * 128, t0, BLK)
                    S.copy(th[e_][:], PA[a][:], R=[f"PA{a}"], W=[f"th_{e_}"])
                    yield

        def alpha2(it):
            q3 = it.q3
            gu = g_u[it.ui % 2]
            if it.bi == "s":
                S.op("act", lambda e: e.activation(out=s_u[:], in_=s_u[:], func=AF.Silu), R=["s_u"], W=["s_u"])
                S.op("pool", lambda e: e.tensor_tensor(s_u[:], s_u[:], gu[0:NS, :], op=ALU.mult), R=["s_u", it.gk], W=["s_u"])
                if not it.gla:
                    S.op("act", lambda e: e.activation(out=s_q[:], in_=s_q[:], func=AF.Silu), R=["s_q"], W=["s_q"])
                    S.op("act", lambda e: e.activation(out=s_k[:], in_=s_k[:], func=AF.Tanh, scale=0.5), R=["s_k"], W=["s_k"])
                return
            ugk = [f"ug{q3}_{i}" for i in range(4)]
            S.op("act", lambda e: e.activation(out=ug[q3][:], in_=ug[q3][:], func=AF.Silu), R=ugk, W=ugk)
            for i in range(4):
                S.op("pool", lambda e, i=i: e.tensor_tensor(ug[q3][:, i, :], ug[q3][:, i, :], gu[:], op=ALU.mult),
                     R=[f"ug{q3}_{i}", it.gk], W=[f"ug{q3}_{i}"])
            if not it.gla:
                for e_ in range(2):
                    S.op("act", lambda e, e_=e_: e.activation(out=sq[e_][:], in_=sq[e_][:], func=AF.Silu),
                         R=[f"sq_{e_}"], W=[f"sq_{e_}"])
                    S.op("act", lambda e, e_=e_: e.activation(out=th[e_][:], in_=th[e_][:], func=AF.Tanh, scale=-0.5),
                         R=[f"th_{e_}"], W=[f"th_{e_}"])

        def load_S0(it, g):
            sl = g % NSL
            n0 = g * GT
            if it.gla:
                S.dma("sp", S0b[sl][:], sg_d[n0:n0 + GT, it.u].rearrange("n k v -> k n v"), f"ld_S0{sl}", W=[f"S0b{sl}"])
            else:
                j = it.u - 4
                for hh in range(2):
                    S.dma("sp", S0b[sl][:, :, hh * 128:(hh + 1) * 128],
                          sh_d[n0:n0 + GT, 2 * j + hh].rearrange("n k v -> k n v"), f"ld_S0{sl}", W=[f"S0b{sl}"])

        def beta(it):
            slot, p = it.slot, it.p
            g1f = g1[:].rearrange("p c t -> p (c t)")
            Ebf = Eb[:].rearrange("p c t -> p (c t)")
            keTf = keT[:].rearrange("p c t -> p (c t)")
            mskf = msk[:].rearrange("p c t -> p (c t)")
            if it.bi == "s":
                nh = it.nh
                for e_ in range(nh):
                    if it.gla:
                        h = it.u
                        a = nxt("A")
                        S.op("pe", lambda e, a=a, h=h: e.matmul(PA[a][:, 0:NS], lhsT=wlr[:, h * 128:(h + 1) * 128],
                                                              rhs=gaT[:, T:T + NS], start=True, stop=True),
                             R=["wlr", "gaT"], W=[f"PA{a}"])
                        S.op("act", lambda e, a=a, h=h, e_=e_: e.activation(out=s_g[:, e_, :], in_=PA[a][:, 0:NS], func=AF.Exp,
                                                                          scale=-1.0, bias=negb[:, h:h + 1]),
                             R=[f"PA{a}", "negb"], W=["s_g"])
                        S.op("act", lambda e, e_=e_: e.activation(out=s_g[:, e_, :], in_=s_g[:, e_, :], func=AF.Ln, scale=1.0,
                                                                  bias=one_t[:]), R=["s_g", "one_t"], W=["s_g"])
                        sE = -1.0 / 16.0
                        a = proj_fm(slot, 128, T, NS)
                        S.op("act", lambda e, a=a, e_=e_: e.activation(out=s_k[:, e_, :], in_=PA[a][:, 0:NS], func=AF.Copy),
                             R=[f"PA{a}"], W=["s_k"])
                        a = proj_fm(slot, 0, T, NS)
                        S.op("act", lambda e, a=a, e_=e_: e.activation(out=s_q[:, e_, :], in_=PA[a][:, 0:NS], func=AF.Identity,
                                                                      scale=128.0 ** -0.5), R=[f"PA{a}"], W=["s_q"])
                    else:
                        hd = hd_of(it, e_)
                        S.op("act", lambda e, e_=e_, hd=hd: e.activation(out=s_g[:, e_, :], in_=s_k[:, e_, :], func=AF.Ln,
                                                                        scale=c1[:, hd:hd + 1], bias=c2[:, hd:hd + 1]),
                             R=["s_k", "c1", "c2"], W=["s_g"])
                        S.op("dve", lambda e, e_=e_, hd=hd: e.tensor_scalar(s_k[:, e_, :], s_k[:, e_, :], nc1[:, hd:hd + 1],
                                                                           c1[:, hd:hd + 1], op0=ALU.mult, op1=ALU.add),
                             R=["s_k", "c1", "nc1", "s_g"], W=["s_k"])
                        sE = 1.0
                    S.op("act", lambda e, e_=e_, sE=sE: e.activation(out=s_e[:, e_, :], in_=s_g[:, e_, :], func=AF.Exp, scale=sE),
                         R=["s_g"], W=["s_e"])
                    yield
                S.op("dve", lambda e: e.tensor_copy(s_kb[:, 0:nh, :], s_k[:, 0:nh, :]), R=["s_k"], W=["s_kb"])
                S.op("dve", lambda e: e.tensor_copy(s_qb[:, 0:nh, :], s_q[:, 0:nh, :]), R=["s_q"], W=["s_qb"])
                S.op("dve", lambda e: e.tensor_tensor(s_qe[:, 0:nh, :], s_q[:, 0:nh, :], s_e[:, 0:nh, :], op=ALU.mult),
                     R=["s_q", "s_e"], W=["s_qe"])
                S.op("dve", lambda e: e.tensor_tensor(
                    Qsel[:, 0:nh, :, :], s_qe[:, 0:nh, :].unsqueeze(3).broadcast_to([128, nh, NS, NS]),
                    idrow[:].unsqueeze(1).broadcast_to([128, nh, NS, NS]), op=ALU.mult), R=["s_qe", "idrow"], W=["Qsel"])
                yield
                for e_ in range(nh):
                    x = nxt("X")
                    S.op("pe", lambda e, e_=e_, x=x: e.matmul(PX[x][0:NS, 0:128], lhsT=s_kb[:, e_, :], rhs=ident_b[:],
                                                            start=True, stop=True), R=["s_kb", "identb"], W=[f"PX{x}"])
                    S.op("dve", lambda e, e_=e_, x=x: e.tensor_copy(ktok[:, e_, :], PX[x][0:NS, 0:128]),
                         R=[f"PX{x}"], W=["ktok"])
                    x = nxt("X")
                    S.op("pe", lambda e, e_=e_, x=x: e.matmul(PX[x][0:NS, 0:NS], lhsT=s_qb[:, e_, :], rhs=s_kb[:, e_, :],
                                                            start=True, stop=True), R=["s_kb", "s_qb"], W=[f"PX{x}"])
                    S.op("dve", lambda e, e_=e_, x=x: e.tensor_tensor(qkd[:, e_, :], PX[x][0:NS, 0:NS], ident_f[0:NS, 0:NS],
                                                                    op=ALU.mult), R=[f"PX{x}", "identf"], W=["qkd"])
                    yield
                return
            t0 = it.bi * BLK
            for e_ in range(it.nh):
                if it.gla:
                    h = it.u
                    a = nxt("A")
                    S.op("pe", lambda e, a=a, h=h: e.matmul(PA[a][:], lhsT=wlr[:, h * 128:(h + 1) * 128],
                                                          rhs=gaT[:, t0:t0 + BLK], start=True, stop=True),
                         R=["wlr", "gaT"], W=[f"PA{a}"])
                    S.op("act", lambda e, a=a, h=h: e.activation(out=g1f, in_=PA[a][:], func=AF.Exp, scale=-1.0,
                                                               bias=negb[:, h:h + 1]), R=[f"PA{a}", "negb"], W=["g1"])
                    S.op("act", lambda e: e.activation(out=g1f, in_=g1f, func=AF.Ln, scale=1.0, bias=one_t[:]),
                         R=["g1", "one_t"], W=["g1"])
                    sE = -1.0 / 16.0
                else:
                    hd = hd_of(it, e_)
                    S.op("act", lambda e, e_=e_, hd=hd: e.activation(out=g1f, in_=th[e_][:], func=AF.Ln, scale=nc1[:, hd:hd + 1],
                                                                    bias=c2[:, hd:hd + 1]), R=[f"th_{e_}", "nc1", "c2"], W=["g1"])
                    sE = 1.0
                yield
                S.op("dve", lambda e: e.tensor_tensor_scan(g1f, mskf, g1f, 0.0, op0=ALU.mult, op1=ALU.add),
                     R=["g1", "msk"], W=["g1"])
                S.op("act", lambda e, sE=sE: e.activation(out=Ebf, in_=g1f, func=AF.Exp, scale=sE), R=["g1"], W=["Eb"])
                if it.gla:
                    S.op("act", lambda e, sE=sE: e.activation(out=g1f, in_=g1f, func=AF.Exp, scale=-sE), R=["g1"], W=["g1"])
                else:
                    S.op("act", lambda e, sE=sE, hd=hd: e.activation(out=g1f, in_=g1f, func=AF.Exp, scale=-sE,
                                                                    bias=lnc1[:, hd:hd + 1]), R=["g1", "lnc1"], W=["g1"])
                yield
                if it.gla:
                    a = proj_fm(slot, 128, t0, BLK)
                    S.op("dve", lambda e, a=a: e.tensor_tensor(keTf, PA[a][:], g1f, op=ALU.mult),
                         R=[f"PA{a}", "g1"], W=["keT"])
                    a = proj_fm(slot, 0, t0, BLK)
                    S.op("dve", lambda e, a=a, e_=e_: e.scalar_tensor_tensor(
                        qeT[p][e_][:], PA[a][:], 128.0 ** -0.5, Ebf, op0=ALU.mult, op1=ALU.mult),
                         R=[f"PA{a}", "Eb"], W=[f"qeT_{p}{e_}"])
                else:
                    S.op("dve", lambda e, e_=e_: e.scalar_tensor_tensor(keTf, th[e_][:], 1.0, g1f, op0=ALU.add, op1=ALU.mult),
                         R=[f"th_{e_}", "g1"], W=["keT"])
                    S.op("pool", lambda e, e_=e_: e.tensor_tensor(qeT[p][e_][:], sq[e_][:], Ebf, op=ALU.mult),
                         R=[f"sq_{e_}", "Eb"], W=[f"qeT_{p}{e_}"])
                yield
                S.op("pool", lambda e: e.tensor_tensor(kdT[:], keT[:], Eb[:, :, 127:128].broadcast_to([128, 4, 128]), op=ALU.mult),
                     R=["keT", "Eb"], W=["kdT"])
                S.op("pool", lambda e, e_=e_: e.tensor_copy(EbL[p][e_][:], Eb[:, :, 127]), R=["Eb"], W=[f"EbL_{p}{e_}"])
                a = nxt("A")
                def f(e, e_=e_, a=a):
                    for cc in range(4):
                        r = e.matmul(PA[a][:, cc * 128:(cc + 1) * 128], lhsT=keT[:, cc, :],
                                     rhs=qeT[p][e_][:, cc * 128:(cc + 1) * 128], start=True, stop=True)
                    return r
                S.op("pe", f, R=["keT", f"qeT_{p}{e_}"], W=[f"PA{a}"])
                S.op("dve", lambda e, e_=e_, a=a: e.tensor_tensor(ATb[p][e_][:].rearrange("p c t -> p (c t)"), PA[a][:],
                                                                 U4[:].rearrange("p c t -> p (c t)"), op=ALU.mult),
                     R=[f"PA{a}", "U4"], W=[f"ATb_{p}{e_}"])
                yield
                x = nxt("X")
                def f(e, x=x):
                    for cc in range(4):
                        r = e.matmul(PX[x][:, cc * 128:(cc + 1) * 128], lhsT=kdT[:, cc, :], rhs=ident_b[:], start=True, stop=True)
                    return r
                S.op("pe", f, R=["kdT", "identb"], W=[f"PX{x}"])
                S.copy(kd[p][e_][:], PX[x][:], R=[f"PX{x}"], W=[f"kd_{p}{e_}"])
                yield

        def stage2(it):
            slot, p, nh, DV, q3 = it.slot, it.p, it.nh, it.DV, it.q3
            if it.bi == "s":
                for n in range(NS):
                    sl = n % NSL
                    bsl = n % 2
                    S.copy(S0bf[bsl][:], S0b[sl][:, 0, :], R=[f"S0b{sl}"], W=[f"S0bf{bsl}"])
                    for e_ in range(nh):
                        vsl = vsl_of(it, e_)
                        d = e_
                        S.op("pe", lambda e, e_=e_, n=n, bsl=bsl, vsl=vsl, d=d: e.matmul(
                            PD[d][0:NS, 0:DV], lhsT=Qsel[:, e_, n, :], rhs=S0bf[bsl][:, vsl], start=(n == 0), stop=False),
                             R=["Qsel", f"S0bf{bsl}"], W=[f"PD{d}"])
                        kr = nxt("K")
                        S.op("dve", lambda e, e_=e_, n=n, kr=kr: e.tensor_scalar(
                            Ks[kr][:], ktok[:, e_, :], ident_f[0:NS, n:n + 1], None, op0=ALU.mult),
                             R=["ktok", "identf"], W=[f"Ks{kr}"])
                        x = nxt("X")
                        S.op("pe", lambda e, kr=kr, vsl=vsl, x=x: e.matmul(
                            PX[x][:, 0:DV], lhsT=Ks[kr][:], rhs=s_v[:, vsl], start=True, stop=True),
                             R=[f"Ks{kr}", "s_v"], W=[f"PX{x}"])
                        S.op("dve", lambda e, e_=e_, n=n, sl=sl, vsl=vsl, x=x: e.scalar_tensor_tensor(
                            S0b[sl][:, 0, vsl], S0b[sl][:, 0, vsl], s_e[:, e_, n:n + 1], PX[x][:, 0:DV],
                            op0=ALU.mult, op1=ALU.add), R=[f"PX{x}", f"S0b{sl}", "s_e"], W=[f"S0b{sl}"])
                    if it.gla:
                        S.dma("pool", gs_d[n:n + 1, it.u].rearrange("n k v -> k n v"), S0b[sl][:], f"st_Sn{sl}", R=[f"S0b{sl}"])
                    else:
                        j = it.u - 4
                        for hh in range(2):
                            S.dma("pool", hs_d[n:n + 1, 2 * j + hh].rearrange("n k v -> k n v"),
                                  S0b[sl][:, :, hh * 128:(hh + 1) * 128], f"st_Sn{sl}", R=[f"S0b{sl}"])
                    if n + NSL < NS:
                        load_S0(it, n + NSL)
                    yield
                for e_ in range(nh):
                    vsl = vsl_of(it, e_)
                    d = e_
                    S.op("pe", lambda e, e_=e_, vsl=vsl, d=d: e.matmul(PD[d][0:NS, 0:DV], lhsT=qkd[:, e_, :], rhs=s_v[:, vsl],
                                                                     start=False, stop=True),
                         R=["qkd", "s_v"], W=[f"PD{d}"])
                    col = e_
                    S.op("act", lambda e, d=d, col=col: e.activation(out=junk[0:NS, 0:DV], in_=PD[d][0:NS, 0:DV], func=AF.Square,
                                                                     accum_out=ssq[0:NS, col:col + 1]),
                         R=[f"PD{d}"], W=["junk", f"ssq{col}"])
                    rstd_from(ssq[0:NS, col:col + 1], rstd[0:NS, col:col + 1], NS, DV, [f"ssq{col}", "eps_t"], [f"rstd{col}"])
                    S.op("dve", lambda e, d=d, col=col, vsl=vsl: e.scalar_tensor_tensor(
                        s_on[:, vsl], PD[d][0:NS, 0:DV], rstd[0:NS, col:col + 1], s_u[:, vsl], op0=ALU.mult, op1=ALU.mult),
                         R=[f"PD{d}", f"rstd{col}", "s_u"], W=["s_on"])
                x = nxt("X")
                def f(e, x=x):
                    for jj in range(2):
                        r = e.matmul(PX[x][:, jj * NS:(jj + 1) * NS], lhsT=s_on[:, jj * 128:(jj + 1) * 128],
                                     rhs=ident_b[0:NS, 0:NS], start=True, stop=True)
                    return r
                S.op("pe", f, R=["s_on", "identb"], W=[f"PX{x}"])
                vc0 = it.vc0
                S.op("act", lambda e, x=x, vc0=vc0: e.activation(
                    out=onT[:, vc0:vc0 + 2, T:T + NS], in_=PX[x][:, 0:2 * NS].rearrange("p (j t) -> p j t", t=NS),
                    func=AF.Copy), R=[f"PX{x}"], W=[f"onT{vc0}_s"])
                yield
                return
            bi = it.bi
            t0 = bi * BLK
            pb = bi % 2
            for e_ in range(nh):
                vsl = vsl_of(it, e_)
                sks = ["Sst0", "Sst1"] if it.gla else [f"Sst{e_}"]
                sbk = (lambda q: [f"Sbf{q}_0", f"Sbf{q}_1"]) if it.gla else (lambda q, e_=e_: [f"Sbf{q}_{e_}"])
                Sv = Sst[:, vsl]
                cpb = 512 // DV
                nbk = 4 // cpb
                xb = [nxt("X") for _ in range(nbk)]
                for bk in range(nbk):
                    def f(e, e_=e_, bk=bk, vsl=vsl):
                        for j in range(cpb):
                            cc = bk * cpb + j
                            r = e.matmul(PX[xb[bk]][:, j * DV:(j + 1) * DV], lhsT=kd[p][e_][:, cc * 128:(cc + 1) * 128],
                                         rhs=vbf[q3][:, cc, vsl], start=True, stop=True)
                        return r
                    S.op("pe", f, R=[f"kd_{p}{e_}"] + [f"vbf{q3}_{bk * cpb + j}" for j in range(cpb)], W=[f"PX{xb[bk]}"])
                for cc in range(4):
                    gc = bi * 4 + cc
                    bk, j = cc // cpb, cc % cpb
                    usl = PX[xb[bk]][:, j * DV:(j + 1) * DV]
                    if gc == 0:
                        S.op("dve", lambda e, usl=usl: e.tensor_copy(Sv, usl), R=[f"PX{xb[bk]}"], W=sks)
                    else:
                        S.op("dve", lambda e, usl=usl, cc=cc: e.scalar_tensor_tensor(
                            Sv, Sv, EbL[p][e_][:, cc:cc + 1], usl, op0=ALU.mult, op1=ALU.add),
                             R=[f"PX{xb[bk]}", f"EbL_{p}{e_}"] + sks, W=sks)
                    S.copy(Sbf[pb][:, cc, vsl], Sv, R=sks, W=sbk(pb))
                yield
                db = [nxt("D") for _ in range(nbk)]
                for bk in range(nbk):
                    def f(e, e_=e_, bk=bk, vsl=vsl):
                        for j in range(cpb):
                            cc = bk * cpb + j
                            gc = bi * 4 + cc
                            osl = PD[db[bk]][:, j * DV:(j + 1) * DV]
                            r = e.matmul(osl, lhsT=ATb[p][e_][:, cc, :], rhs=vbf[q3][:, cc, vsl], start=True, stop=(gc == 0))
                            if gc > 0:
                                prev = Sbf[1 - pb][:, 3, vsl] if cc == 0 else Sbf[pb][:, cc - 1, vsl]
                                r = e.matmul(osl, lhsT=qeT[p][e_][:, cc * 128:(cc + 1) * 128], rhs=prev, start=False, stop=True)
                        return r
                    S.op("pe", f, R=[f"ATb_{p}{e_}", f"qeT_{p}{e_}"] + sbk(pb) + sbk(1 - pb) +
                         [f"vbf{q3}_{bk * cpb + j}" for j in range(cpb)], W=[f"PD{db[bk]}"])
                yield
                for cc in range(4):
                    bk, j = cc // cpb, cc % cpb
                    col = e_ * 4 + cc
                    S.op("act", lambda e, bk=bk, j=j, col=col: e.activation(
                        out=junk[:, 0:DV], in_=PD[db[bk]][:, j * DV:(j + 1) * DV], func=AF.Square, accum_out=ssq[:, col:col + 1]),
                         R=[f"PD{db[bk]}"], W=["junk", f"ssq{e_}"])
                rstd_from(ssq[:, e_ * 4:e_ * 4 + 4], rstd[:, e_ * 4:e_ * 4 + 4], 128, DV, [f"ssq{e_}", "eps_t"], [f"rstd{e_}"])
                for cc in range(4):
                    bk, j = cc // cpb, cc % cpb
                    col = e_ * 4 + cc
                    S.op("dve", lambda e, bk=bk, j=j, col=col, cc=cc: e.scalar_tensor_tensor(
                        onb[e_][:, cc, 0:DV], PD[db[bk]][:, j * DV:(j + 1) * DV], rstd[:, col:col + 1], ug[q3][:, cc, vsl],
                        op0=ALU.mult, op1=ALU.mult),
                         R=[f"PD{db[bk]}", f"rstd{e_}", f"ug{q3}_{cc}"], W=[f"onb{e_}"])
                yield
                nv = DV // 128
                for jj in range(nv):
                    x2 = nxt("X")
                    def f(e, e_=e_, jj=jj, x2=x2):
                        for cc in range(4):
                            r = e.matmul(PX[x2][:, cc * 128:(cc + 1) * 128], lhsT=onb[e_][:, cc, jj * 128:(jj + 1) * 128],
                                         rhs=ident_b[:], start=True, stop=True)
                        return r
                    S.op("pe", f, R=[f"onb{e_}", "identb"], W=[f"PX{x2}"])
                    vc = it.vc0 + (jj if it.gla else e_)
                    S.copy(onT[:, vc, t0:t0 + BLK], PX[x2][:], R=[f"PX{x2}"], W=[f"onT{vc}_{bi}"])
                yield
            if bi == NBLK - 1:
                for e_ in range(nh):
                    if it.gla:
                        S.dma("sp", gp_d[it.u], Sst[:], "st_gp", R=["Sst0", "Sst1"])
                    else:
                        S.dma("sp", hp_d[hd_of(it, e_)], Sst[:, e_ * 128:(e_ + 1) * 128], f"st_hp{e_}", R=[f"Sst{e_}"])

        def run_all(g):
            for _ in g:
                pass

        def interleave(gens, beta_idx=None):
            alive = [True] * len(gens)
            paused = [False] * len(gens)
            while any(alive):
                for k in range(len(gens)):
                    if not alive[k]:
                        continue
                    if paused[k]:
                        if beta_idx is not None and alive[beta_idx]:
                            continue
                        paused[k] = False
                    try:
                        r = next(gens[k])
                        if r == "WAIT_BETA":
                            paused[k] = True
                    except StopIteration:
                        alive[k] = False

        nit = len(items)
        if "lsched" not in cache:
            cache["lsched"] = ListScheduler()
            cache["orders"] = {}
        lsched = cache["lsched"]

        def flush(tag):
            recs = S.capture
            S.capture = None
            if tag not in cache["orders"]:
                cache["orders"][tag] = lsched.schedule(recs)
            order_, choice_ = cache["orders"][tag]
            for i in order_:
                S.emit(recs[i], choice_.get(i))

        WIN = 8
        S.capture = []
        run_all(alpha1(items[0]))
        alpha2(items[0])
        run_all(beta(items[0]))
        run_all(alpha1(items[1]))
        for k, it in enumerate(items):
            if k + 1 < nit:
                alpha2(items[k + 1])
            gens = [stage2(it)]
            bidx = None
            if k + 1 < nit:
                gens.append(beta(items[k + 1]))
                bidx = 1
            if k + 2 < nit:
                ga1 = alpha1(items[k + 2])
                if items[k + 2].bi != "s":
                    for r in ga1:
                        if r == "TILES_DONE":
                            break
                gens.append(ga1)
            interleave(gens, bidx)
            if k + 1 < nit:
                nx = items[k + 1]
                if nx.bi == "s" and nx.ui + 2 < len(order):
                    load_unit_weights(order[nx.ui + 2], nx.slot)
            if k % WIN == WIN - 1 or k == nit - 1:
                flush(f"step{k}")
                S.capture = []
        S.capture = None

        ONT_KEYS = [k for k in list(S.last_w.keys()) if k.startswith("onT")]
        sb.cur = region_mark + 64
        ALLU = [k for k in list(S.last_w.keys()) + list(S.readers.keys())
                if not (k.startswith("onT") or k.startswith("xT") or k in ("identf", "identb", "eps_t", "one_t"))]
        ALLU = sorted(set(ALLU))
        wo = sb.alloc("wo", [128, 8, 1024], BF16)
        sb.cur = region_mark + 64
        fw = [sb.alloc(f"fw{i}", [128, 8, 1024], BF16) for i in range(4)]
        wo_r = wout_d.rearrange("(c p) n -> p c n", p=128)
        mT = sb.alloc("mT", [128, 8, TT], BF16)
        tha = sb.alloc("tha", [128, 512], BF16)
        thb = sb.alloc("thb", [128, 512], F32)
        m1 = sb.alloc("m1", [128, 512], F32)
        f1_end = sb.cur
        assert f1_end <= sb.top
        for q in ("pool", "sp", "pe", "act", "dve"):
            S._wait(q, S._deps([], ALLU))
        srcs = [win_r[:, :, MGA_OFF:MGA_OFF + 1024], win_r[:, :, MGB_OFF:MGB_OFF + 1024],
                wbg_d.rearrange("(c p) n -> p c n", p=128), wbh_d.rearrange("(c p) n -> p c n", p=128)]
        for dc in range(8):
            for i in range(4):
                S.dma("pool", fw[i][:, :, dc * 128:(dc + 1) * 128], srcs[i][:, :, dc * 128:(dc + 1) * 128],
                      f"ld_fw{i}_{dc}", W=[f"fw{i}_{dc}"])

        S.capture = []
        for dc in range(8):
            for bi in range(NBLK + 1):
                t0 = bi * BLK
                n = BLK if bi < NBLK else NS
                onk = [k for k in ONT_KEYS if k.endswith(f"_{bi}" if bi < NBLK else "_s")]
                def mm(bank, wi, src_is_x, base):
                    def f(e):
                        for c in range(8):
                            rhs = xT[:, c, t0:t0 + n] if src_is_x else onT[:, base + c, t0:t0 + n]
                            r = e.matmul(bank[:, 0:n], lhsT=fw[wi][:, c, dc * 128:(dc + 1) * 128], rhs=rhs,
                                         start=(c == 0), stop=(c == 7))
                        return r
                    return f
                a = nxt("A")
                S.op("pe", mm(PA[a], 0, True, 0), R=[f"fw0_{dc}"] + XK(t0), W=[f"PA{a}"])
                S.op("act", lambda e, a=a: e.activation(out=tha[:, 0:n], in_=PA[a][:, 0:n], func=AF.Tanh, scale=0.5),
                     R=[f"PA{a}"], W=["tha"])
                a = nxt("A")
                S.op("pe", mm(PA[a], 1, True, 0), R=[f"fw1_{dc}"] + XK(t0), W=[f"PA{a}"])
                S.op("act", lambda e, a=a: e.activation(out=thb[:, 0:n], in_=PA[a][:, 0:n], func=AF.Tanh, scale=0.5),
                     R=[f"PA{a}"], W=["thb"])
                b = nxt("B")
                S.op("pe", mm(PB[b], 2, False, 0), R=[f"fw2_{dc}"] + onk, W=[f"PB{b}"])
                S.op("dve", lambda e, b=b: e.scalar_tensor_tensor(m1[:, 0:n], tha[:, 0:n], 1.0, PB[b][:, 0:n],
                                                                  op0=ALU.add, op1=ALU.mult), R=[f"PB{b}", "tha"], W=["m1"])
                b = nxt("B")
                S.op("pe", mm(PB[b], 3, False, 8), R=[f"fw3_{dc}"] + onk, W=[f"PB{b}"])
                S.op("dve", lambda e, b=b: e.scalar_tensor_tensor(thb[:, 0:n], thb[:, 0:n], 1.0, PB[b][:, 0:n],
                                                                  op0=ALU.add, op1=ALU.mult), R=[f"PB{b}", "thb"], W=["thb"])
                S.op("pool", lambda e, dc=dc: e.tensor_tensor(mT[:, dc, t0:t0 + n], m1[:, 0:n], thb[:, 0:n], op=ALU.add),
                     R=["m1", "thb"], W=[f"mT_{bi}"])
            S.dma("pool", wo[:, :, dc * 128:(dc + 1) * 128], wo_r[:, :, dc * 128:(dc + 1) * 128], f"ld_wo_{dc}", W=[f"fw0_{dc}"])

        flush("F1")
        for q in ("pool", "sp", "pe", "act", "dve"):
            S._wait(q, S._deps([], [f"fw{i}_{dc}" for i in (1, 2) for dc in range(8)]))
        sb.cur = region_mark + 64 + 16384
        lng = sb.alloc("lng", [128, D], F32)
        lnb = sb.alloc("lnb", [128, D], F32)
        xt = [sb.alloc(f"xt{i}", [128, D], F32) for i in range(4)]
        stt = sb.alloc("stt", [128, 12], F32)
        junk2 = sb.alloc("junk2", [128, D], BF16)
        mv = sb.alloc("mv", [128, 2], F32)
        rs2 = sb.alloc("rs2", [128, 2], F32)
        eps2 = sb.alloc("eps2", [128, 1], F32)
        assert sb.cur <= region_mark + 64 + 3 * 16384
        S.capture = []
        S.dma("sp", lng[:], lng_d, "ld_ln", W=["lng"])
        S.dma("sp", lnb[:], lnb_d, "ld_lnb", W=["lnb"])
        S.op("dve", lambda e: e.memset(eps2[:], EPS / (ALPHA * ALPHA)), W=["eps2"])
        CY = 0.5 / ALPHA
        ntile = T // 128 + 1
        for ti in range(ntile):
            r0 = ti * 128
            m = 128 if ti < T // 128 else NS
            sl = ti % 4
            bi = min(ti // 4, NBLK)
            if ti == 0:
                for tj in range(min(2, ntile)):
                    mj = 128 if tj < T // 128 else NS
                    S.dma("sp", xt[tj % 4][0:mj, :], xtok_d[tj * 128:tj * 128 + mj, :], f"ld_xt{tj % 4}", W=[f"xt{tj % 4}"])
            if ti + 2 < ntile:
                tj = ti + 2
                mj = 128 if tj < T // 128 else NS
                S.dma("sp", xt[tj % 4][0:mj, :], xtok_d[tj * 128:tj * 128 + mj, :], f"ld_xt{tj % 4}", W=[f"xt{tj % 4}"])
            for hh in range(2):
                bq = (2 * ti + hh) % 4
                bank, bkey = ((PA, "PA") if bq < 2 else (PB, "PB"))
                bank = bank[bq % 2]
                bkey = f"{bkey}{bq % 2}"
                def f(e, bank=bank, hh=hh, r0=r0, m=m):
                    for c in range(8):
                        r = e.matmul(bank[0:m, :], lhsT=mT[:, c, r0:r0 + m], rhs=wo[:, c, hh * 512:(hh + 1) * 512],
                                     start=(c == 0), stop=(c == 7))
                    return r
                S.op("pe", f, R=[f"mT_{bi}"] + [f"fw0_{dc}" for dc in range(4 * hh, 4 * hh + 4)], W=[bkey])
                S.op("dve", lambda e, bank=bank, hh=hh, sl=sl, m=m: e.scalar_tensor_tensor(
                    xt[sl][0:m, hh * 512:(hh + 1) * 512], bank[0:m, :], CY, xt[sl][0:m, hh * 512:(hh + 1) * 512],
                    op0=ALU.mult, op1=ALU.add), R=[bkey, f"xt{sl}"], W=[f"xt{sl}"])
            S.op("act", lambda e, sl=sl, m=m: e.activation(out=junk2[0:m, :], in_=xt[sl][0:m, :], func=AF.Copy,
                                                           accum_out=stt[0:m, 0:1]), R=[f"xt{sl}"], W=["junk2", "stt0"])
            S.op("act", lambda e, sl=sl, m=m: e.activation(out=junk2[0:m, :], in_=xt[sl][0:m, :], func=AF.Square,
                                                           accum_out=stt[0:m, 1:2]), R=[f"xt{sl}"], W=["junk2", "stt1"])
            S.op("dve", lambda e, m=m: e.tensor_scalar(mv[0:m, 0:1], stt[0:m, 0:1], 1.0 / D, None, op0=ALU.mult),
                 R=["stt0"], W=["mv"])
            S.op("dve", lambda e, m=m: e.tensor_tensor(mv[0:m, 1:2], mv[0:m, 0:1], mv[0:m, 0:1], op=ALU.mult),
                 R=["mv"], W=["mv"])
            S.op("dve", lambda e, m=m: e.scalar_tensor_tensor(mv[0:m, 1:2], stt[0:m, 1:2], 1.0 / D, mv[0:m, 1:2],
                                                              op0=ALU.mult, op1=ALU.subtract), R=["stt1", "mv"], W=["mv"])
            S.op("act", lambda e, m=m: e.activation(out=rs2[0:m, 0:1], in_=mv[0:m, 1:2], func=AF.Ln, scale=1.0, bias=eps2[0:m, :]),
                 R=["mv", "eps2"], W=["rs2"])
            S.op("act", lambda e, m=m: e.activation(out=rs2[0:m, 0:1], in_=rs2[0:m, 0:1], func=AF.Exp, scale=-0.5),
                 R=["rs2"], W=["rs2"])
            S.op("dve", lambda e, m=m: e.scalar_tensor_tensor(rs2[0:m, 1:2], mv[0:m, 0:1], -1.0, rs2[0:m, 0:1],
                                                              op0=ALU.mult, op1=ALU.mult), R=["rs2", "mv"], W=["rs2"])
            S.op("act", lambda e, sl=sl, m=m: e.activation(out=xt[sl][0:m, :], in_=xt[sl][0:m, :], func=AF.Identity,
                                                           scale=rs2[0:m, 0:1], bias=rs2[0:m, 1:2]),
                 R=[f"xt{sl}", "rs2"], W=[f"xt{sl}"])
            S.op("dve", lambda e, sl=sl, m=m: e.tensor_tensor(xt[sl][0:m, :], xt[sl][0:m, :], lng[0:m, :], op=ALU.mult),
                 R=[f"xt{sl}", "lng"], W=[f"xt{sl}"])
            S.op("pool", lambda e, sl=sl, m=m: e.tensor_tensor(xt[sl][0:m, :], xt[sl][0:m, :], lnb[0:m, :], op=ALU.add),
                 R=[f"xt{sl}", "lnb"], W=[f"xt{sl}"])
            S.dma("sp", y_d[r0:r0 + m, :], xt[sl][0:m, :], f"st_y{sl}", R=[f"xt{sl}"])

        flush("F2")
        S.final_wait("sp", [k for k in S.dcnt if k.startswith("st_")])

    with nc.Block() as block:
        @block.sync
        def _(e):
            program("sp", e)

        @block.gpsimd
        def _(e):
            program("pool", e)

        @block.tensor
        def _(e):
            program("pe", e)

        @block.scalar
        def _(e):
            program("act", e)

        @block.vector
        def _(e):
            program("dve", e)
    return nc


_NC_CACHE = {}


def kernel(x_prompt, x_sample, state_gla, state_hgrn, w_in, w_gate_lr, b_gate_lr, gla_norm_g, w_br_gla,
           hgrn_lb_param, hgrn_norm_g, w_br_hgrn, w_out, ln_g, ln_b):
    f = lambda a: np.ascontiguousarray(np.asarray(a, dtype=np.float32))
    x_prompt, x_sample = f(x_prompt), f(x_sample)
    state_gla, state_hgrn = f(state_gla), f(state_hgrn)
    if "nc" not in _NC_CACHE:
        _NC_CACHE["nc"] = build_nc()
    nc = _NC_CACHE["nc"]
    shared = {
        "w_in": f(w_in)[0],
        "wlr": f(w_gate_lr)[0],
        "bgl": f(f(b_gate_lr)[0].reshape(4, 128).T),
        "glag": f(np.broadcast_to(f(gla_norm_g)[0].reshape(1, 1024), (128, 1024))),
        "lbp": f(f(hgrn_lb_param).reshape(2, 8, 128).transpose(2, 0, 1).reshape(128, 16)),
        "hgg": f(np.broadcast_to(f(hgrn_norm_g)[0].reshape(1, 1024), (128, 1024))),
        "wbg": f(w_br_gla)[0],
        "wbh": f(w_br_hgrn)[0],
        "wout": f(w_out)[0],
        "lng": f(np.broadcast_to(f(ln_g)[0].reshape(1, D), (128, D))),
        "lnb": f(np.broadcast_to(f(ln_b)[0].reshape(1, D), (128, D))),
    }
    in_maps = []
    for b in range(NCORES):
        xs = x_sample[b * NS:(b + 1) * NS, 0, :]
        xtok = np.concatenate([x_prompt[b], xs], axis=0)
        m = dict(shared)
        m["xtok"] = f(xtok)
        m["xT"] = f(xtok.T)
        m["sg"] = f(state_gla[0, b * NS:(b + 1) * NS])
        m["sh"] = f(state_hgrn[0, b * NS:(b + 1) * NS])
        in_maps.append(m)
    res = run_bass_kernel_spmd(nc, in_maps, core_ids=list(range(NCORES)))
    rs = res.results
    y_prompt = np.stack([r["y"][:T] for r in rs], axis=0)
    y_sample = np.concatenate([r["y"][T:TT] for r in rs], axis=0)[:, None, :]
    gp = np.stack([r["gp"] for r in rs], axis=0)[None]
    hp = np.stack([r["hp"] for r in rs], axis=0)[None]
    gs = np.concatenate([r["gs"] for r in rs], axis=0)[None]
    hs = np.concatenate([r["hs"] for r in rs], axis=0)[None]
    return (y_prompt.astype(np.float32), y_sample.astype(np.float32), gp.astype(np.float32),
            hp.astype(np.float32), gs.astype(np.float32), hs.astype(np.float32))
```
